# Optimizing a Trainium2 kernel written in Bass

```python
import math
import jax
import jax.numpy as jnp
from jax import lax
import numpy as np

D_MODEL = 1024
BATCH = 4
SEQ = 4096
DEPTH = 4
DEC_BATCH = 8
DEC_SEQ = 64
PAST_LEN = 1024

CHUNK = 64
N_ATT_HEADS = 8
HEAD_DIM = 64
D_ATT = N_ATT_HEADS * HEAD_DIM
D_SSM = D_MODEL - D_ATT
D_MIX = D_ATT + D_SSM
SSM_GROUP = 16
N_SSM_GROUPS = D_SSM // SSM_GROUP
SSM_STATE = 64
D_FF = ((8 * D_MODEL + 3 * 256 - 1) // (3 * 256)) * 256
D_IN = 3 * D_ATT + N_ATT_HEADS + D_SSM
Q_BLOCK = 128
FORGET_BIAS_INIT = 2.0
EPS = 1e-6

kernel_name = "fox_s5_hybrid_stream_step"


def _rmsnorm(x, g):
    x32 = x.astype(jnp.float32)
    y = x32 * lax.rsqrt(jnp.mean(x32 * x32, axis=-1, keepdims=True) + EPS) * g.astype(jnp.float32)
    return y.astype(x.dtype)


def _heads(t):
    b, n, _ = t.shape
    return t.reshape(b, n, N_ATT_HEADS, HEAD_DIM).transpose(0, 2, 1, 3)


def _fox_attention(q, k, v, f_q, f_k, q_pos, k_pos):
    s = jnp.einsum("bhqd,bhkd->bhqk", q, k).astype(jnp.float32) * (HEAD_DIM ** -0.5)
    s = s + f_q[..., :, None] - f_k[..., None, :]
    s = jnp.where(k_pos[None, :] <= q_pos[:, None], s, -jnp.inf)
    p = jax.nn.softmax(s, axis=-1)
    return jnp.einsum("bhqk,bhkd->bhqd", p.astype(v.dtype), v)


def _fox_prompt(q, k, v, logf):
    b, h, s, d = q.shape
    nb = s // Q_BLOCK
    f_cum = jnp.cumsum(logf, axis=-1)
    pos = jnp.arange(s)
    qb = q.reshape(b, h, nb, Q_BLOCK, d).transpose(2, 0, 1, 3, 4)
    fb = f_cum.reshape(b, h, nb, Q_BLOCK).transpose(2, 0, 1, 3)
    pb = pos.reshape(nb, Q_BLOCK)
    out = lax.map(lambda a: _fox_attention(a[0], k, v, a[1], f_cum, a[2], pos), (qb, fb, pb))
    return out.transpose(1, 2, 0, 3, 4).reshape(b, h, s, d)


def _ssm_combine(left, right):
    a_l, b_l = left
    a_r, b_r = right
    return (a_l * a_r, a_r * b_l + b_r)


def _s5_mix(u, p, h0):
    bsz, t, _ = u.shape
    f32 = jnp.float32
    u32 = u.astype(f32).reshape(bsz, t, N_SSM_GROUPS, SSM_GROUP)
    lam = lax.complex(p["ssm_a_re"].astype(f32), p["ssm_a_im"].astype(f32))
    dt = jnp.exp(p["ssm_log_dt"].astype(f32))[:, None]
    a_bar = jnp.exp(lam * dt)
    b_c = lax.complex(p["ssm_b_re"].astype(f32), p["ssm_b_im"].astype(f32))
    b_bar = ((a_bar - 1.0) / lam)[..., None] * b_c
    bu = jnp.einsum("btgm,gpm->btgp", u32.astype(jnp.complex64), b_bar)
    if h0 is not None:
        bu = bu.at[:, 0].add(a_bar[None] * h0)
    a_seq = jnp.broadcast_to(a_bar, bu.shape)
    _, h = lax.associative_scan(_ssm_combine, (a_seq, bu), axis=1)
    c_c = lax.complex(p["ssm_c_re"].astype(f32), p["ssm_c_im"].astype(f32))
    y = jnp.einsum("btgp,gmp->btgm", h, c_c).real
    y = y + p["ssm_d"].astype(f32).reshape(N_SSM_GROUPS, SSM_GROUP) * u32
    z = jax.nn.gelu(y.reshape(bsz, t, D_SSM))
    out = z * jax.nn.sigmoid(z @ p["w_glu"].astype(f32))
    return out.astype(u.dtype), h[:, -1]


def _layer(x, c, p, past):
    bsz, t, _ = x.shape
    mod = jax.nn.silu(c) @ p["w_ada"] + p["b_ada"]
    sh1, sc1, g1, sh2, sc2, g2 = jnp.split(mod[:, None, :], 6, axis=-1)
    hm = _rmsnorm(x, p["g_pre_mix"]) * (1.0 + sc1) + sh1
    proj = hm @ p["w_in"]
    q = _heads(proj[..., :D_ATT])
    k = _heads(proj[..., D_ATT:2 * D_ATT])
    v = _heads(proj[..., 2 * D_ATT:3 * D_ATT])
    gate_logit = proj[..., 3 * D_ATT:3 * D_ATT + N_ATT_HEADS].astype(jnp.float32)
    logf = jax.nn.log_sigmoid(gate_logit + p["b_forget"].astype(jnp.float32)).transpose(0, 2, 1)
    u = proj[..., 3 * D_ATT + N_ATT_HEADS:]
    if past is None:
        att = _fox_prompt(q, k, v, logf)
        h0 = None
    else:
        k_past, v_past, logf_past, h0_re, h0_im = past
        n_past = k_past.shape[2]
        k_all = jnp.concatenate([k_past.astype(k.dtype), k], axis=2)
        v_all = jnp.concatenate([v_past.astype(v.dtype), v], axis=2)
        f_cum = jnp.cumsum(jnp.concatenate([logf_past.astype(jnp.float32), logf], axis=-1), axis=-1)
        att = _fox_attention(q, k_all, v_all, f_cum[..., n_past:], f_cum,
                             n_past + jnp.arange(t), jnp.arange(n_past + t))
        h0 = lax.complex(h0_re.astype(jnp.float32), h0_im.astype(jnp.float32))
    ssm_out, h_last = _s5_mix(u, p, h0)
    mix = jnp.concatenate([att.transpose(0, 2, 1, 3).reshape(bsz, t, D_ATT), ssm_out], axis=-1)
    x = x + g1 * _rmsnorm(mix @ p["w_out"], p["g_post_mix"])
    hf = _rmsnorm(x, p["g_pre_ffn"]) * (1.0 + sc2) + sh2
    f = (jax.nn.silu(hf @ p["w_gate"]) * (hf @ p["w_up"])) @ p["w_down"]
    x = x + g2 * _rmsnorm(f, p["g_post_ffn"])
    return x, (k, v, logf, h_last.real, h_last.imag)


def setup_inputs(seed: int = 0) -> dict:
    key = jax.random.key(seed)
    ks = jax.random.split(key, 32)
    f32 = jnp.float32

    def nrm(k, shape, scale):
        return jax.random.normal(k, shape, f32) * scale

    n_idx = jnp.arange(SSM_STATE, dtype=f32)
    G, P, M = N_SSM_GROUPS, SSM_STATE, SSM_GROUP
    return {
        "x_prompt": nrm(ks[0], (BATCH, SEQ, D_MODEL), 1.0),
        "x_sample": nrm(ks[1], (DEC_BATCH, DEC_SEQ, D_MODEL), 1.0),
        "c_prompt": nrm(ks[2], (BATCH, D_MODEL), 1.0),
        "c_sample": nrm(ks[3], (DEC_BATCH, D_MODEL), 1.0),
        "cache_k": nrm(ks[4], (DEPTH, DEC_BATCH, N_ATT_HEADS, PAST_LEN, HEAD_DIM), 1.0),
        "cache_v": nrm(ks[5], (DEPTH, DEC_BATCH, N_ATT_HEADS, PAST_LEN, HEAD_DIM), 1.0),
        "cache_logf": jax.nn.log_sigmoid(FORGET_BIAS_INIT + nrm(ks[6], (DEPTH, DEC_BATCH, N_ATT_HEADS, PAST_LEN), 0.5)),
        "state_ssm_re": nrm(ks[7], (DEPTH, DEC_BATCH, G, P), 0.5),
        "state_ssm_im": nrm(ks[8], (DEPTH, DEC_BATCH, G, P), 0.5),
        "w_ada": nrm(ks[9], (DEPTH, D_MODEL, 6 * D_MODEL), D_MODEL ** -0.5),
        "b_ada": nrm(ks[10], (DEPTH, 6 * D_MODEL), 0.02),
        "g_pre_mix": 1.0 + nrm(ks[11], (DEPTH, D_MODEL), 0.02),
        "g_post_mix": 1.0 + nrm(ks[12], (DEPTH, D_MODEL), 0.02),
        "g_pre_ffn": 1.0 + nrm(ks[13], (DEPTH, D_MODEL), 0.02),
        "g_post_ffn": 1.0 + nrm(ks[14], (DEPTH, D_MODEL), 0.02),
        "w_in": nrm(ks[15], (DEPTH, D_MODEL, D_IN), D_MODEL ** -0.5),
        "b_forget": FORGET_BIAS_INIT + nrm(ks[16], (DEPTH, N_ATT_HEADS), 0.1),
        "ssm_a_re": -0.5 + nrm(ks[17], (DEPTH, G, P), 0.01),
        "ssm_a_im": math.pi * n_idx + nrm(ks[18], (DEPTH, G, P), 0.01),
        "ssm_log_dt": jax.random.uniform(ks[19], (DEPTH, G), f32, math.log(1e-3), math.log(1e-1)),
        "ssm_b_re": nrm(ks[20], (DEPTH, G, P, M), (2 * M) ** -0.5),
        "ssm_b_im": nrm(ks[21], (DEPTH, G, P, M), (2 * M) ** -0.5),
        "ssm_c_re": nrm(ks[22], (DEPTH, G, M, P), P ** -0.5),
        "ssm_c_im": nrm(ks[23], (DEPTH, G, M, P), P ** -0.5),
        "ssm_d": nrm(ks[24], (DEPTH, D_SSM), 1.0),
        "w_glu": nrm(ks[25], (DEPTH, D_SSM, D_SSM), D_SSM ** -0.5),
        "w_out": nrm(ks[26], (DEPTH, D_MIX, D_MODEL), D_MIX ** -0.5),
        "w_gate": nrm(ks[27], (DEPTH, D_MODEL, D_FF), D_MODEL ** -0.5),
        "w_up": nrm(ks[28], (DEPTH, D_MODEL, D_FF), D_MODEL ** -0.5),
        "w_down": nrm(ks[29], (DEPTH, D_FF, D_MODEL), D_FF ** -0.5),
    }


def reference(x_prompt, x_sample, c_prompt, c_sample, cache_k, cache_v, cache_logf,
              state_ssm_re, state_ssm_im, w_ada, b_ada, g_pre_mix, g_post_mix, g_pre_ffn,
              g_post_ffn, w_in, b_forget, ssm_a_re, ssm_a_im, ssm_log_dt, ssm_b_re, ssm_b_im,
              ssm_c_re, ssm_c_im, ssm_d, w_glu, w_out, w_gate, w_up, w_down):
    yp, ys = x_prompt, x_sample
    kp, vp, lp, rp, ip = [], [], [], [], []
    ks_, vs_, ls_, rs_, is_ = [], [], [], [], []
    for l in range(DEPTH):
        p = {
            "w_ada": w_ada[l], "b_ada": b_ada[l],
            "g_pre_mix": g_pre_mix[l], "g_post_mix": g_post_mix[l],
            "g_pre_ffn": g_pre_ffn[l], "g_post_ffn": g_post_ffn[l],
            "w_in": w_in[l], "b_forget": b_forget[l],
            "ssm_a_re": ssm_a_re[l], "ssm_a_im": ssm_a_im[l], "ssm_log_dt": ssm_log_dt[l],
            "ssm_b_re": ssm_b_re[l], "ssm_b_im": ssm_b_im[l],
            "ssm_c_re": ssm_c_re[l], "ssm_c_im": ssm_c_im[l], "ssm_d": ssm_d[l],
            "w_glu": w_glu[l], "w_out": w_out[l],
            "w_gate": w_gate[l], "w_up": w_up[l], "w_down": w_down[l],
        }
        yp, (k1, v1, l1, r1, i1) = _layer(yp, c_prompt, p, None)
        ys, (k2, v2, l2, r2, i2) = _layer(ys, c_sample, p, (cache_k[l], cache_v[l], cache_logf[l],
                                                            state_ssm_re[l], state_ssm_im[l]))
        kp.append(k1); vp.append(v1); lp.append(l1); rp.append(r1); ip.append(i1)
        ks_.append(k2); vs_.append(v2); ls_.append(l2); rs_.append(r2); is_.append(i2)
    return (yp, ys,
            jnp.stack(kp), jnp.stack(vp), jnp.stack(lp), jnp.stack(rp), jnp.stack(ip),
            jnp.stack(ks_), jnp.stack(vs_), jnp.stack(ls_), jnp.stack(rs_), jnp.stack(is_))
```

```python
import contextlib
import math
import numpy as np
import concourse.bass as bass
import concourse.mybir as mybir
from concourse.bass_utils import run_bass_kernel_spmd

F32 = mybir.dt.float32
BF16 = mybir.dt.bfloat16
AF = mybir.ActivationFunctionType
ALU = mybir.AluOpType

ENGS = ["pe", "act", "dve", "pool", "sync"]
COMPUTE = {"pe", "act", "dve", "pool"}
RING = 8
EPOCH = 20000


class Buf:
    __slots__ = ("name", "last_w", "readers", "psum")

    def __init__(self, name="", psum=False):
        self.name = name
        self.last_w = None
        self.readers = []
        self.psum = psum


class Op:
    __slots__ = ("eng", "fn", "dma", "deps", "idx", "signal", "sem", "val", "ringwait")

    def __init__(self, eng, fn, dma):
        self.eng = eng
        self.fn = fn
        self.dma = dma
        self.deps = set()
        self.signal = dma
        self.sem = None
        self.val = 0
        self.ringwait = None


class Prog:
    def __init__(self, nc):
        self.nc = nc
        self.ops = {e: [] for e in ENGS}
        self.allops = []
        self.stack = contextlib.ExitStack()
        self.pending_barrier = {e: None for e in ENGS}

    def sb(self, name, shape, dtype=F32):
        return self.stack.enter_context(self.nc.sbuf_tensor(name, list(shape), dtype))

    def ps(self, name, shape, dtype=F32):
        return self.stack.enter_context(self.nc.psum_tensor(name, list(shape), dtype))

    def buf(self, name=""):
        return Buf(name)

    def barrier(self):
        snap = set()
        for e in ENGS:
            lst = self.ops[e]
            if not lst:
                continue
            snap.add((e, len(lst) - 1))
            cnt = 0
            for i in range(len(lst) - 1, -1, -1):
                if lst[i].dma:
                    snap.add((e, i))
                    cnt += 1
                    if cnt >= RING:
                        break
                if len(lst) - i > 4 * RING + 64:
                    break
        for e in ENGS:
            self.pending_barrier[e] = snap

    def add(self, eng, fn, reads=(), writes=(), dma=False):
        op = Op(eng, fn, dma)
        lst = self.ops[eng]
        op.idx = len(lst)
        me = (eng, op.idx)
        same_ok = (eng == "pe") and not dma
        pb = self.pending_barrier[eng]
        if pb is not None:
            for d in pb:
                if d != me:
                    op.deps.add(d)
            self.pending_barrier[eng] = None
        for b in reads:
            w = b.last_w
            if w is not None and w != me:
                op.deps.add(w)
            if b.psum:
                for r in b.readers:
                    if r[0] != eng:
                        op.deps.add(r)
        for b in writes:
            w = b.last_w
            if w is not None and w != me:
                wop = self.ops[w[0]][w[1]]
                if not (same_ok and w[0] == eng and not wop.dma):
                    op.deps.add(w)
            for r in b.readers:
                if r == me:
                    continue
                rop = self.ops[r[0]][r[1]]
                if same_ok and r[0] == eng and not rop.dma:
                    continue
                op.deps.add(r)
        for b in reads:
            b.readers.append(me)
        for b in writes:
            b.last_w = me
            b.readers = []
        lst.append(op)
        self.allops.append(op)
        return op

    def emit(self):
        nc = self.nc
        for op in self.allops:
            for (e, i) in op.deps:
                self.ops[e][i].signal = True
        for e in ENGS:
            if e in COMPUTE:
                for op in reversed(self.ops[e]):
                    if not op.dma:
                        op.signal = True
                        break
        nsem = [0]

        def newsem(tag):
            nsem[0] += 1
            return self.stack.enter_context(nc.semaphore(f"s_{tag}_{nsem[0]}"))

        for e in ENGS:
            cur = None
            cnt = 0
            ring = None
            ringcnt = None
            ringprev = None
            nd = 0
            for op in self.ops[e]:
                if op.dma:
                    if ring is None or nd >= (EPOCH // 16) * RING:
                        ring = [newsem(e + "r") for _ in range(RING)]
                        ringcnt = [0] * RING
                        ringprev = [None] * RING
                        nd = 0
                    slot = nd % RING
                    if ringprev[slot] is not None:
                        op.ringwait = ringprev[slot]
                    ringcnt[slot] += 16
                    op.sem = ring[slot]
                    op.val = ringcnt[slot]
                    ringprev[slot] = (op.sem, op.val)
                    nd += 1
                elif op.signal:
                    if cur is None or cnt >= EPOCH:
                        cur = newsem(e)
                        cnt = 0
                    cnt += 1
                    op.sem = cur
                    op.val = cnt
        self.nsem = nsem[0]
        finals = {}
        for e in ENGS:
            for op in self.ops[e]:
                if op.dma:
                    k = id(op.sem)
                    if k not in finals or finals[k][1] < op.val:
                        finals[k] = (op.sem, op.val)
            if e in COMPUTE:
                for op in reversed(self.ops[e]):
                    if not op.dma:
                        finals[id(op.sem)] = (op.sem, op.val)
                        break
        prog = self

        def emit_engine(e, eng):
            known = {}
            for op in prog.ops[e]:
                waits = {}
                for (de, di) in op.deps:
                    dop = prog.ops[de][di]
                    k = id(dop.sem)
                    if k not in waits or waits[k][1] < dop.val:
                        waits[k] = (dop.sem, dop.val)
                if op.ringwait is not None:
                    k = id(op.ringwait[0])
                    if k not in waits or waits[k][1] < op.ringwait[1]:
                        waits[k] = op.ringwait
                for k, (s, v) in waits.items():
                    if known.get(k, 0) >= v:
                        continue
                    known[k] = v
                    eng.wait_ge(s, v)
                ins = op.fn(eng)
                if op.signal:
                    ins.then_inc(op.sem, 16 if op.dma else 1)
            if e == "sync":
                for k, (s, v) in finals.items():
                    if known.get(k, 0) < v:
                        eng.wait_ge(s, v)

        with nc.Block() as block:
            @block.tensor
            def _(eng):
                emit_engine("pe", eng)

            @block.scalar
            def _(eng):
                emit_engine("act", eng)

            @block.vector
            def _(eng):
                emit_engine("dve", eng)

            @block.gpsimd
            def _(eng):
                emit_engine("pool", eng)

            @block.sync
            def _(eng):
                emit_engine("sync", eng)

    def close(self):
        self.stack.close()


D = 1024
NH = 8
DH = 64
DATT = 512
DSSM = 512
DFF = 2816
DIN = 2056
NG = 32
NST = 64
EPS = 1e-6
NJ = DFF // 128
VW = 72
MAGIC = 12582912.0
TWO_PI = 2.0 * math.pi
CW1 = 6.28125
CW2 = float(np.float32(TWO_PI - 6.28125))
PI_LO = 3.1415925


class Tile:
    __slots__ = ("ap", "b")

    def __init__(self, ap, b):
        self.ap = ap
        self.b = b


def build(T, TS, PAST, DEPTH, stop_after=None):
    nc = bass.Bass("TRN2", target_bir_lowering=False)
    P = Prog(nc)

    def din(name, shape):
        return nc.dram_tensor(name, list(shape), F32, kind="ExternalInput").ap()

    def dout(name, shape):
        return nc.dram_tensor(name, list(shape), F32, kind="ExternalOutput").ap()

    def dscr(name, shape, dt=F32):
        return nc.dram_tensor(name, list(shape), dt, kind="Internal").ap()

    xp = din("xp", [T, D]); xs = din("xs", [TS, D]); c2 = din("c2", [2, D])
    ck = din("ck", [DEPTH, NH, PAST, DH]); cv = din("cv", [DEPTH, NH, PAST, DH])
    clf = din("clf", [DEPTH, NH, PAST])
    sre = din("sre", [DEPTH, NG, NST]); sim = din("sim", [DEPTH, NG, NST])
    w_ada = din("w_ada", [DEPTH, D, 6 * D]); b_ada = din("b_ada", [DEPTH, 6 * D])
    g_pre_mix = din("g_pre_mix", [DEPTH, D]); g_post_mix = din("g_post_mix", [DEPTH, D])
    g_pre_ffn = din("g_pre_ffn", [DEPTH, D]); g_post_ffn = din("g_post_ffn", [DEPTH, D])
    w_in = din("w_in", [DEPTH, D, DIN]); b_forget = din("b_forget", [DEPTH, NH])
    ssm_a_re = din("ssm_a_re", [DEPTH, NG, NST]); ssm_a_im = din("ssm_a_im", [DEPTH, NG, NST])
    ssm_log_dt = din("ssm_log_dt", [DEPTH, NG])
    ssm_b_re = din("ssm_b_re", [DEPTH, NG, NST, 16]); ssm_b_im = din("ssm_b_im", [DEPTH, NG, NST, 16])
    ssm_c_re = din("ssm_c_re", [DEPTH, NG, 16, NST]); ssm_c_im = din("ssm_c_im", [DEPTH, NG, 16, NST])
    ssm_d = din("ssm_d", [DEPTH, DSSM]); w_glu = din("w_glu", [DEPTH, DSSM, DSSM])
    w_out = din("w_out", [DEPTH, D, D]); w_gate = din("w_gate", [DEPTH, D, DFF])
    w_up = din("w_up", [DEPTH, D, DFF]); w_down = din("w_down", [DEPTH, DFF, D])

    yp = dout("yp", [T, D]); ys = dout("ys", [TS, D])
    nkp = dout("nkp", [DEPTH, NH, T, DH]); nvp = dout("nvp", [DEPTH, NH, T, DH])
    nlp = dout("nlp", [DEPTH, NH, T])
    nrp = dout("nrp", [DEPTH, NG, NST]); nip = dout("nip", [DEPTH, NG, NST])
    nks = dout("nks", [DEPTH, NH, TS, DH]); nvs = dout("nvs", [DEPTH, NH, TS, DH])
    nls = dout("nls", [DEPTH, NH, TS])
    nrs = dout("nrs", [DEPTH, NG, NST]); nis = dout("nis", [DEPTH, NG, NST])

    WG2 = dscr("WG2", [DEPTH, 128, 8, DFF], BF16)
    WU2 = dscr("WU2", [DEPTH, 128, 8, DFF], BF16)
    WD2 = dscr("WD2", [DEPTH, 128, NJ, D], BF16)
    B_WFF = [P.buf() for _ in range(DEPTH)]

    class Stream:
        pass

    def mk_stream(name, s, Tn, TT, koff, x_in, y_out, nk_o, nv_o, nl_o, nr_o, ni_o):
        S = Stream()
        S.name = name; S.s = s; S.T = Tn; S.TT = TT; S.nt = Tn // TT; S.koff = koff
        S.QB = min(128, Tn)
        S.nqg = Tn // S.QB
        S.QG = min(512, Tn)
        S.ngr = Tn // S.QG
        S.TK = koff + Tn
        S.nkt = (S.TK + 127) // 128
        S.x_in = x_in; S.y_out = y_out
        S.nk_o = nk_o; S.nv_o = nv_o; S.nl_o = nl_o; S.nr_o = nr_o; S.ni_o = ni_o
        S.xT = dscr(name + "_xT", [D, Tn]); S.B_xT = [P.buf() for _ in range(S.nt)]
        S.QT = dscr(name + "_QT", [NH, DH + 3, Tn], BF16)
        S.KT = dscr(name + "_KT", [NH, DH + 3, Tn], BF16)
        S.VX = dscr(name + "_VX", [Tn, NH, VW], BF16)
        S.B_qkv = P.buf(); S.B_kones = P.buf()
        S.uT = dscr(name + "_uT", [DSSM, Tn], BF16); S.B_uT = P.buf()
        S.soT = dscr(name + "_soT", [DSSM, Tn], BF16); S.B_soT = P.buf()
        return S

    SP = mk_stream("p", 0, T, min(512, T), 0, xp, yp, nkp, nvp, nlp, nrp, nip)
    SS = mk_stream("s", 1, TS, TS, PAST, xs, ys, nks, nvs, nls, nrs, nis)
    STREAMS = [SP, SS]

    def ptile(name, shape, dt=F32):
        h = P.sb(name, shape, dt)
        return Tile(h[tuple(slice(None) for _ in shape)], P.buf(name))

    identf = ptile("identf", [128, 128]); identb = ptile("identb", [128, 128], BF16)
    onesM = ptile("onesM", [128, 128], BF16); tri = ptile("tri", [128, 128], BF16)
    maskT = ptile("maskT", [128, 128], BF16); onesbf = ptile("onesbf", [8, 512], BF16)
    ntri = ptile("ntri", [128, 128], BF16)
    ones8 = ptile("ones8", [8, 512]); epsc = ptile("epsc", [128, 1]); one1 = ptile("one1", [128, 1])
    halfpi = ptile("halfpi", [128, 1]); trow = ptile("trow", [128, 128]); tcol = ptile("tcol", [128, 1])
    ntcol = ptile("ntcol", [128, 1]); sel8 = ptile("sel8", [8, NH, 128])
    PAR = ptile("PAR", [128, DEPTH, 2, 6, 8])
    scT = ptile("scT", [128, 8, 2], BF16)
    ATT = [ptile("ATTp", [128, SP.nqg, DATT], BF16), ptile("ATTs", [SS.QB, 1, DATT], BF16)]
    F_T = [ptile("F_Tp", [128, SP.nkt, NH]), ptile("F_Ts", [128, SS.nkt, NH])]
    CBC = [ptile("CBCp", [128, NH, SP.ngr]), ptile("CBCs", [128, NH, SS.ngr])]
    Fcs = [ptile("Fcp", [8, SP.ngr]), ptile("Fcs", [8, SS.ngr])]
    negb = ptile("negb", [8, 1])
    fcarry = ptile("fcarry", [8, 1])

    PSB = [Tile(P.ps(f"psb{i}", [128, 512])[:, :], Buf(f"psb{i}", psum=True)) for i in range(8)]

    ARW = 40960
    AR = P.sb("arena", [128, ARW])

    class Arena:
        def __init__(self):
            self.off = 0

        def reset(self):
            self.off = 0

        def get(self, nwords_f32, dt=F32, shape=None, parts=128):
            n = (nwords_f32 + 7) // 8 * 8
            assert self.off + n <= ARW, ("arena overflow", self.off, n)
            ap = AR[0:parts, self.off:self.off + nwords_f32]
            self.off += n
            if dt != F32:
                ap = ap.bitcast(dt)
            if shape is not None and len(shape) > 1:
                if len(shape) == 2:
                    ap = ap.rearrange("p (a b) -> p a b", a=shape[0])
                elif len(shape) == 3:
                    ap = ap.rearrange("p (a b c) -> p a b c", a=shape[0], b=shape[1])
                elif len(shape) == 4:
                    ap = ap.rearrange("p (a b c d) -> p a b c d", a=shape[0], b=shape[1], c=shape[2])
            return Tile(ap, P.buf())

        def f32(self, shape, parts=128):
            return self.get(int(np.prod(shape)), F32, shape, parts)

        def bf(self, shape, parts=128):
            n = int(np.prod(shape))
            assert n % 2 == 0
            return self.get(n // 2, BF16, shape, parts)

    A = Arena()

    def rb(ts):
        return [t.b for t in ts]

    def MM(out, lhsT, rhs, start, stop, reads, writes, **kw):
        P.add("pe", lambda e: e.matmul(out, lhsT=lhsT, rhs=rhs, start=start, stop=stop, **kw), rb(reads), rb(writes))

    def TR(out, in_, ident, reads, writes):
        P.add("pe", lambda e: e.transpose(out, in_, ident.ap[0:in_.shape[0], 0:in_.shape[0]]), rb(reads) + [ident.b], rb(writes))

    def ACT(out, in_, func, reads, writes, bias=None, scale=None):
        kw = {}
        if bias is not None:
            kw["bias"] = bias
        if scale is not None:
            kw["scale"] = scale
        P.add("act", lambda e: e.activation(out=out, in_=in_, func=func, **kw), rb(reads), rb(writes))

    def TT_(eng, out, in0, in1, op, reads, writes):
        P.add(eng, lambda e: e.tensor_tensor(out=out, in0=in0, in1=in1, op=op), rb(reads), rb(writes))

    def TS_(eng, out, in0, s1, s2, op0, op1, reads, writes):
        if s2 is None:
            P.add(eng, lambda e: e.tensor_scalar(out=out, in0=in0, scalar1=s1, scalar2=None, op0=op0), rb(reads), rb(writes))
        else:
            P.add(eng, lambda e: e.tensor_scalar(out=out, in0=in0, scalar1=s1, scalar2=s2, op0=op0, op1=op1), rb(reads), rb(writes))

    def STT(out, in0, scalar, in1, op0, op1, reads, writes):
        P.add("dve", lambda e: e.scalar_tensor_tensor(out=out, in0=in0, scalar=scalar, in1=in1, op0=op0, op1=op1), rb(reads), rb(writes))

    def CP(eng, out, in_, reads, writes):
        if eng == "act":
            P.add("act", lambda e: e.copy(out=out, in_=in_), rb(reads), rb(writes))
        else:
            P.add(eng, lambda e: e.tensor_copy(out=out, in_=in_), rb(reads), rb(writes))

    def MSET(eng, t, val):
        P.add(eng, lambda e: e.memset(t.ap, val), [], [t.b])

    def DMA(q, out, in_, reads, writes):
        P.add(q, lambda e: e.dma_start(out=out, in_=in_), reads, writes, dma=True)

    psrr = [0]

    def PS():
        t = PSB[psrr[0] % 8]
        psrr[0] += 1
        return t

    def body_fn():
        MSET("pool", identf, 1.0)
        P.add("pool", lambda e: e.affine_select(out=identf.ap, in_=identf.ap, pattern=[[1, 128]], compare_op=ALU.is_equal, fill=0.0, base=0, channel_multiplier=-1), [identf.b], [identf.b])
        MSET("pool", identb, 1.0)
        P.add("pool", lambda e: e.affine_select(out=identb.ap, in_=identb.ap, pattern=[[1, 128]], compare_op=ALU.is_equal, fill=0.0, base=0, channel_multiplier=-1), [identb.b], [identb.b])
        MSET("pool", tri, 1.0)
        P.add("pool", lambda e: e.affine_select(out=tri.ap, in_=tri.ap, pattern=[[1, 128]], compare_op=ALU.is_ge, fill=0.0, base=0, channel_multiplier=-1), [tri.b], [tri.b])
        MSET("pool", maskT, -30000.0)
        P.add("pool", lambda e: e.affine_select(out=maskT.ap, in_=maskT.ap, pattern=[[1, 128]], compare_op=ALU.is_gt, fill=0.0, base=0, channel_multiplier=-1), [maskT.b], [maskT.b])
        MSET("dve", onesbf, 1.0)
        for S_ in STREAMS:
            for r in range(3):
                for t0_ in range(0, S_.T, 512):
                    tw = min(512, S_.T - t0_)
                    DMA("sync", S_.KT[:, DH + r, t0_:t0_ + tw], onesbf.ap[:, 0:tw], [onesbf.b], [S_.B_kones])
        TS_("dve", ntri.ap, tri.ap, -1.0, None, ALU.mult, None, [tri], [ntri])
        MSET("dve", onesM, 1.0 / 1024.0)
        MSET("dve", ones8, 1.0)
        MSET("dve", epsc, EPS)
        MSET("dve", one1, 1.0)
        MSET("dve", halfpi, math.pi / 2.0)
        P.add("pool", lambda e: e.iota(trow.ap, pattern=[[1, 128]], base=0, channel_multiplier=0, allow_small_or_imprecise_dtypes=True), [], [trow.b])
        P.add("pool", lambda e: e.iota(tcol.ap, pattern=[[0, 1]], base=0, channel_multiplier=1, allow_small_or_imprecise_dtypes=True), [], [tcol.b])
        TS_("dve", ntcol.ap, tcol.ap, -1.0, None, ALU.mult, None, [tcol], [ntcol])
        MSET("pool", sel8, 1.0)
        P.add("pool", lambda e: e.affine_select(out=sel8.ap, in_=sel8.ap, pattern=[[-1, NH], [0, 128]], compare_op=ALU.is_equal, fill=0.0, base=0, channel_multiplier=1), [sel8.b], [sel8.b])

        if stop_after == "P0":
            return
        A.reset()
        c2n = A.f32([D], parts=2)
        DMA("sync", c2n.ap, c2, [], [c2n.b])
        pc = PS()
        for kc in range(8):
            TR(pc.ap[:, kc * 2:kc * 2 + 2], c2n.ap[:, kc * 128:(kc + 1) * 128], identf, [c2n], [pc])
        ACT(scT.ap.rearrange("p a b -> p (a b)"), pc.ap[:, 0:16], AF.Silu, [pc], [scT])
        spn = A.f32([128], parts=80)
        spT = A.f32([80])
        modT = A.f32([48, 2])
        WA = [A.bf([8, 1024]), A.bf([8, 1024])]
        wai = 0
        for l in range(DEPTH):
            DMA("sync", spn.ap[0:48, :], b_ada[l].rearrange("(a b) -> a b", b=128), [], [spn.b])
            for i, g in enumerate([g_pre_mix, g_post_mix, g_pre_ffn, g_post_ffn]):
                DMA("sync", spn.ap[48 + 8 * i:56 + 8 * i, :], g[l].rearrange("(a b) -> a b", b=128), [], [spn.b])
            pt = PS()
            TR(pt.ap[:, 0:80], spn.ap, identf, [spn], [pt])
            CP("dve", spT.ap, pt.ap[:, 0:80], [pt], [spT])
            pm = PS()
            for piece in range(6):
                wa = WA[wai % 2]; wai += 1
                DMA("pool", wa.ap, w_ada[l, :, piece * 1024:(piece + 1) * 1024].rearrange("(kc p) n -> p kc n", p=128), [], [wa.b])
                for j in range(8):
                    cj = piece * 8 + j
                    for kc in range(8):
                        MM(pm.ap[:, cj * 2:cj * 2 + 2], wa.ap[:, kc, j * 128:(j + 1) * 128], scT.ap[:, kc, :], kc == 0, kc == 7, [wa, scT], [pm])
            for s in range(2):
                TT_("dve", modT.ap[:, :, s], pm.ap[:, 0:96].rearrange("p (a b) -> p a b", b=2)[:, :, s], spT.ap[:, 0:48], ALU.add, [pm, spT], [modT])
            for s in range(2):
                par = PAR.ap[:, l, s]
                STT(par[:, 0, :], modT.ap[:, 8:16, s], 1.0, spT.ap[:, 48:56], ALU.add, ALU.mult, [modT, spT], [PAR])
                CP("dve", par[:, 1, :], modT.ap[:, 0:8, s], [modT], [PAR])
                TT_("dve", par[:, 2, :], modT.ap[:, 16:24, s], spT.ap[:, 56:64], ALU.mult, [modT, spT], [PAR])
                STT(par[:, 3, :], modT.ap[:, 32:40, s], 1.0, spT.ap[:, 64:72], ALU.add, ALU.mult, [modT, spT], [PAR])
                CP("dve", par[:, 4, :], modT.ap[:, 24:32, s], [modT], [PAR])
                TT_("dve", par[:, 5, :], modT.ap[:, 40:48, s], spT.ap[:, 72:80], ALU.mult, [modT, spT], [PAR])

        if stop_after == "P1":
            return
        for l in range(DEPTH):
            DMA("pool", WG2[l], w_gate[l].rearrange("(kc p) n -> p kc n", p=128), [], [B_WFF[l]])
            DMA("pool", WU2[l], w_up[l].rearrange("(kc p) n -> p kc n", p=128), [], [B_WFF[l]])
            DMA("pool", WD2[l], w_down[l].rearrange("(j p) n -> p j n", p=128), [], [B_WFF[l]])

        if stop_after == "P2":
            return
        P.barrier()
        A.reset()
        xin = [A.f32([D]), A.f32([D])]
        xtr = [A.f32([8, 128]), A.f32([8, 128])]
        cnt = 0
        for S in STREAMS:
            nb = S.T // S.QB
            for i in range(nb):
                xi = xin[cnt % 2]; xo = xtr[cnt % 2]; cnt += 1
                qb = S.QB
                DMA("sync", xi.ap[0:qb, :], S.x_in[i * qb:(i + 1) * qb, :], [], [xi.b])
                for half in range(2):
                    pt = PS()
                    for c4 in range(4):
                        c = half * 4 + c4
                        TR(pt.ap[:, c4 * 128:c4 * 128 + qb], xi.ap[0:qb, c * 128:(c + 1) * 128], identf, [xi], [pt])
                    CP("act" if half == 0 else "dve", xo.ap[:, half * 4:half * 4 + 4, 0:qb], pt.ap.rearrange("p (a b) -> p a b", b=128)[:, :, 0:qb], [pt], [xo])
                ti = (i * qb) // S.TT
                DMA("sync", S.xT.rearrange("(c p) t -> p c t", p=128)[:, :, i * qb:(i + 1) * qb], xo.ap[:, :, 0:qb], [xo.b], [S.B_xT[ti]])

        if stop_after == "X":
            return
        def rmsnorm_mod(S, l, xt, which, hm, tmp2, sq, rstd):
            TT = S.TT
            ACT(sq.ap, xt.ap, AF.Square, [xt], [sq])
            pss = PS()
            for c in range(8):
                MM(pss.ap[:, 0:TT], onesM.ap, sq.ap[:, c, :], c == 0, c == 7, [onesM, sq], [pss])
            ACT(rstd.ap, pss.ap[:, 0:TT], AF.Sqrt, [pss, epsc], [rstd], bias=epsc.ap, scale=1.0)
            P.add("dve", lambda e: e.reciprocal(out=rstd.ap, in_=rstd.ap), [rstd.b], [rstd.b])
            for c in range(8):
                tm = tmp2[c % 2]
                STT(tm.ap, xt.ap[:, c, :], PAR.ap[:, l, S.s, which, c:c + 1], rstd.ap, ALU.mult, ALU.mult, [xt, PAR, rstd], [tm])
                ACT(hm.ap[:, c, :], tm.ap, AF.Identity, [tm, PAR], [hm], bias=PAR.ap[:, l, S.s, which + 1, c:c + 1], scale=1.0)

        def phase_A(S, l, WIN):
            TT = S.TT; QB = S.QB; nsub = TT // QB
            s = S.s
            xts = [A.f32([8, TT]), A.f32([8, TT])]
            sq = A.bf([8, TT]); rstd = A.f32([TT])
            tmp2 = [A.f32([TT]), A.f32([TT])]
            hms = [A.bf([8, TT]), A.bf([8, TT])]
            qTa = A.bf([NH, TT], parts=64); kTa = A.bf([NH, TT], parts=64)
            ktm = [A.f32([DATT]), A.f32([DATT])]; vtm = [A.f32([DATT]), A.f32([DATT])]
            vx = [A.bf([NH, VW]), A.bf([NH, VW])]
            uTa = A.bf([4, TT])
            gT = A.f32([TT], parts=8); lf = A.f32([TT], parts=8); Fp = A.f32([TT], parts=8)
            Dq = A.f32([TT], parts=8); Dsp = [A.bf([TT], parts=8) for _ in range(3)]
            lfn = 0
            if S.koff > 0:
                MSET("dve", F_T[s], 0.0)
                npast = S.koff
                clt = A.f32([npast], parts=8)
                Fpast = A.f32([npast], parts=8)
                DMA("sync", clt.ap, clf[l], [], [clt.b])
                MSET("dve", fcarry, 0.0)
                cs_ = min(512, npast)
                for i0 in range(0, npast, cs_):
                    P.add("dve", lambda e, i0=i0: e.tensor_tensor_scan(out=Fpast.ap[:, i0:i0 + cs_], data0=ones8.ap[:, 0:cs_], data1=clt.ap[:, i0:i0 + cs_], initial=fcarry.ap, op0=ALU.mult, op1=ALU.add), [ones8.b, clt.b, fcarry.b], [Fpast.b])
                    CP("dve", fcarry.ap, Fpast.ap[:, i0 + cs_ - 1:i0 + cs_], [Fpast], [fcarry])
                pf = PS()
                for kt in range(npast // 128):
                    TR(pf.ap[:, kt * 8:kt * 8 + 8], Fpast.ap[:, kt * 128:(kt + 1) * 128], identf, [Fpast], [pf])
                CP("dve", F_T[s].ap[:, 0:npast // 128, :], pf.ap[:, 0:(npast // 128) * 8].rearrange("p (a b) -> p a b", b=8), [pf], [F_T[s]])
            else:
                MSET("dve", fcarry, 0.0)
            DMA("sync", negb.ap, b_forget[l].rearrange("(a b) -> a b", b=1), [], [negb.b])
            TS_("dve", negb.ap, negb.ap, -1.0, None, ALU.mult, None, [negb], [negb])
            for i in range(S.nt):
                xt = xts[i % 2]; hm = hms[i % 2]
                t0 = i * TT
                DMA("sync", xt.ap, S.xT.rearrange("(c p) t -> p c t", p=128)[:, :, t0:t0 + TT], [S.B_xT[i]], [xt.b])
                rmsnorm_mod(S, l, xt, 0, hm, tmp2, sq, rstd)
                if stop_after == "A1":
                    return
                for h in range(NH):
                    pq = PS()
                    for c in range(8):
                        MM(pq.ap[0:64, 0:TT], WIN.ap[:, c, h * 64:(h + 1) * 64], hm.ap[:, c, :], c == 0, c == 7, [WIN, hm], [pq])
                    ACT(qTa.ap[:, h, :], pq.ap[0:64, 0:TT], AF.Identity, [pq], [qTa], scale=0.125)
                    pk = PS()
                    for c in range(8):
                        MM(pk.ap[0:64, 0:TT], WIN.ap[:, c, DATT + h * 64:DATT + (h + 1) * 64], hm.ap[:, c, :], c == 0, c == 7, [WIN, hm], [pk])
                    CP("dve", kTa.ap[:, h, :], pk.ap[0:64, 0:TT], [pk], [kTa])
                DMA("sync", S.QT.rearrange("h d t -> d h t")[0:DH, :, t0:t0 + TT], qTa.ap, [qTa.b], [S.B_qkv])
                DMA("sync", S.KT.rearrange("h d t -> d h t")[0:DH, :, t0:t0 + TT], kTa.ap, [kTa.b], [S.B_qkv])
                if stop_after == "A2":
                    return
                for j in range(nsub):
                    tb0 = t0 + j * QB
                    kt_ = ktm[j % 2]; vt_ = vtm[j % 2]; vx_ = vx[j % 2]
                    pk = PS()
                    for c in range(8):
                        MM(pk.ap[0:QB, :], hm.ap[:, c, j * QB:(j + 1) * QB], WIN.ap[:, c, DATT:2 * DATT], c == 0, c == 7, [WIN, hm], [pk])
                    CP("act", kt_.ap[0:QB, :], pk.ap[0:QB, :], [pk], [kt_])
                    if stop_after == "A2a":
                        return
                    DMA("sync", S.nk_o[l].rearrange("h t d -> t h d")[tb0:tb0 + QB], kt_.ap[0:QB, :].rearrange("p (h d) -> p h d", d=DH), [kt_.b], [])
                    if stop_after == "A2b":
                        return
                    pv = PS()
                    for c in range(8):
                        MM(pv.ap[0:QB, :], hm.ap[:, c, j * QB:(j + 1) * QB], WIN.ap[:, c, 2 * DATT:3 * DATT], c == 0, c == 7, [WIN, hm], [pv])
                    CP("act", vt_.ap[0:QB, :], pv.ap[0:QB, :], [pv], [vt_])
                    DMA("sync", S.nv_o[l].rearrange("h t d -> t h d")[tb0:tb0 + QB], vt_.ap[0:QB, :].rearrange("p (h d) -> p h d", d=DH), [vt_.b], [])
                    if stop_after == "A2c":
                        return
                    P.add("pool", lambda e, vx_=vx_: e.memset(vx_.ap[0:QB].rearrange("p h d -> p (h d)"), 1.0), [], [vx_.b])
                    CP("dve", vx_.ap[0:QB, :, 0:DH], vt_.ap[0:QB, :].rearrange("p (h d) -> p h d", d=DH), [vt_], [vx_])
                    if stop_after == "A2d":
                        return
                    DMA("sync", S.VX[tb0:tb0 + QB], vx_.ap[0:QB], [vx_.b], [S.B_qkv])
                if stop_after == "A3":
                    return
                for m in range(4):
                    pu = PS()
                    for c in range(8):
                        MM(pu.ap[:, 0:TT], WIN.ap[:, c, 3 * DATT + NH + m * 128:3 * DATT + NH + (m + 1) * 128], hm.ap[:, c, :], c == 0, c == 7, [WIN, hm], [pu])
                    CP("act" if m % 2 else "dve", uTa.ap[:, m, :], pu.ap[:, 0:TT], [pu], [uTa])
                DMA("sync", S.uT.rearrange("(c p) t -> p c t", p=128)[:, :, t0:t0 + TT], uTa.ap, [uTa.b], [S.B_uT])
                if stop_after == "A4":
                    return
                pg = PS()
                for c in range(8):
                    MM(pg.ap[0:8, 0:TT], WIN.ap[:, c, 3 * DATT:3 * DATT + NH], hm.ap[:, c, :], c == 0, c == 7, [WIN, hm], [pg])
                ACT(gT.ap, pg.ap[0:8, 0:TT], AF.Exp, [pg, negb], [gT], bias=negb.ap, scale=-1.0)
                ACT(gT.ap, gT.ap, AF.Ln, [gT, one1], [gT], bias=one1.ap[0:8], scale=1.0)
                TS_("dve", lf.ap, gT.ap, -1.0, None, ALU.mult, None, [gT], [lf])
                DMA("sync", S.nl_o[l][:, t0:t0 + TT], lf.ap, [lf.b], [])
                P.add("dve", lambda e: e.tensor_tensor_scan(out=Fp.ap, data0=ones8.ap[:, 0:TT], data1=lf.ap, initial=fcarry.ap, op0=ALU.mult, op1=ALU.add), [ones8.b, lf.b, fcarry.b], [Fp.b])
                CP("dve", fcarry.ap, Fp.ap[:, TT - 1:TT], [Fp], [fcarry])
                if stop_after == "A5":
                    return
                assert S.TT == S.QG
                CP("dve", Fcs[s].ap[:, i:i + 1], Fp.ap[:, 0:1], [Fp], [Fcs[s]])
                TS_("dve", Dq.ap, Fp.ap, Fcs[s].ap[:, i:i + 1], None, ALU.subtract, None, [Fp, Fcs[s]], [Dq])
                for r in range(3):
                    CP("dve", Dsp[r].ap, Dq.ap, [Dq], [Dsp[r]])
                    if r < 2:
                        TT_("dve", Dq.ap, Dq.ap, Dsp[r].ap, ALU.subtract, [Dq, Dsp[r]], [Dq])
                    DMA("sync", S.QT[:, DH + r, t0:t0 + TT], Dsp[r].ap, [Dsp[r].b], [S.B_qkv])
                pf = PS()
                for j in range(nsub):
                    TR(pf.ap[0:QB, j * 8:j * 8 + 8], Fp.ap[:, j * QB:(j + 1) * QB], identf, [Fp], [pf])
                kt0 = (S.koff + t0) // 128
                CP("dve", F_T[s].ap[0:QB, kt0:kt0 + nsub, :], pf.ap[0:QB, 0:nsub * 8].rearrange("p (a b) -> p a b", b=8), [pf], [F_T[s]])
            pcb = PS()
            for h in range(NH):
                MM(pcb.ap[:, h * S.ngr:(h + 1) * S.ngr], sel8.ap[:, h, :], Fcs[s].ap, True, True, [sel8, Fcs[s]], [pcb])
            CP("dve", CBC[s].ap.rearrange("p a b -> p (a b)"), pcb.ap[:, 0:NH * S.ngr], [pcb], [CBC[s]])

        def phase_B(S, l):
            s = S.s; QB = S.QB; QG = S.QG; nsg = QG // QB; nkt = S.nkt; koff = S.koff; Tn = S.T; TK = S.TK
            npast_t = koff // 128
            KA = DH + 3
            QTh = [A.bf([Tn], parts=KA), A.bf([Tn], parts=KA)]
            KTh = [A.bf([TK], parts=KA), A.bf([TK], parts=KA)]
            VXh = [A.bf([nkt, VW]), A.bf([nkt, VW])]
            NPT = 4
            pTs = [A.bf([QG]) for _ in range(NPT)]
            biasg = [A.f32([nkt]), A.f32([nkt])]
            rsum = [A.f32([nsg]), A.f32([nsg])]
            if koff > 0:
                ckt = [A.f32([npast_t, DH]), A.f32([npast_t, DH])]
            LA = 2
            cnt = {"pt": 0, "bg": 0, "po": 0, "ps": 0}
            for h in range(NH):
                qt = QTh[h % 2]; kt = KTh[h % 2]; vxh = VXh[h % 2]
                DMA("sync", qt.ap, S.QT[h], [S.B_qkv], [qt.b])
                DMA("sync", kt.ap[:, koff:koff + Tn], S.KT[h], [S.B_qkv, S.B_kones], [kt.b])
                if koff > 0:
                    P.add("dve", lambda e, kt=kt: e.memset(kt.ap[DH:DH + 3, 0:koff], 1.0), [], [kt.b])
                ntile_new = (Tn + 127) // 128
                if Tn >= 128:
                    DMA("sync", vxh.ap[:, npast_t:npast_t + ntile_new, :], S.VX.rearrange("(a p) h d -> p a h d", p=128)[:, :, h, :], [S.B_qkv], [vxh.b])
                else:
                    DMA("sync", vxh.ap[0:Tn, npast_t, :], S.VX[:, h, :], [S.B_qkv], [vxh.b])
                if koff > 0:
                    ck_ = ckt[h % 2]
                    DMA("sync", ck_.ap, ck[l, h].rearrange("(a p) d -> p a d", p=128), [], [ck_.b])
                    for a0 in range(0, npast_t, 4):
                        pt = PSB[cnt["ps"] % 4]; cnt["ps"] += 1
                        for a in range(a0, min(a0 + 4, npast_t)):
                            TR(pt.ap[0:64, (a - a0) * 128:(a - a0 + 1) * 128], ck_.ap[:, a, :], identf, [ck_], [pt])
                        na = min(4, npast_t - a0)
                        CP("dve", kt.ap[0:DH, a0 * 128:(a0 + na) * 128], pt.ap[0:64, 0:na * 128], [pt], [kt])
                    P.add("pool", lambda e, vxh=vxh: e.memset(vxh.ap[:, 0:npast_t, :].rearrange("p a d -> p (a d)"), 1.0), [], [vxh.b])
                    DMA("pool", vxh.ap[:, 0:npast_t, 0:DH], cv[l, h].rearrange("(a p) d -> p a d", p=128), [vxh.b], [vxh.b])
                for G in range(S.ngr):
                    q0 = G * QG
                    last_kt = (koff + q0 + QG - 1) // 128
                    first_diag = (koff + q0) // 128
                    bg = biasg[cnt["bg"] % 2]; cnt["bg"] += 1
                    TS_("dve", bg.ap[:, 0:last_kt + 1], F_T[s].ap[:, 0:last_kt + 1, h], -1.0, CBC[s].ap[:, h, G:G + 1], ALU.mult, ALU.add, [F_T[s], CBC[s]], [bg])
                    po = PSB[6 + (cnt["po"] % 2)]; cnt["po"] += 1
                    pov = po.ap[:, 0:nsg * 128].rearrange("p (i d) -> p i d", d=128)
                    blocks = []
                    for k_ in range(last_kt + 1):
                        kp = min(128, TK - k_ * 128)
                        j = max(0, k_ - first_diag)
                        blocks.append((k_, kp, j))

                    def score(bi):
                        k_, kp, j = blocks[bi]
                        psc = PSB[cnt["ps"] % 4]; cnt["ps"] += 1
                        c0 = j * QB
                        diag = k_ >= first_diag
                        MM(psc.ap[0:kp, c0:QG], kt.ap[:, k_ * 128:k_ * 128 + kp], qt.ap[:, q0 + c0:q0 + QG], True, not diag, [kt, qt], [psc])
                        if diag:
                            MM(psc.ap[0:kp, c0:c0 + QB], maskT.ap[0:QB, 0:kp], identb.ap[0:QB, 0:QB], False, True, [maskT, identb], [psc])
                        pT = pTs[cnt["pt"] % NPT]; cnt["pt"] += 1
                        ACT(pT.ap[0:kp, c0:QG], psc.ap[0:kp, c0:QG], AF.Exp, [psc, bg], [pT], bias=bg.ap[0:kp, k_:k_ + 1], scale=1.0)
                        return pT

                    def pv(bi, pT):
                        k_, kp, j = blocks[bi]
                        for i in range(j, nsg):
                            first = (bi == 0 and i == j)
                            last = (bi == len(blocks) - 1 and i == nsg - 1)
                            MM(pov[0:QB, i, 0:DH + 1], pT.ap[0:kp, i * QB:(i + 1) * QB], vxh.ap[0:kp, k_, 0:DH + 1], first, last, [pT, vxh], [po])

                    pend = []
                    nb = len(blocks)
                    for bi in range(nb + LA):
                        if bi < nb:
                            pend.append((bi, score(bi)))
                        if bi >= LA:
                            b2, pT2 = pend.pop(0)
                            pv(b2, pT2)
                    rs = rsum[G % 2]
                    P.add("dve", lambda e, rs=rs, pov=pov: e.reciprocal(out=rs.ap[0:QB, :], in_=pov[0:QB, :, DH]), [po.b], [rs.b])
                    TT_("dve", ATT[s].ap[0:QB, G * nsg:(G + 1) * nsg, h * DH:(h + 1) * DH], pov[0:QB, :, 0:DH], rs.ap[0:QB, :].unsqueeze(2).to_broadcast([QB, nsg, DH]), ALU.mult, [po, rs], [ATT[s]])

        def s5_prep(l):
            tb = {}
            Ainv = A.f32([16, 2, 128]); ApT = A.f32([16, 2, 128]); T2 = A.bf([4, 2, 128]); CT = A.bf([4, 2, 128])
            Dd = A.bf([4, 128]); a1 = A.f32([16, 2]); WGLU = A.bf([4, DSSM]); T2f = A.bf([4, 4, 2, 128]); CTn = A.bf([4, 128])
            tb.update(Ainv=Ainv, ApT=ApT, T2=T2, CT=CT, Dd=Dd, a1=a1, WGLU=WGLU, T2f=T2f, CTn=CTn)
            mark = A.off
            tb["mark"] = mark
            are_n = A.f32([128], parts=16); aim_n = A.f32([128], parts=16); ldt = A.f32([2], parts=16)
            DMA("sync", are_n.ap, ssm_a_re[l].rearrange("(a b) p -> a (b p)", b=2), [], [are_n.b])
            DMA("sync", aim_n.ap, ssm_a_im[l].rearrange("(a b) p -> a (b p)", b=2), [], [aim_n.b])
            DMA("sync", ldt.ap, ssm_log_dt[l].rearrange("(a b) -> a b", b=2), [], [ldt.b])
            ACT(ldt.ap, ldt.ap, AF.Exp, [ldt], [ldt])
            al_n = A.f32([128], parts=16); th_n = A.f32([128], parts=16)
            dtb = ldt.ap.unsqueeze(2).to_broadcast([16, 2, 64])
            TT_("dve", al_n.ap.rearrange("p (a b) -> p a b", a=2), are_n.ap.rearrange("p (a b) -> p a b", a=2), dtb, ALU.mult, [are_n, ldt], [al_n])
            TT_("dve", th_n.ap.rearrange("p (a b) -> p a b", a=2), aim_n.ap.rearrange("p (a b) -> p a b", a=2), dtb, ALU.mult, [aim_n, ldt], [th_n])
            sm = A.f32([4, 16])
            pt = PS()
            for i, src in enumerate([al_n, th_n, are_n, aim_n]):
                TR(pt.ap[:, i * 16:(i + 1) * 16], src.ap, identf, [src], [pt])
            CP("dve", sm.ap.rearrange("p a b -> p (a b)"), pt.ap[:, 0:64], [pt], [sm])

            def sincos(ang, shape_elems, cosv, sinv, tmpk):
                TS_("dve", tmpk.ap, ang.ap, 1.0 / TWO_PI, MAGIC, ALU.mult, ALU.add, [ang], [tmpk])
                TS_("dve", tmpk.ap, tmpk.ap, -MAGIC, None, ALU.add, None, [tmpk], [tmpk])
                STT(ang.ap, tmpk.ap, -CW1, ang.ap, ALU.mult, ALU.add, [tmpk, ang], [ang])
                STT(ang.ap, tmpk.ap, -CW2, ang.ap, ALU.mult, ALU.add, [tmpk, ang], [ang])
                TS_("dve", ang.ap, ang.ap, PI_LO, -PI_LO, ALU.min, ALU.max, [ang], [ang])
                ACT(sinv.ap, ang.ap, AF.Sin, [ang], [sinv])
                ACT(tmpk.ap, ang.ap, AF.Abs, [ang], [tmpk])
                ACT(cosv.ap, tmpk.ap, AF.Sin, [tmpk, halfpi], [cosv], bias=halfpi.ap[0:cosv.ap.shape[0]], scale=-1.0)

            angS = A.f32([16, 128]); kS = A.f32([16, 128]); cS = A.f32([16, 128]); sS = A.f32([16, 128]); eS = A.f32([16, 128])
            trb = trow.ap.unsqueeze(1).to_broadcast([128, 16, 128])
            TT_("dve", angS.ap, sm.ap[:, 1, :].unsqueeze(2).to_broadcast([128, 16, 128]), trb, ALU.mult, [sm, trow], [angS])
            TT_("dve", eS.ap, sm.ap[:, 0, :].unsqueeze(2).to_broadcast([128, 16, 128]), trb, ALU.mult, [sm, trow], [eS])
            ACT(eS.ap, eS.ap, AF.Exp, [eS], [eS])
            sincos(angS, 2048, cS, sS, kS)
            TT_("dve", ApT.ap[:, :, 0, :], eS.ap, cS.ap, ALU.mult, [eS, cS], [ApT])
            TT_("dve", ApT.ap[:, :, 1, :], eS.ap, sS.ap, ALU.mult, [eS, sS], [ApT])
            CP("dve", a1.ap, ApT.ap[:, :, :, 1], [ApT], [a1])
            if stop_after == "Ca":
                return tb
            cf = A.f32([8, 16])
            are = sm.ap[:, 2, :]; aim = sm.ap[:, 3, :]
            TS_("dve", cf.ap[:, 0, :], a1.ap[:, :, 0], -1.0, None, ALU.add, None, [a1], [cf])
            TT_("dve", cf.ap[:, 1, :], are, are, ALU.mult, [sm], [cf])
            TT_("dve", cf.ap[:, 2, :], aim, aim, ALU.mult, [sm], [cf])
            TT_("dve", cf.ap[:, 1, :], cf.ap[:, 1, :], cf.ap[:, 2, :], ALU.add, [cf], [cf])
            P.add("dve", lambda e: e.reciprocal(out=cf.ap[:, 1, :], in_=cf.ap[:, 1, :]), [cf.b], [cf.b])
            TT_("dve", cf.ap[:, 2, :], cf.ap[:, 0, :], are, ALU.mult, [cf, sm], [cf])
            TT_("dve", cf.ap[:, 3, :], a1.ap[:, :, 1], aim, ALU.mult, [a1, sm], [cf])
            TT_("dve", cf.ap[:, 2, :], cf.ap[:, 2, :], cf.ap[:, 3, :], ALU.add, [cf], [cf])
            TT_("dve", cf.ap[:, 4, :], cf.ap[:, 2, :], cf.ap[:, 1, :], ALU.mult, [cf], [cf])
            TT_("dve", cf.ap[:, 2, :], a1.ap[:, :, 1], are, ALU.mult, [a1, sm], [cf])
            TT_("dve", cf.ap[:, 3, :], cf.ap[:, 0, :], aim, ALU.mult, [cf, sm], [cf])
            TT_("dve", cf.ap[:, 2, :], cf.ap[:, 2, :], cf.ap[:, 3, :], ALU.subtract, [cf], [cf])
            TT_("dve", cf.ap[:, 5, :], cf.ap[:, 2, :], cf.ap[:, 1, :], ALU.mult, [cf], [cf])
            if stop_after == "Cb":
                return tb
            bre = A.f32([16, 16]); bim = A.f32([16, 16]); Z = A.f32([16, 2, 2, 16]); tq = A.f32([16, 16])
            for g2 in range(2):
                DMA("sync", bre.ap[g2 * 64:(g2 + 1) * 64], ssm_b_re[l].rearrange("(a b) p m -> b p a m", b=2)[g2], [], [bre.b])
                DMA("sync", bim.ap[g2 * 64:(g2 + 1) * 64], ssm_b_im[l].rearrange("(a b) p m -> b p a m", b=2)[g2], [], [bim.b])
            MSET("pool", Z, 0.0)
            cre_b = cf.ap[:, 4, :].unsqueeze(2).to_broadcast([128, 16, 16])
            cim_b = cf.ap[:, 5, :].unsqueeze(2).to_broadcast([128, 16, 16])
            for g2 in range(2):
                ps_ = slice(g2 * 64, (g2 + 1) * 64)
                TT_("dve", tq.ap[ps_], bim.ap[ps_], cim_b[ps_], ALU.mult, [bim, cf], [tq])
                TT_("dve", Z.ap[ps_, :, 0, g2, :], bre.ap[ps_], cre_b[ps_], ALU.mult, [bre, cf], [Z])
                TT_("dve", Z.ap[ps_, :, 0, g2, :], Z.ap[ps_, :, 0, g2, :], tq.ap[ps_], ALU.subtract, [Z, tq], [Z])
                TT_("dve", tq.ap[ps_], bre.ap[ps_], cim_b[ps_], ALU.mult, [bre, cf], [tq])
                TT_("dve", Z.ap[ps_, :, 1, g2, :], bim.ap[ps_], cre_b[ps_], ALU.mult, [bim, cf], [Z])
                TT_("dve", Z.ap[ps_, :, 1, g2, :], Z.ap[ps_, :, 1, g2, :], tq.ap[ps_], ALU.add, [Z, tq], [Z])
            for ch in range(4):
                for ri in range(2):
                    pt = PS()
                    zin = A.f32([128])
                    CP("pool", zin.ap.rearrange("p (a b c) -> p a b c", a=4, b=2), Z.ap[:, ch * 4:(ch + 1) * 4, ri, :, :], [Z], [zin])
                    TR(pt.ap[:, 0:128], zin.ap, identf, [zin], [pt])
                    CP("dve", T2.ap[:, ch, ri, :], pt.ap[:, 0:128], [pt], [T2])
            MSET("pool", T2f, 0.0)
            for i4 in range(4):
                CP("pool", T2f.ap[i4 * 32:(i4 + 1) * 32, :, i4, :, :], T2.ap[i4 * 32:(i4 + 1) * 32, :, :, :], [T2], [T2f])
            if stop_after == "Cc":
                return tb
            Zc = A.f32([4, 2, 128])
            MSET("pool", Zc, 0.0)
            for ri, csrc in enumerate([ssm_c_re, ssm_c_im]):
                cview = csrc[l].rearrange("(ch i b) m p -> i b m ch p", i=4, b=2)
                for i4 in range(4):
                    for g2 in range(2):
                        r0 = i4 * 32 + g2 * 16
                        DMA("sync", Zc.ap[r0:r0 + 16, :, ri, g2 * 64:(g2 + 1) * 64], cview[i4, g2], [], [Zc.b])
            for ch in range(4):
                for ri in range(2):
                    pt = PS()
                    TR(pt.ap[:, 0:128], Zc.ap[:, ch, ri, :], identf, [Zc], [pt])
                    if ri == 0:
                        CP("dve", CT.ap[:, ch, ri, :], pt.ap[:, 0:128], [pt], [CT])
                        TS_("dve", CTn.ap[:, ch, :], pt.ap[:, 0:128], -1.0, None, ALU.mult, None, [pt], [CTn])
                    else:
                        TS_("dve", CT.ap[:, ch, ri, :], pt.ap[:, 0:128], -1.0, None, ALU.mult, None, [pt], [CT])
            dcol = A.f32([4]); dnat = A.f32([128], parts=4)
            DMA("sync", dnat.ap, ssm_d[l].rearrange("(c p) -> c p", p=128), [], [dnat.b])
            pt = PS()
            TR(pt.ap[:, 0:4], dnat.ap, identf, [dnat], [pt])
            CP("dve", dcol.ap, pt.ap[:, 0:4], [pt], [dcol])
            for ch in range(4):
                TS_("dve", Dd.ap[:, ch, :], identf.ap, dcol.ap[:, ch:ch + 1], None, ALU.mult, None, [identf, dcol], [Dd])
            if stop_after == "Cd":
                return tb
            arb = A.f32([2048]); aib = A.f32([2048]); ldb = A.f32([32])
            DMA("sync", arb.ap, ssm_a_re[l].rearrange("g p -> (g p)").partition_broadcast(128), [], [arb.b])
            DMA("sync", aib.ap, ssm_a_im[l].rearrange("g p -> (g p)").partition_broadcast(128), [], [aib.b])
            DMA("sync", ldb.ap, ssm_log_dt[l].partition_broadcast(128), [], [ldb.b])
            ACT(ldb.ap, ldb.ap, AF.Exp, [ldb], [ldb])
            dtbb = ldb.ap.unsqueeze(2).to_broadcast([128, 32, 64])
            TT_("dve", arb.ap.rearrange("p (g q) -> p g q", q=64), arb.ap.rearrange("p (g q) -> p g q", q=64), dtbb, ALU.mult, [arb, ldb], [arb])
            TT_("dve", aib.ap.rearrange("p (g q) -> p g q", q=64), aib.ap.rearrange("p (g q) -> p g q", q=64), dtbb, ALU.mult, [aib, ldb], [aib])
            angT = angS; kT_ = kS; cT_ = cS; sT_ = sS; eT = eS
            fl = lambda t: t.ap.rearrange("p a b -> p (a b)")
            TS_("dve", fl(angT), aib.ap, tcol.ap, None, ALU.mult, None, [aib, tcol], [angT])
            ACT(fl(eT), arb.ap, AF.Exp, [arb, ntcol], [eT], scale=ntcol.ap)
            sincos(angT, 2048, cT_, sT_, kT_)
            TT_("dve", Ainv.ap[:, :, 0, :], eT.ap, cT_.ap, ALU.mult, [eT, cT_], [Ainv])
            STT(Ainv.ap[:, :, 1, :], eT.ap, -1.0, sT_.ap, ALU.mult, ALU.mult, [eT, sT_], [Ainv])
            DMA("pool", WGLU.ap, w_glu[l].rearrange("(kc p) n -> p kc n", p=128), [], [WGLU.b])
            tb["mark"] = mark
            return tb

        def phase_C(S, l, tb, hc_init_from=None):
            s = S.s; QB = S.QB; nblk = S.T // QB
            Ainv = tb["Ainv"]; ApT = tb["ApT"]; T2f = tb["T2f"]; CT = tb["CT"]; Dd = tb["Dd"]; a1 = tb["a1"]; WGLU = tb["WGLU"]; CTn = tb["CTn"]
            uTb = [A.bf([4, QB]), A.bf([4, QB])]
            bus = [A.f32([16, 2, 128]), A.f32([16, 2, 128])]
            cus = A.f32([16, 2, 128])
            t1 = A.bf([16, 128]); t2 = A.bf([16, 128]); t3 = A.bf([16, 128]); t4 = A.bf([16, 128])
            w1 = A.bf([16, 128]); w2 = A.bf([16, 128]); w3 = A.bf([16, 128]); w4 = A.bf([16, 128])
            hl = A.f32([16, 2]); hc = A.f32([16, 2]); hq = A.f32([16, 2])
            y2 = A.f32([DSSM]); yin = A.f32([DSSM]); zt = A.bf([DSSM]); zT = A.bf([4, 128]); sg = A.f32([4, 128]); soT = A.bf([4, 128])
            hn = A.f32([128], parts=16)

            def carry_from_hl():
                TT_("dve", hq.ap[:, :, 0], a1.ap[:, :, 0], hl.ap[:, :, 0], ALU.mult, [a1, hl], [hq])
                TT_("dve", hq.ap[:, :, 1], a1.ap[:, :, 1], hl.ap[:, :, 1], ALU.mult, [a1, hl], [hq])
                TT_("dve", hc.ap[:, :, 0], hq.ap[:, :, 0], hq.ap[:, :, 1], ALU.subtract, [hq], [hc])
                TT_("dve", hq.ap[:, :, 0], a1.ap[:, :, 0], hl.ap[:, :, 1], ALU.mult, [a1, hl], [hq])
                TT_("dve", hq.ap[:, :, 1], a1.ap[:, :, 1], hl.ap[:, :, 0], ALU.mult, [a1, hl], [hq])
                TT_("dve", hc.ap[:, :, 1], hq.ap[:, :, 0], hq.ap[:, :, 1], ALU.add, [hq], [hc])

            if S.koff > 0:
                for ri, src in enumerate([sre, sim]):
                    DMA("sync", hn.ap, src[l].rearrange("(a b) p -> a (b p)", b=2), [], [hn.b])
                    pt = PSB[5]
                    TR(pt.ap[:, 0:16], hn.ap, identf, [hn], [pt])
                    CP("dve", hl.ap[:, :, ri], pt.ap[:, 0:16], [pt], [hl])
                carry_from_hl()
            else:
                MSET("dve", hc, 0.0)

            def stage1(blk):
                t0 = blk * QB
                ut = uTb[blk % 2]; bu = bus[blk % 2]
                DMA("sync", ut.ap, S.uT.rearrange("(c p) t -> p c t", p=128)[:, :, t0:t0 + QB], [S.B_uT], [ut.b])
                for ch in range(4):
                    pb = [PSB[(ch % 2) * 2], PSB[(ch % 2) * 2 + 1]]
                    for hf_ in range(2):
                        MM(pb[hf_].ap[0:QB, :], ut.ap[:, ch, :], T2f.ap[:, ch, hf_ * 2:hf_ * 2 + 2, :, :].rearrange("p i r q -> p (i r q)"), True, True, [ut, T2f], [pb[hf_]])
                        CP("act", bu.ap[0:QB, ch * 4 + hf_ * 2:ch * 4 + hf_ * 2 + 2].rearrange("p i r q -> p (i r q)"), pb[hf_].ap[0:QB, :], [pb[hf_]], [bu])

            def stage2a(blk):
                t0 = blk * QB
                ut = uTb[blk % 2]; bu = bus[blk % 2]
                TT_("dve", w1.ap[0:QB], bu.ap[0:QB, :, 0, :], Ainv.ap[0:QB, :, 0, :], ALU.mult, [bu, Ainv], [w1])
                TT_("pool", w4.ap[0:QB], bu.ap[0:QB, :, 1, :], Ainv.ap[0:QB, :, 0, :], ALU.mult, [bu, Ainv], [w4])
                TT_("pool", w2.ap[0:QB], bu.ap[0:QB, :, 1, :], Ainv.ap[0:QB, :, 1, :], ALU.mult, [bu, Ainv], [w2])
                TT_("dve", w3.ap[0:QB], bu.ap[0:QB, :, 0, :], Ainv.ap[0:QB, :, 1, :], ALU.mult, [bu, Ainv], [w3])

            def stageC(blk):
                t0 = blk * QB
                ut = uTb[blk % 2]; bu = bus[blk % 2]
                for ch in range(4):
                    pc = [PSB[4], PSB[5]]
                    for i4 in range(4):
                        pr = ch * 4 + i4
                        tgt = pc[i4 // 2]
                        cre_ = ((i4 % 2) * 2 + 0) * 128; cim_ = ((i4 % 2) * 2 + 1) * 128
                        MM(tgt.ap[:, cre_:cre_ + QB], w1.ap[0:QB, pr, :], tri.ap[0:QB, 0:QB], True, False, [w1, tri], [tgt])
                        MM(tgt.ap[:, cre_:cre_ + QB], w2.ap[0:QB, pr, :], ntri.ap[0:QB, 0:QB], False, True, [w2, ntri], [tgt])
                        MM(tgt.ap[:, cim_:cim_ + QB], w3.ap[0:QB, pr, :], tri.ap[0:QB, 0:QB], True, False, [w3, tri], [tgt])
                        MM(tgt.ap[:, cim_:cim_ + QB], w4.ap[0:QB, pr, :], tri.ap[0:QB, 0:QB], False, True, [w4, tri], [tgt])
                    for i4 in range(4):
                        pr = ch * 4 + i4
                        for ri in range(2):
                            col = ((i4 % 2) * 2 + ri) * 128
                            ACT(cus.ap[:, pr, ri, 0:QB], pc[i4 // 2].ap[:, col:col + QB], AF.Identity, [pc[i4 // 2], hc], [cus], bias=hc.ap[:, pr, ri:ri + 1], scale=1.0)

            def stageH(blk):
                t0 = blk * QB
                ut = uTb[blk % 2]; bu = bus[blk % 2]
                apr = ApT.ap[:, :, 0, 0:QB]; api = ApT.ap[:, :, 1, 0:QB]
                cre = cus.ap[:, :, 0, 0:QB]; cim = cus.ap[:, :, 1, 0:QB]
                L = QB - 1
                TT_("dve", hq.ap[:, :, 0], ApT.ap[:, :, 0, L], cus.ap[:, :, 0, L], ALU.mult, [ApT, cus], [hq])
                TT_("dve", hq.ap[:, :, 1], ApT.ap[:, :, 1, L], cus.ap[:, :, 1, L], ALU.mult, [ApT, cus], [hq])
                TT_("dve", hl.ap[:, :, 0], hq.ap[:, :, 0], hq.ap[:, :, 1], ALU.subtract, [hq], [hl])
                TT_("dve", hq.ap[:, :, 0], ApT.ap[:, :, 0, L], cus.ap[:, :, 1, L], ALU.mult, [ApT, cus], [hq])
                TT_("dve", hq.ap[:, :, 1], ApT.ap[:, :, 1, L], cus.ap[:, :, 0, L], ALU.mult, [ApT, cus], [hq])
                TT_("dve", hl.ap[:, :, 1], hq.ap[:, :, 0], hq.ap[:, :, 1], ALU.add, [hq], [hl])
                carry_from_hl()
                TT_("dve", t1.ap[:, :, 0:QB], apr, cre, ALU.mult, [ApT, cus], [t1])
                TT_("pool", t4.ap[:, :, 0:QB], api, cre, ALU.mult, [ApT, cus], [t4])
                TT_("pool", t2.ap[:, :, 0:QB], api, cim, ALU.mult, [ApT, cus], [t2])
                TT_("dve", t3.ap[:, :, 0:QB], apr, cim, ALU.mult, [ApT, cus], [t3])
                py = PSB[6]
                for ch in range(4):
                    MM(py.ap[0:QB, ch * 128:(ch + 1) * 128], ut.ap[:, ch, :], Dd.ap[:, ch, :], ch == 0, False, [ut, Dd], [py])
                for ch in range(4):
                    for i4 in range(4):
                        pr = ch * 4 + i4
                        o_ = py.ap[0:QB, pr * 32:(pr + 1) * 32]
                        cs_ = slice(i4 * 32, (i4 + 1) * 32)
                        MM(o_, t1.ap[:, pr, 0:QB], CT.ap[:, ch, 0, cs_], False, False, [t1, CT], [py])
                        MM(o_, t2.ap[:, pr, 0:QB], CTn.ap[:, ch, cs_], False, False, [t2, CTn], [py])
                        MM(o_, t3.ap[:, pr, 0:QB], CT.ap[:, ch, 1, cs_], False, False, [t3, CT], [py])
                        MM(o_, t4.ap[:, pr, 0:QB], CT.ap[:, ch, 1, cs_], False, pr == 15, [t4, CT], [py])

            def stage2c(blk):
                t0 = blk * QB
                ut = uTb[blk % 2]; bu = bus[blk % 2]
                py = PSB[6]
                ACT(y2.ap[0:QB], py.ap[0:QB, :], AF.Square, [py], [y2])
                TS_("dve", y2.ap[0:QB], y2.ap[0:QB], 0.044715, 1.0, ALU.mult, ALU.add, [y2], [y2])
                TT_("dve", yin.ap[0:QB], y2.ap[0:QB], py.ap[0:QB, :], ALU.mult, [y2, py], [yin])
                ACT(yin.ap[0:QB], yin.ap[0:QB], AF.Sigmoid, [yin], [yin], scale=1.5957691216057308)
                TT_("dve", zt.ap[0:QB], yin.ap[0:QB], py.ap[0:QB, :], ALU.mult, [yin, py], [zt])
                pz = PSB[7]
                pzb = pz.ap.bitcast(BF16)
                for c in range(4):
                    TR(pzb[:, c * 128:c * 128 + QB], zt.ap[0:QB, c * 128:(c + 1) * 128], identb, [zt], [pz])
                CP("act", zT.ap[:, :, 0:QB], pzb[:, 0:512].rearrange("p (a b) -> p a b", b=128)[:, :, 0:QB], [pz], [zT])
                pg = PSB[7]
                for m in range(4):
                    for kc in range(4):
                        MM(pg.ap[:, m * 128:m * 128 + QB], WGLU.ap[:, kc, m * 128:(m + 1) * 128], zT.ap[:, kc, 0:QB], kc == 0, kc == 3, [WGLU, zT], [pg])
                ACT(sg.ap[:, :, 0:QB], pg.ap.rearrange("p (a b) -> p a b", b=128)[:, :, 0:QB], AF.Sigmoid, [pg], [sg])
                TT_("dve", soT.ap[:, :, 0:QB], zT.ap[:, :, 0:QB], sg.ap[:, :, 0:QB], ALU.mult, [zT, sg], [soT])
                DMA("sync", S.soT.rearrange("(c p) t -> p c t", p=128)[:, :, t0:t0 + QB], soT.ap[:, :, 0:QB], [soT.b], [S.B_soT])

            stage1(0)
            stage2a(0)
            stageC(0)
            for blk in range(nblk):
                if blk + 1 < nblk:
                    stage1(blk + 1)
                stageH(blk)
                if blk + 1 < nblk:
                    stage2a(blk + 1)
                    stageC(blk + 1)
                stage2c(blk)
            for ri, dst in enumerate([S.nr_o, S.ni_o]):
                pt = PSB[4 + ri]
                hlc = A.f32([16])
                CP("dve", hlc.ap, hl.ap[:, :, ri], [hl], [hlc])
                TR(pt.ap[0:16, 0:128], hlc.ap, identf, [hlc], [pt])
                ho = A.f32([128], parts=16)
                CP("dve", ho.ap, pt.ap[0:16, 0:128], [pt], [ho])
                DMA("sync", dst[l].rearrange("(a b) p -> a (b p)", b=2), ho.ap, [ho.b], [])

        def post_norm_residual(S, l, oT, osq, xt, xn, gslot, rstd, tmp2):
            TT = S.TT
            pss = PS()
            for c in range(8):
                MM(pss.ap[:, 0:TT], onesM.ap, osq.ap[:, c, :], c == 0, c == 7, [onesM, osq], [pss])
            ACT(rstd.ap, pss.ap[:, 0:TT], AF.Sqrt, [pss, epsc], [rstd], bias=epsc.ap, scale=1.0)
            P.add("dve", lambda e: e.reciprocal(out=rstd.ap, in_=rstd.ap), [rstd.b], [rstd.b])
            for c in range(8):
                tm = tmp2[c % 2]
                STT(tm.ap, oT.ap[:, c, :], PAR.ap[:, l, S.s, gslot, c:c + 1], rstd.ap, ALU.mult, ALU.mult, [oT, PAR, rstd], [tm])
                TT_("dve", xn.ap[:, c, :], xt.ap[:, c, :], tm.ap, ALU.add, [xt, tm], [xn])

        def phase_DE(S, l, WOUT):
            s = S.s; TT = S.TT; QB = S.QB; nsub = TT // QB
            xt = A.f32([8, TT])
            xn = A.f32([8, TT]); oT = A.f32([8, TT]); sq = A.bf([8, TT]); osq = sq
            rstd = A.f32([TT]); tmp2 = [A.f32([TT]), A.f32([TT])]
            mixT = A.bf([8, TT]); hf = A.bf([8, TT]); hT = A.bf([NJ, TT]); sgt = A.f32([TT])
            WGs = [A.bf([8, 256]), A.bf([8, 256])]; WUs = [A.bf([8, 256]), A.bf([8, 256])]
            WD = A.bf([NJ, 512])
            wgi = 0
            for i in range(S.nt):
                t0 = i * TT
                DMA("sync", xt.ap, S.xT.rearrange("(c p) t -> p c t", p=128)[:, :, t0:t0 + TT], [S.B_xT[i]], [xt.b])
                DMA("sync", mixT.ap[:, 4:8, :], S.soT.rearrange("(c p) t -> p c t", p=128)[:, :, t0:t0 + TT], [S.B_soT], [mixT.b])
                for j in range(nsub):
                    g = (t0 // QB) + j
                    pa = PS(); pab = pa.ap.bitcast(BF16)
                    for c in range(4):
                        TR(pab[:, c * 128:c * 128 + QB], ATT[s].ap[0:QB, g, c * 128:(c + 1) * 128], identb, [ATT[s]], [pa])
                    CP("act", mixT.ap[:, 0:4, j * QB:(j + 1) * QB], pab[:, 0:512].rearrange("p (a b) -> p a b", b=128)[:, :, 0:QB], [pa], [mixT])
                for m in range(8):
                    po = PS()
                    for kc in range(8):
                        MM(po.ap[:, 0:TT], WOUT.ap[:, kc, m * 128:(m + 1) * 128], mixT.ap[:, kc, :], kc == 0, kc == 7, [WOUT, mixT], [po])
                    CP("dve", oT.ap[:, m, :], po.ap[:, 0:TT], [po], [oT])
                    ACT(osq.ap[:, m, :], po.ap[:, 0:TT], AF.Square, [po], [osq])
                post_norm_residual(S, l, oT, osq, xt, xn, 2, rstd, tmp2)
                rmsnorm_mod(S, l, xn, 3, hf, tmp2, sq, rstd)
                for n0 in range(0, DFF, 256):
                    wg = WGs[wgi % 2]; wu = WUs[wgi % 2]; wgi += 1
                    DMA("sync", wg.ap, WG2[l][:, :, n0:n0 + 256], [B_WFF[l]], [wg.b])
                    DMA("sync", wu.ap, WU2[l][:, :, n0:n0 + 256], [B_WFF[l]], [wu.b])
                    for jj in range(2):
                        j = n0 // 128 + jj
                        pg = PS(); pu = PS()
                        for kc in range(8):
                            MM(pg.ap[:, 0:TT], wg.ap[:, kc, jj * 128:(jj + 1) * 128], hf.ap[:, kc, :], kc == 0, kc == 7, [wg, hf], [pg])
                        for kc in range(8):
                            MM(pu.ap[:, 0:TT], wu.ap[:, kc, jj * 128:(jj + 1) * 128], hf.ap[:, kc, :], kc == 0, kc == 7, [wu, hf], [pu])
                        ACT(sgt.ap, pg.ap[:, 0:TT], AF.Silu, [pg], [sgt])
                        TT_("dve", hT.ap[:, j, :], sgt.ap, pu.ap[:, 0:TT], ALU.mult, [sgt, pu], [hT])
                for half in range(2):
                    DMA("sync", WD.ap, WD2[l][:, :, half * 512:(half + 1) * 512], [B_WFF[l]], [WD.b])
                    for mm in range(4):
                        m = half * 4 + mm
                        pf = PS()
                        for j in range(NJ):
                            MM(pf.ap[:, 0:TT], WD.ap[:, j, mm * 128:(mm + 1) * 128], hT.ap[:, j, :], j == 0, j == NJ - 1, [WD, hT], [pf])
                        CP("dve", oT.ap[:, m, :], pf.ap[:, 0:TT], [pf], [oT])
                        ACT(osq.ap[:, m, :], pf.ap[:, 0:TT], AF.Square, [pf], [osq])
                post_norm_residual(S, l, oT, osq, xn, xt, 5, rstd, tmp2)
                DMA("sync", S.xT.rearrange("(c p) t -> p c t", p=128)[:, :, t0:t0 + TT], xt.ap, [xt.b], [S.B_xT[i]])

        def phase_final():
            A.reset()
            xin_ = [A.f32([8, 128]), A.f32([8, 128])]
            xo_ = [A.f32([D]), A.f32([D])]
            cnt = 0
            for S in STREAMS:
                qb = S.QB
                for i in range(S.T // qb):
                    xi = xin_[cnt % 2]; xo = xo_[cnt % 2]; cnt += 1
                    ti = (i * qb) // S.TT
                    DMA("sync", xi.ap[:, :, 0:qb], S.xT.rearrange("(c p) t -> p c t", p=128)[:, :, i * qb:(i + 1) * qb], [S.B_xT[ti]], [xi.b])
                    for half in range(2):
                        pt = PS()
                        for c4 in range(4):
                            TR(pt.ap[0:qb, c4 * 128:(c4 + 1) * 128], xi.ap[:, half * 4 + c4, 0:qb], identf, [xi], [pt])
                        CP("act" if half == 0 else "dve", xo.ap[0:qb, half * 512:(half + 1) * 512], pt.ap[0:qb, :], [pt], [xo])
                    DMA("sync", S.y_out[i * qb:(i + 1) * qb, :], xo.ap[0:qb, :], [xo.b], [])

        for l in range(DEPTH if stop_after != "XF" else 0):
            P.barrier(); A.reset()
            WIN = A.bf([8, DIN])
            DMA("pool", WIN.ap, w_in[l].rearrange("(kc p) n -> p kc n", p=128), [], [WIN.b])
            mark = A.off
            for S in STREAMS:
                A.off = mark
                phase_A(S, l, WIN)
                P.barrier()
            if stop_after is not None and stop_after.startswith("A"):
                break
            P.barrier(); A.reset()
            for S in STREAMS:
                A.reset()
                phase_B(S, l)
                P.barrier()
            if stop_after == "B":
                break
            P.barrier(); A.reset()
            tb = s5_prep(l)
            P.barrier()
            if stop_after in ("C0", "Ca", "Cb", "Cc", "Cd"):
                break
            A.off = tb["mark"]
            mark = A.off
            for S in STREAMS:
                A.off = mark
                phase_C(S, l, tb)
                P.barrier()
            if stop_after is not None and stop_after.startswith("C"):
                break
            P.barrier(); A.reset()
            WOUT = A.bf([8, D])
            DMA("pool", WOUT.ap, w_out[l].rearrange("(kc p) n -> p kc n", p=128), [], [WOUT.b])
            mark = A.off
            for S in STREAMS:
                A.off = mark
                phase_DE(S, l, WOUT)
                P.barrier()
        P.barrier()
        phase_final()


    body_fn()

    with nc.allow_low_precision("bf16 matmul operands, fp32 accumulate"):
        P.emit()
    P.close()
    return nc


_CACHE = {}


def kernel(x_prompt, x_sample, c_prompt, c_sample, cache_k, cache_v, cache_logf,
           state_ssm_re, state_ssm_im, w_ada, b_ada, g_pre_mix, g_post_mix, g_pre_ffn,
           g_post_ffn, w_in, b_forget, ssm_a_re, ssm_a_im, ssm_log_dt, ssm_b_re, ssm_b_im,
           ssm_c_re, ssm_c_im, ssm_d, w_glu, w_out, w_gate, w_up, w_down, _stop_after=None):
    f = lambda a: np.ascontiguousarray(np.asarray(a, dtype=np.float32))
    x_prompt = f(x_prompt); x_sample = f(x_sample)
    B, T, _ = x_prompt.shape
    BS, TS, _ = x_sample.shape
    DEPTH = w_in.shape[0]
    PAST = cache_k.shape[3]
    NC = 8
    key = (T, TS, PAST, DEPTH, _stop_after)
    if key not in _CACHE:
        _CACHE[key] = build(T, TS, PAST, DEPTH, _stop_after)
    nc = _CACHE[key]
    shared = dict(w_ada=f(w_ada), b_ada=f(b_ada), g_pre_mix=f(g_pre_mix), g_post_mix=f(g_post_mix),
                  g_pre_ffn=f(g_pre_ffn), g_post_ffn=f(g_post_ffn), w_in=f(w_in), b_forget=f(b_forget),
                  ssm_a_re=f(ssm_a_re), ssm_a_im=f(ssm_a_im), ssm_log_dt=f(ssm_log_dt),
                  ssm_b_re=f(ssm_b_re), ssm_b_im=f(ssm_b_im), ssm_c_re=f(ssm_c_re), ssm_c_im=f(ssm_c_im),
                  ssm_d=f(ssm_d), w_glu=f(w_glu), w_out=f(w_out), w_gate=f(w_gate), w_up=f(w_up), w_down=f(w_down))
    cache_k = f(cache_k); cache_v = f(cache_v); cache_logf = f(cache_logf)
    state_ssm_re = f(state_ssm_re); state_ssm_im = f(state_ssm_im)
    c_prompt = f(c_prompt); c_sample = f(c_sample)
    in_maps = []
    for c in range(NC):
        bp = c % B
        bs = c % BS
        m = dict(shared)
        m["xp"] = x_prompt[bp]; m["xs"] = x_sample[bs]
        m["c2"] = np.ascontiguousarray(np.stack([c_prompt[bp], c_sample[bs]], axis=0))
        m["ck"] = np.ascontiguousarray(cache_k[:, bs]); m["cv"] = np.ascontiguousarray(cache_v[:, bs])
        m["clf"] = np.ascontiguousarray(cache_logf[:, bs])
        m["sre"] = np.ascontiguousarray(state_ssm_re[:, bs]); m["sim"] = np.ascontiguousarray(state_ssm_im[:, bs])
        in_maps.append(m)
    res = run_bass_kernel_spmd(nc, in_maps, core_ids=list(range(NC)))
    R = res.results
    yp = np.stack([R[b]["yp"] for b in range(B)], axis=0)
    ys = np.stack([R[b]["ys"] for b in range(BS)], axis=0)
    st = lambda name, n: np.stack([R[b][name] for b in range(n)], axis=1)
    return (yp, ys, st("nkp", B), st("nvp", B), st("nlp", B), st("nrp", B), st("nip", B),
            st("nks", BS), st("nvs", BS), st("nls", BS), st("nrs", BS), st("nis", BS))
```

```python
import contextlib
import math
import numpy as np
import concourse.bass as bass
import concourse.mybir as mybir
from concourse.bass_utils import run_bass_kernel_spmd

F32 = mybir.dt.float32
BF16 = mybir.dt.bfloat16
AF = mybir.ActivationFunctionType
ALU = mybir.AluOpType

ENGS = ["pe", "act", "dve", "pool", "sync"]
COMPUTE = {"pe", "act", "dve", "pool"}
RING = 8
EPOCH = 20000


class Buf:
    __slots__ = ("name", "last_w", "readers", "psum")

    def __init__(self, name="", psum=False):
        self.name = name
        self.last_w = None
        self.readers = []
        self.psum = psum


class Op:
    __slots__ = ("eng", "fn", "dma", "deps", "idx", "signal", "sem", "val", "ringwait")

    def __init__(self, eng, fn, dma):
        self.eng = eng
        self.fn = fn
        self.dma = dma
        self.deps = set()
        self.signal = dma
        self.sem = None
        self.val = 0
        self.ringwait = None


class Prog:
    def __init__(self, nc):
        self.nc = nc
        self.ops = {e: [] for e in ENGS}
        self.allops = []
        self.stack = contextlib.ExitStack()
        self.pending_barrier = {e: None for e in ENGS}

    def sb(self, name, shape, dtype=F32):
        return self.stack.enter_context(self.nc.sbuf_tensor(name, list(shape), dtype))

    def ps(self, name, shape, dtype=F32):
        return self.stack.enter_context(self.nc.psum_tensor(name, list(shape), dtype))

    def buf(self, name=""):
        return Buf(name)

    def barrier(self):
        snap = set()
        for e in ENGS:
            lst = self.ops[e]
            if not lst:
                continue
            snap.add((e, len(lst) - 1))
            cnt = 0
            for i in range(len(lst) - 1, -1, -1):
                if lst[i].dma:
                    snap.add((e, i))
                    cnt += 1
                    if cnt >= RING:
                        break
                if len(lst) - i > 4 * RING + 64:
                    break
        for e in ENGS:
            self.pending_barrier[e] = snap

    def add(self, eng, fn, reads=(), writes=(), dma=False):
        op = Op(eng, fn, dma)
        lst = self.ops[eng]
        op.idx = len(lst)
        me = (eng, op.idx)
        same_ok = (eng == "pe") and not dma
        pb = self.pending_barrier[eng]
        if pb is not None:
            for d in pb:
                if d != me:
                    op.deps.add(d)
            self.pending_barrier[eng] = None
        for b in reads:
            w = b.last_w
            if w is not None and w != me:
                op.deps.add(w)
            if b.psum:
                for r in b.readers:
                    if r[0] != eng:
                        op.deps.add(r)
        for b in writes:
            w = b.last_w
            if w is not None and w != me:
                wop = self.ops[w[0]][w[1]]
                if not (same_ok and w[0] == eng and not wop.dma):
                    op.deps.add(w)
            for r in b.readers:
                if r == me:
                    continue
                rop = self.ops[r[0]][r[1]]
                if same_ok and r[0] == eng and not rop.dma:
                    continue
                op.deps.add(r)
        for b in reads:
            b.readers.append(me)
        for b in writes:
            b.last_w = me
            b.readers = []
        lst.append(op)
        self.allops.append(op)
        return op

    def emit(self):
        nc = self.nc
        for op in self.allops:
            for (e, i) in op.deps:
                self.ops[e][i].signal = True
        for e in ENGS:
            if e in COMPUTE:
                for op in reversed(self.ops[e]):
                    if not op.dma:
                        op.signal = True
                        break
        nsem = [0]

        def newsem(tag):
            nsem[0] += 1
            return self.stack.enter_context(nc.semaphore(f"s_{tag}_{nsem[0]}"))

        for e in ENGS:
            cur = None
            cnt = 0
            ring = None
            ringcnt = None
            ringprev = None
            nd = 0
            for op in self.ops[e]:
                if op.dma:
                    if ring is None or nd >= (EPOCH // 16) * RING:
                        ring = [newsem(e + "r") for _ in range(RING)]
                        ringcnt = [0] * RING
                        ringprev = [None] * RING
                        nd = 0
                    slot = nd % RING
                    if ringprev[slot] is not None:
                        op.ringwait = ringprev[slot]
                    ringcnt[slot] += 16
                    op.sem = ring[slot]
                    op.val = ringcnt[slot]
                    ringprev[slot] = (op.sem, op.val)
                    nd += 1
                elif op.signal:
                    if cur is None or cnt >= EPOCH:
                        cur = newsem(e)
                        cnt = 0
                    cnt += 1
                    op.sem = cur
                    op.val = cnt
        self.nsem = nsem[0]
        finals = {}
        for e in ENGS:
            for op in self.ops[e]:
                if op.dma:
                    k = id(op.sem)
                    if k not in finals or finals[k][1] < op.val:
                        finals[k] = (op.sem, op.val)
            if e in COMPUTE:
                for op in reversed(self.ops[e]):
                    if not op.dma:
                        finals[id(op.sem)] = (op.sem, op.val)
                        break
        prog = self

        def emit_engine(e, eng):
            known = {}
            for op in prog.ops[e]:
                waits = {}
                for (de, di) in op.deps:
                    dop = prog.ops[de][di]
                    k = id(dop.sem)
                    if k not in waits or waits[k][1] < dop.val:
                        waits[k] = (dop.sem, dop.val)
                if op.ringwait is not None:
                    k = id(op.ringwait[0])
                    if k not in waits or waits[k][1] < op.ringwait[1]:
                        waits[k] = op.ringwait
                for k, (s, v) in waits.items():
                    if known.get(k, 0) >= v:
                        continue
                    known[k] = v
                    eng.wait_ge(s, v)
                ins = op.fn(eng)
                if op.signal:
                    ins.then_inc(op.sem, 16 if op.dma else 1)
            if e == "sync":
                for k, (s, v) in finals.items():
                    if known.get(k, 0) < v:
                        eng.wait_ge(s, v)

        with nc.Block() as block:
            @block.tensor
            def _(eng):
                emit_engine("pe", eng)

            @block.scalar
            def _(eng):
                emit_engine("act", eng)

            @block.vector
            def _(eng):
                emit_engine("dve", eng)

            @block.gpsimd
            def _(eng):
                emit_engine("pool", eng)

            @block.sync
            def _(eng):
                emit_engine("sync", eng)

    def close(self):
        self.stack.close()


D = 1024
NH = 8
DH = 64
DATT = 512
DSSM = 512
DFF = 2816
DIN = 2056
NG = 32
NST = 64
EPS = 1e-6
NJ = DFF // 128
VW = 72
MAGIC = 12582912.0
TWO_PI = 2.0 * math.pi
CW1 = 6.28125
CW2 = float(np.float32(TWO_PI - 6.28125))
PI_LO = 3.1415925


class Tile:
    __slots__ = ("ap", "b")

    def __init__(self, ap, b):
        self.ap = ap
        self.b = b


def build(T, TS, PAST, DEPTH, stop_after=None):
    nc = bass.Bass("TRN2", target_bir_lowering=False)
    P = Prog(nc)

    def din(name, shape):
        return nc.dram_tensor(name, list(shape), F32, kind="ExternalInput").ap()

    def dout(name, shape):
        return nc.dram_tensor(name, list(shape), F32, kind="ExternalOutput").ap()

    def dscr(name, shape, dt=F32):
        return nc.dram_tensor(name, list(shape), dt, kind="Internal").ap()

    xp = din("xp", [T, D]); xs = din("xs", [TS, D]); c2 = din("c2", [2, D])
    ck = din("ck", [DEPTH, NH, PAST, DH]); cv = din("cv", [DEPTH, NH, PAST, DH])
    clf = din("clf", [DEPTH, NH, PAST])
    sre = din("sre", [DEPTH, NG, NST]); sim = din("sim", [DEPTH, NG, NST])
    w_ada = din("w_ada", [DEPTH, D, 6 * D]); b_ada = din("b_ada", [DEPTH, 6 * D])
    g_pre_mix = din("g_pre_mix", [DEPTH, D]); g_post_mix = din("g_post_mix", [DEPTH, D])
    g_pre_ffn = din("g_pre_ffn", [DEPTH, D]); g_post_ffn = din("g_post_ffn", [DEPTH, D])
    w_in = din("w_in", [DEPTH, D, DIN]); b_forget = din("b_forget", [DEPTH, NH])
    ssm_a_re = din("ssm_a_re", [DEPTH, NG, NST]); ssm_a_im = din("ssm_a_im", [DEPTH, NG, NST])
    ssm_log_dt = din("ssm_log_dt", [DEPTH, NG])
    ssm_b_re = din("ssm_b_re", [DEPTH, NG, NST, 16]); ssm_b_im = din("ssm_b_im", [DEPTH, NG, NST, 16])
    ssm_c_re = din("ssm_c_re", [DEPTH, NG, 16, NST]); ssm_c_im = din("ssm_c_im", [DEPTH, NG, 16, NST])
    ssm_d = din("ssm_d", [DEPTH, DSSM]); w_glu = din("w_glu", [DEPTH, DSSM, DSSM])
    w_out = din("w_out", [DEPTH, D, D]); w_gate = din("w_gate", [DEPTH, D, DFF])
    w_up = din("w_up", [DEPTH, D, DFF]); w_down = din("w_down", [DEPTH, DFF, D])

    yp = dout("yp", [T, D]); ys = dout("ys", [TS, D])
    nkp = dout("nkp", [DEPTH, NH, T, DH]); nvp = dout("nvp", [DEPTH, NH, T, DH])
    nlp = dout("nlp", [DEPTH, NH, T])
    nrp = dout("nrp", [DEPTH, NG, NST]); nip = dout("nip", [DEPTH, NG, NST])
    nks = dout("nks", [DEPTH, NH, TS, DH]); nvs = dout("nvs", [DEPTH, NH, TS, DH])
    nls = dout("nls", [DEPTH, NH, TS])
    nrs = dout("nrs", [DEPTH, NG, NST]); nis = dout("nis", [DEPTH, NG, NST])

    WG2 = dscr("WG2", [DEPTH, 128, 8, DFF], BF16)
    WU2 = dscr("WU2", [DEPTH, 128, 8, DFF], BF16)
    WD2 = dscr("WD2", [DEPTH, 128, NJ, D], BF16)
    B_WFF = [P.buf() for _ in range(DEPTH)]

    class Stream:
        pass

    def mk_stream(name, s, Tn, TT, koff, x_in, y_out, nk_o, nv_o, nl_o, nr_o, ni_o):
        S = Stream()
        S.name = name; S.s = s; S.T = Tn; S.TT = TT; S.nt = Tn // TT; S.koff = koff
        S.QB = min(128, Tn)
        S.nqg = Tn // S.QB
        S.QG = min(512, Tn)
        S.ngr = Tn // S.QG
        S.TK = koff + Tn
        S.nkt = (S.TK + 127) // 128
        S.x_in = x_in; S.y_out = y_out
        S.nk_o = nk_o; S.nv_o = nv_o; S.nl_o = nl_o; S.nr_o = nr_o; S.ni_o = ni_o
        S.xT = dscr(name + "_xT", [D, Tn]); S.B_xT = [P.buf() for _ in range(S.nt)]
        S.QT = dscr(name + "_QT", [NH, DH + 3, Tn], BF16)
        S.KT = dscr(name + "_KT", [NH, DH + 3, Tn], BF16)
        S.VX = dscr(name + "_VX", [Tn, NH, VW], BF16)
        S.B_qkv = P.buf(); S.B_kones = P.buf()
        S.uT = dscr(name + "_uT", [DSSM, Tn], BF16); S.B_uT = P.buf()
        S.soT = dscr(name + "_soT", [DSSM, Tn], BF16); S.B_soT = P.buf()
        return S

    SP = mk_stream("p", 0, T, min(512, T), 0, xp, yp, nkp, nvp, nlp, nrp, nip)
    SS = mk_stream("s", 1, TS, TS, PAST, xs, ys, nks, nvs, nls, nrs, nis)
    STREAMS = [SP, SS]

    def ptile(name, shape, dt=F32):
        h = P.sb(name, shape, dt)
        return Tile(h[tuple(slice(None) for _ in shape)], P.buf(name))

    identf = ptile("identf", [128, 128]); identb = ptile("identb", [128, 128], BF16)
    onesM = ptile("onesM", [128, 128], BF16); tri = ptile("tri", [128, 128], BF16)
    maskT = ptile("maskT", [128, 128], BF16); onesbf = ptile("onesbf", [8, 512], BF16)
    ntri = ptile("ntri", [128, 128], BF16)
    ones8 = ptile("ones8", [8, 512]); epsc = ptile("epsc", [128, 1]); one1 = ptile("one1", [128, 1])
    halfpi = ptile("halfpi", [128, 1]); trow = ptile("trow", [128, 128]); tcol = ptile("tcol", [128, 1])
    ntcol = ptile("ntcol", [128, 1]); sel8 = ptile("sel8", [8, NH, 128])
    PAR = ptile("PAR", [128, DEPTH, 2, 6, 8])
    scT = ptile("scT", [128, 8, 2], BF16)
    ATT = [ptile("ATTp", [128, SP.nqg, DATT], BF16), ptile("ATTs", [SS.QB, 1, DATT], BF16)]
    F_T = [ptile("F_Tp", [128, SP.nkt, NH]), ptile("F_Ts", [128, SS.nkt, NH])]
    CBC = [ptile("CBCp", [128, NH, SP.ngr]), ptile("CBCs", [128, NH, SS.ngr])]
    Fcs = [ptile("Fcp", [8, SP.ngr]), ptile("Fcs", [8, SS.ngr])]
    negb = ptile("negb", [8, 1])
    fcarry = ptile("fcarry", [8, 1])

    PSB = [Tile(P.ps(f"psb{i}", [128, 512])[:, :], Buf(f"psb{i}", psum=True)) for i in range(8)]

    ARW = 40960
    AR = P.sb("arena", [128, ARW])

    class Arena:
        def __init__(self):
            self.off = 0

        def reset(self):
            self.off = 0

        def get(self, nwords_f32, dt=F32, shape=None, parts=128):
            n = (nwords_f32 + 7) // 8 * 8
            assert self.off + n <= ARW, ("arena overflow", self.off, n)
            ap = AR[0:parts, self.off:self.off + nwords_f32]
            self.off += n
            if dt != F32:
                ap = ap.bitcast(dt)
            if shape is not None and len(shape) > 1:
                if len(shape) == 2:
                    ap = ap.rearrange("p (a b) -> p a b", a=shape[0])
                elif len(shape) == 3:
                    ap = ap.rearrange("p (a b c) -> p a b c", a=shape[0], b=shape[1])
                elif len(shape) == 4:
                    ap = ap.rearrange("p (a b c d) -> p a b c d", a=shape[0], b=shape[1], c=shape[2])
            return Tile(ap, P.buf())

        def f32(self, shape, parts=128):
            return self.get(int(np.prod(shape)), F32, shape, parts)

        def bf(self, shape, parts=128):
            n = int(np.prod(shape))
            assert n % 2 == 0
            return self.get(n // 2, BF16, shape, parts)

    A = Arena()

    def rb(ts):
        return [t.b for t in ts]

    def MM(out, lhsT, rhs, start, stop, reads, writes, **kw):
        P.add("pe", lambda e: e.matmul(out, lhsT=lhsT, rhs=rhs, start=start, stop=stop, **kw), rb(reads), rb(writes))

    def TR(out, in_, ident, reads, writes):
        P.add("pe", lambda e: e.transpose(out, in_, ident.ap[0:in_.shape[0], 0:in_.shape[0]]), rb(reads) + [ident.b], rb(writes))

    def ACT(out, in_, func, reads, writes, bias=None, scale=None):
        kw = {}
        if bias is not None:
            kw["bias"] = bias
        if scale is not None:
            kw["scale"] = scale
        P.add("act", lambda e: e.activation(out=out, in_=in_, func=func, **kw), rb(reads), rb(writes))

    def TT_(eng, out, in0, in1, op, reads, writes):
        P.add(eng, lambda e: e.tensor_tensor(out=out, in0=in0, in1=in1, op=op), rb(reads), rb(writes))

    def TS_(eng, out, in0, s1, s2, op0, op1, reads, writes):
        if s2 is None:
            P.add(eng, lambda e: e.tensor_scalar(out=out, in0=in0, scalar1=s1, scalar2=None, op0=op0), rb(reads), rb(writes))
        else:
            P.add(eng, lambda e: e.tensor_scalar(out=out, in0=in0, scalar1=s1, scalar2=s2, op0=op0, op1=op1), rb(reads), rb(writes))

    def STT(out, in0, scalar, in1, op0, op1, reads, writes):
        P.add("dve", lambda e: e.scalar_tensor_tensor(out=out, in0=in0, scalar=scalar, in1=in1, op0=op0, op1=op1), rb(reads), rb(writes))

    def CP(eng, out, in_, reads, writes):
        if eng == "act":
            P.add("act", lambda e: e.copy(out=out, in_=in_), rb(reads), rb(writes))
        else:
            P.add(eng, lambda e: e.tensor_copy(out=out, in_=in_), rb(reads), rb(writes))

    def MSET(eng, t, val):
        P.add(eng, lambda e: e.memset(t.ap, val), [], [t.b])

    def DMA(q, out, in_, reads, writes):
        P.add(q, lambda e: e.dma_start(out=out, in_=in_), reads, writes, dma=True)

    psrr = [0]

    def PS():
        t = PSB[psrr[0] % 8]
        psrr[0] += 1
        return t

    def body_fn():
        MSET("pool", identf, 1.0)
        P.add("pool", lambda e: e.affine_select(out=identf.ap, in_=identf.ap, pattern=[[1, 128]], compare_op=ALU.is_equal, fill=0.0, base=0, channel_multiplier=-1), [identf.b], [identf.b])
        MSET("pool", identb, 1.0)
        P.add("pool", lambda e: e.affine_select(out=identb.ap, in_=identb.ap, pattern=[[1, 128]], compare_op=ALU.is_equal, fill=0.0, base=0, channel_multiplier=-1), [identb.b], [identb.b])
        MSET("pool", tri, 1.0)
        P.add("pool", lambda e: e.affine_select(out=tri.ap, in_=tri.ap, pattern=[[1, 128]], compare_op=ALU.is_ge, fill=0.0, base=0, channel_multiplier=-1), [tri.b], [tri.b])
        MSET("pool", maskT, -30000.0)
        P.add("pool", lambda e: e.affine_select(out=maskT.ap, in_=maskT.ap, pattern=[[1, 128]], compare_op=ALU.is_gt, fill=0.0, base=0, channel_multiplier=-1), [maskT.b], [maskT.b])
        MSET("dve", onesbf, 1.0)
        for S_ in STREAMS:
            for r in range(3):
                for t0_ in range(0, S_.T, 512):
                    tw = min(512, S_.T - t0_)
                    DMA("sync", S_.KT[:, DH + r, t0_:t0_ + tw], onesbf.ap[:, 0:tw], [onesbf.b], [S_.B_kones])
        TS_("dve", ntri.ap, tri.ap, -1.0, None, ALU.mult, None, [tri], [ntri])
        MSET("dve", onesM, 1.0 / 1024.0)
        MSET("dve", ones8, 1.0)
        MSET("dve", epsc, EPS)
        MSET("dve", one1, 1.0)
        MSET("dve", halfpi, math.pi / 2.0)
        P.add("pool", lambda e: e.iota(trow.ap, pattern=[[1, 128]], base=0, channel_multiplier=0, allow_small_or_imprecise_dtypes=True), [], [trow.b])
        P.add("pool", lambda e: e.iota(tcol.ap, pattern=[[0, 1]], base=0, channel_multiplier=1, allow_small_or_imprecise_dtypes=True), [], [tcol.b])
        TS_("dve", ntcol.ap, tcol.ap, -1.0, None, ALU.mult, None, [tcol], [ntcol])
        MSET("pool", sel8, 1.0)
        P.add("pool", lambda e: e.affine_select(out=sel8.ap, in_=sel8.ap, pattern=[[-1, NH], [0, 128]], compare_op=ALU.is_equal, fill=0.0, base=0, channel_multiplier=1), [sel8.b], [sel8.b])

        if stop_after == "P0":
            return
        A.reset()
        c2n = A.f32([D], parts=2)
        DMA("sync", c2n.ap, c2, [], [c2n.b])
        pc = PS()
        for kc in range(8):
            TR(pc.ap[:, kc * 2:kc * 2 + 2], c2n.ap[:, kc * 128:(kc + 1) * 128], identf, [c2n], [pc])
        ACT(scT.ap.rearrange("p a b -> p (a b)"), pc.ap[:, 0:16], AF.Silu, [pc], [scT])
        spn = A.f32([128], parts=80)
        spT = A.f32([80])
        modT = A.f32([48, 2])
        WA = [A.bf([8, 1024]), A.bf([8, 1024])]
        wai = 0
        for l in range(DEPTH):
            DMA("sync", spn.ap[0:48, :], b_ada[l].rearrange("(a b) -> a b", b=128), [], [spn.b])
            for i, g in enumerate([g_pre_mix, g_post_mix, g_pre_ffn, g_post_ffn]):
                DMA("sync", spn.ap[48 + 8 * i:56 + 8 * i, :], g[l].rearrange("(a b) -> a b", b=128), [], [spn.b])
            pt = PS()
            TR(pt.ap[:, 0:80], spn.ap, identf, [spn], [pt])
            CP("dve", spT.ap, pt.ap[:, 0:80], [pt], [spT])
            pm = PS()
            for piece in range(6):
                wa = WA[wai % 2]; wai += 1
                DMA("pool", wa.ap, w_ada[l, :, piece * 1024:(piece + 1) * 1024].rearrange("(kc p) n -> p kc n", p=128), [], [wa.b])
                for j in range(8):
                    cj = piece * 8 + j
                    for kc in range(8):
                        MM(pm.ap[:, cj * 2:cj * 2 + 2], wa.ap[:, kc, j * 128:(j + 1) * 128], scT.ap[:, kc, :], kc == 0, kc == 7, [wa, scT], [pm])
            for s in range(2):
                TT_("dve", modT.ap[:, :, s], pm.ap[:, 0:96].rearrange("p (a b) -> p a b", b=2)[:, :, s], spT.ap[:, 0:48], ALU.add, [pm, spT], [modT])
            for s in range(2):
                par = PAR.ap[:, l, s]
                STT(par[:, 0, :], modT.ap[:, 8:16, s], 1.0, spT.ap[:, 48:56], ALU.add, ALU.mult, [modT, spT], [PAR])
                CP("dve", par[:, 1, :], modT.ap[:, 0:8, s], [modT], [PAR])
                TT_("dve", par[:, 2, :], modT.ap[:, 16:24, s], spT.ap[:, 56:64], ALU.mult, [modT, spT], [PAR])
                STT(par[:, 3, :], modT.ap[:, 32:40, s], 1.0, spT.ap[:, 64:72], ALU.add, ALU.mult, [modT, spT], [PAR])
                CP("dve", par[:, 4, :], modT.ap[:, 24:32, s], [modT], [PAR])
                TT_("dve", par[:, 5, :], modT.ap[:, 40:48, s], spT.ap[:, 72:80], ALU.mult, [modT, spT], [PAR])

        if stop_after == "P1":
            return
        for l in range(DEPTH):
            DMA("pool", WG2[l], w_gate[l].rearrange("(kc p) n -> p kc n", p=128), [], [B_WFF[l]])
            DMA("pool", WU2[l], w_up[l].rearrange("(kc p) n -> p kc n", p=128), [], [B_WFF[l]])
            DMA("pool", WD2[l], w_down[l].rearrange("(j p) n -> p j n", p=128), [], [B_WFF[l]])

        if stop_after == "P2":
            return
        P.barrier()
        A.reset()
        xin = [A.f32([D]), A.f32([D])]
        xtr = [A.f32([8, 128]), A.f32([8, 128])]
        cnt = 0
        for S in STREAMS:
            nb = S.T // S.QB
            for i in range(nb):
                xi = xin[cnt % 2]; xo = xtr[cnt % 2]; cnt += 1
                qb = S.QB
                DMA("sync", xi.ap[0:qb, :], S.x_in[i * qb:(i + 1) * qb, :], [], [xi.b])
                for half in range(2):
                    pt = PS()
                    for c4 in range(4):
                        c = half * 4 + c4
                        TR(pt.ap[:, c4 * 128:c4 * 128 + qb], xi.ap[0:qb, c * 128:(c + 1) * 128], identf, [xi], [pt])
                    CP("act" if half == 0 else "dve", xo.ap[:, half * 4:half * 4 + 4, 0:qb], pt.ap.rearrange("p (a b) -> p a b", b=128)[:, :, 0:qb], [pt], [xo])
                ti = (i * qb) // S.TT
                DMA("sync", S.xT.rearrange("(c p) t -> p c t", p=128)[:, :, i * qb:(i + 1) * qb], xo.ap[:, :, 0:qb], [xo.b], [S.B_xT[ti]])

        if stop_after == "X":
            return
        def rmsnorm_mod(S, l, xt, which, hm, tmp2, sq, rstd):
            TT = S.TT
            ACT(sq.ap, xt.ap, AF.Square, [xt], [sq])
            pss = PS()
            for c in range(8):
                MM(pss.ap[:, 0:TT], onesM.ap, sq.ap[:, c, :], c == 0, c == 7, [onesM, sq], [pss])
            ACT(rstd.ap, pss.ap[:, 0:TT], AF.Sqrt, [pss, epsc], [rstd], bias=epsc.ap, scale=1.0)
            P.add("dve", lambda e: e.reciprocal(out=rstd.ap, in_=rstd.ap), [rstd.b], [rstd.b])
            for c in range(8):
                tm = tmp2[c % 2]
                STT(tm.ap, xt.ap[:, c, :], PAR.ap[:, l, S.s, which, c:c + 1], rstd.ap, ALU.mult, ALU.mult, [xt, PAR, rstd], [tm])
                ACT(hm.ap[:, c, :], tm.ap, AF.Identity, [tm, PAR], [hm], bias=PAR.ap[:, l, S.s, which + 1, c:c + 1], scale=1.0)

        def phase_A(S, l, WIN):
            TT = S.TT; QB = S.QB; nsub = TT // QB
            s = S.s
            xts = [A.f32([8, TT]), A.f32([8, TT])]
            sq = A.bf([8, TT]); rstd = A.f32([TT])
            tmp2 = [A.f32([TT]), A.f32([TT])]
            hms = [A.bf([8, TT]), A.bf([8, TT])]
            qTa = A.bf([NH, TT], parts=64); kTa = A.bf([NH, TT], parts=64)
            ktm = [A.f32([DATT]), A.f32([DATT])]; vtm = [A.f32([DATT]), A.f32([DATT])]
            vx = [A.bf([NH, VW]), A.bf([NH, VW])]
            uTa = A.bf([4, TT])
            gT = A.f32([TT], parts=8); lf = A.f32([TT], parts=8); Fp = A.f32([TT], parts=8)
            Dq = A.f32([TT], parts=8); Dsp = [A.bf([TT], parts=8) for _ in range(3)]
            lfn = 0
            if S.koff > 0:
                MSET("dve", F_T[s], 0.0)
                npast = S.koff
                clt = A.f32([npast], parts=8)
                Fpast = A.f32([npast], parts=8)
                DMA("sync", clt.ap, clf[l], [], [clt.b])
                MSET("dve", fcarry, 0.0)
                cs_ = min(512, npast)
                for i0 in range(0, npast, cs_):
                    P.add("dve", lambda e, i0=i0: e.tensor_tensor_scan(out=Fpast.ap[:, i0:i0 + cs_], data0=ones8.ap[:, 0:cs_], data1=clt.ap[:, i0:i0 + cs_], initial=fcarry.ap, op0=ALU.mult, op1=ALU.add), [ones8.b, clt.b, fcarry.b], [Fpast.b])
                    CP("dve", fcarry.ap, Fpast.ap[:, i0 + cs_ - 1:i0 + cs_], [Fpast], [fcarry])
                pf = PS()
                for kt in range(npast // 128):
                    TR(pf.ap[:, kt * 8:kt * 8 + 8], Fpast.ap[:, kt * 128:(kt + 1) * 128], identf, [Fpast], [pf])
                CP("dve", F_T[s].ap[:, 0:npast // 128, :], pf.ap[:, 0:(npast // 128) * 8].rearrange("p (a b) -> p a b", b=8), [pf], [F_T[s]])
            else:
                MSET("dve", fcarry, 0.0)
            DMA("sync", negb.ap, b_forget[l].rearrange("(a b) -> a b", b=1), [], [negb.b])
            TS_("dve", negb.ap, negb.ap, -1.0, None, ALU.mult, None, [negb], [negb])
            for i in range(S.nt):
                xt = xts[i % 2]; hm = hms[i % 2]
                t0 = i * TT
                DMA("sync", xt.ap, S.xT.rearrange("(c p) t -> p c t", p=128)[:, :, t0:t0 + TT], [S.B_xT[i]], [xt.b])
                rmsnorm_mod(S, l, xt, 0, hm, tmp2, sq, rstd)
                if stop_after == "A1":
                    return
                for h in range(NH):
                    pq = PS()
                    for c in range(8):
                        MM(pq.ap[0:64, 0:TT], WIN.ap[:, c, h * 64:(h + 1) * 64], hm.ap[:, c, :], c == 0, c == 7, [WIN, hm], [pq])
                    ACT(qTa.ap[:, h, :], pq.ap[0:64, 0:TT], AF.Identity, [pq], [qTa], scale=0.125)
                    pk = PS()
                    for c in range(8):
                        MM(pk.ap[0:64, 0:TT], WIN.ap[:, c, DATT + h * 64:DATT + (h + 1) * 64], hm.ap[:, c, :], c == 0, c == 7, [WIN, hm], [pk])
                    CP("dve", kTa.ap[:, h, :], pk.ap[0:64, 0:TT], [pk], [kTa])
                DMA("sync", S.QT.rearrange("h d t -> d h t")[0:DH, :, t0:t0 + TT], qTa.ap, [qTa.b], [S.B_qkv])
                DMA("sync", S.KT.rearrange("h d t -> d h t")[0:DH, :, t0:t0 + TT], kTa.ap, [kTa.b], [S.B_qkv])
                if stop_after == "A2":
                    return
                for j in range(nsub):
                    tb0 = t0 + j * QB
                    kt_ = ktm[j % 2]; vt_ = vtm[j % 2]; vx_ = vx[j % 2]
                    pk = PS()
                    for c in range(8):
                        MM(pk.ap[0:QB, :], hm.ap[:, c, j * QB:(j + 1) * QB], WIN.ap[:, c, DATT:2 * DATT], c == 0, c == 7, [WIN, hm], [pk])
                    CP("act", kt_.ap[0:QB, :], pk.ap[0:QB, :], [pk], [kt_])
                    if stop_after == "A2a":
                        return
                    DMA("sync", S.nk_o[l].rearrange("h t d -> t h d")[tb0:tb0 + QB], kt_.ap[0:QB, :].rearrange("p (h d) -> p h d", d=DH), [kt_.b], [])
                    if stop_after == "A2b":
                        return
                    pv = PS()
                    for c in range(8):
                        MM(pv.ap[0:QB, :], hm.ap[:, c, j * QB:(j + 1) * QB], WIN.ap[:, c, 2 * DATT:3 * DATT], c == 0, c == 7, [WIN, hm], [pv])
                    CP("act", vt_.ap[0:QB, :], pv.ap[0:QB, :], [pv], [vt_])
                    DMA("sync", S.nv_o[l].rearrange("h t d -> t h d")[tb0:tb0 + QB], vt_.ap[0:QB, :].rearrange("p (h d) -> p h d", d=DH), [vt_.b], [])
                    if stop_after == "A2c":
                        return
                    P.add("pool", lambda e, vx_=vx_: e.memset(vx_.ap[0:QB].rearrange("p h d -> p (h d)"), 1.0), [], [vx_.b])
                    CP("dve", vx_.ap[0:QB, :, 0:DH], vt_.ap[0:QB, :].rearrange("p (h d) -> p h d", d=DH), [vt_], [vx_])
                    if stop_after == "A2d":
                        return
                    DMA("sync", S.VX[tb0:tb0 + QB], vx_.ap[0:QB], [vx_.b], [S.B_qkv])
                if stop_after == "A3":
                    return
                for m in range(4):
                    pu = PS()
                    for c in range(8):
                        MM(pu.ap[:, 0:TT], WIN.ap[:, c, 3 * DATT + NH + m * 128:3 * DATT + NH + (m + 1) * 128], hm.ap[:, c, :], c == 0, c == 7, [WIN, hm], [pu])
                    CP("act" if m % 2 else "dve", uTa.ap[:, m, :], pu.ap[:, 0:TT], [pu], [uTa])
                DMA("sync", S.uT.rearrange("(c p) t -> p c t", p=128)[:, :, t0:t0 + TT], uTa.ap, [uTa.b], [S.B_uT])
                if stop_after == "A4":
                    return
                pg = PS()
                for c in range(8):
                    MM(pg.ap[0:8, 0:TT], WIN.ap[:, c, 3 * DATT:3 * DATT + NH], hm.ap[:, c, :], c == 0, c == 7, [WIN, hm], [pg])
                ACT(gT.ap, pg.ap[0:8, 0:TT], AF.Exp, [pg, negb], [gT], bias=negb.ap, scale=-1.0)
                ACT(gT.ap, gT.ap, AF.Ln, [gT, one1], [gT], bias=one1.ap[0:8], scale=1.0)
                TS_("dve", lf.ap, gT.ap, -1.0, None, ALU.mult, None, [gT], [lf])
                DMA("sync", S.nl_o[l][:, t0:t0 + TT], lf.ap, [lf.b], [])
                P.add("dve", lambda e: e.tensor_tensor_scan(out=Fp.ap, data0=ones8.ap[:, 0:TT], data1=lf.ap, initial=fcarry.ap, op0=ALU.mult, op1=ALU.add), [ones8.b, lf.b, fcarry.b], [Fp.b])
                CP("dve", fcarry.ap, Fp.ap[:, TT - 1:TT], [Fp], [fcarry])
                if stop_after == "A5":
                    return
                assert S.TT == S.QG
                CP("dve", Fcs[s].ap[:, i:i + 1], Fp.ap[:, 0:1], [Fp], [Fcs[s]])
                TS_("dve", Dq.ap, Fp.ap, Fcs[s].ap[:, i:i + 1], None, ALU.subtract, None, [Fp, Fcs[s]], [Dq])
                for r in range(3):
                    CP("dve", Dsp[r].ap, Dq.ap, [Dq], [Dsp[r]])
                    if r < 2:
                        TT_("dve", Dq.ap, Dq.ap, Dsp[r].ap, ALU.subtract, [Dq, Dsp[r]], [Dq])
                    DMA("sync", S.QT[:, DH + r, t0:t0 + TT], Dsp[r].ap, [Dsp[r].b], [S.B_qkv])
                pf = PS()
                for j in range(nsub):
                    TR(pf.ap[0:QB, j * 8:j * 8 + 8], Fp.ap[:, j * QB:(j + 1) * QB], identf, [Fp], [pf])
                kt0 = (S.koff + t0) // 128
                CP("dve", F_T[s].ap[0:QB, kt0:kt0 + nsub, :], pf.ap[0:QB, 0:nsub * 8].rearrange("p (a b) -> p a b", b=8), [pf], [F_T[s]])
            pcb = PS()
            for h in range(NH):
                MM(pcb.ap[:, h * S.ngr:(h + 1) * S.ngr], sel8.ap[:, h, :], Fcs[s].ap, True, True, [sel8, Fcs[s]], [pcb])
            CP("dve", CBC[s].ap.rearrange("p a b -> p (a b)"), pcb.ap[:, 0:NH * S.ngr], [pcb], [CBC[s]])

        def phase_B(S, l):
            s = S.s; QB = S.QB; QG = S.QG; nsg = QG // QB; nkt = S.nkt; koff = S.koff; Tn = S.T; TK = S.TK
            npast_t = koff // 128
            KA = DH + 3
            QTh = [A.bf([Tn], parts=KA), A.bf([Tn], parts=KA)]
            KTh = [A.bf([TK], parts=KA), A.bf([TK], parts=KA)]
            VXh = [A.bf([nkt, VW]), A.bf([nkt, VW])]
            NPT = 4
            pTs = [A.bf([QG]) for _ in range(NPT)]
            biasg = [A.f32([nkt]), A.f32([nkt])]
            rsum = [A.f32([nsg]), A.f32([nsg])]
            if koff > 0:
                ckt = [A.f32([npast_t, DH]), A.f32([npast_t, DH])]
            LA = 2
            cnt = {"pt": 0, "bg": 0, "po": 0, "ps": 0}
            for h in range(NH):
                qt = QTh[h % 2]; kt = KTh[h % 2]; vxh = VXh[h % 2]
                DMA("sync", qt.ap, S.QT[h], [S.B_qkv], [qt.b])
                DMA("sync", kt.ap[:, koff:koff + Tn], S.KT[h], [S.B_qkv, S.B_kones], [kt.b])
                if koff > 0:
                    P.add("dve", lambda e, kt=kt: e.memset(kt.ap[DH:DH + 3, 0:koff], 1.0), [], [kt.b])
                ntile_new = (Tn + 127) // 128
                if Tn >= 128:
                    DMA("sync", vxh.ap[:, npast_t:npast_t + ntile_new, :], S.VX.rearrange("(a p) h d -> p a h d", p=128)[:, :, h, :], [S.B_qkv], [vxh.b])
                else:
                    DMA("sync", vxh.ap[0:Tn, npast_t, :], S.VX[:, h, :], [S.B_qkv], [vxh.b])
                if koff > 0:
                    ck_ = ckt[h % 2]
                    DMA("sync", ck_.ap, ck[l, h].rearrange("(a p) d -> p a d", p=128), [], [ck_.b])
                    for a0 in range(0, npast_t, 4):
                        pt = PSB[cnt["ps"] % 4]; cnt["ps"] += 1
                        for a in range(a0, min(a0 + 4, npast_t)):
                            TR(pt.ap[0:64, (a - a0) * 128:(a - a0 + 1) * 128], ck_.ap[:, a, :], identf, [ck_], [pt])
                        na = min(4, npast_t - a0)
                        CP("dve", kt.ap[0:DH, a0 * 128:(a0 + na) * 128], pt.ap[0:64, 0:na * 128], [pt], [kt])
                    P.add("pool", lambda e, vxh=vxh: e.memset(vxh.ap[:, 0:npast_t, :].rearrange("p a d -> p (a d)"), 1.0), [], [vxh.b])
                    DMA("pool", vxh.ap[:, 0:npast_t, 0:DH], cv[l, h].rearrange("(a p) d -> p a d", p=128), [vxh.b], [vxh.b])
                for G in range(S.ngr):
                    q0 = G * QG
                    last_kt = (koff + q0 + QG - 1) // 128
                    first_diag = (koff + q0) // 128
                    bg = biasg[cnt["bg"] % 2]; cnt["bg"] += 1
                    TS_("dve", bg.ap[:, 0:last_kt + 1], F_T[s].ap[:, 0:last_kt + 1, h], -1.0, CBC[s].ap[:, h, G:G + 1], ALU.mult, ALU.add, [F_T[s], CBC[s]], [bg])
                    po = PSB[6 + (cnt["po"] % 2)]; cnt["po"] += 1
                    pov = po.ap[:, 0:nsg * 128].rearrange("p (i d) -> p i d", d=128)
                    blocks = []
                    for k_ in range(last_kt + 1):
                        kp = min(128, TK - k_ * 128)
                        j = max(0, k_ - first_diag)
                        blocks.append((k_, kp, j))

                    def score(bi):
                        k_, kp, j = blocks[bi]
                        psc = PSB[cnt["ps"] % 4]; cnt["ps"] += 1
                        c0 = j * QB
                        diag = k_ >= first_diag
                        MM(psc.ap[0:kp, c0:QG], kt.ap[:, k_ * 128:k_ * 128 + kp], qt.ap[:, q0 + c0:q0 + QG], True, not diag, [kt, qt], [psc])
                        if diag:
                            MM(psc.ap[0:kp, c0:c0 + QB], maskT.ap[0:QB, 0:kp], identb.ap[0:QB, 0:QB], False, True, [maskT, identb], [psc])
                        pT = pTs[cnt["pt"] % NPT]; cnt["pt"] += 1
                        ACT(pT.ap[0:kp, c0:QG], psc.ap[0:kp, c0:QG], AF.Exp, [psc, bg], [pT], bias=bg.ap[0:kp, k_:k_ + 1], scale=1.0)
                        return pT

                    def pv(bi, pT):
                        k_, kp, j = blocks[bi]
                        for i in range(j, nsg):
                            first = (bi == 0 and i == j)
                            last = (bi == len(blocks) - 1 and i == nsg - 1)
                            MM(pov[0:QB, i, 0:DH + 1], pT.ap[0:kp, i * QB:(i + 1) * QB], vxh.ap[0:kp, k_, 0:DH + 1], first, last, [pT, vxh], [po])

                    pend = []
                    nb = len(blocks)
                    for bi in range(nb + LA):
                        if bi < nb:
                            pend.append((bi, score(bi)))
                        if bi >= LA:
                            b2, pT2 = pend.pop(0)
                            pv(b2, pT2)
                    rs = rsum[G % 2]
                    P.add("dve", lambda e, rs=rs, pov=pov: e.reciprocal(out=rs.ap[0:QB, :], in_=pov[0:QB, :, DH]), [po.b], [rs.b])
                    TT_("dve", ATT[s].ap[0:QB, G * nsg:(G + 1) * nsg, h * DH:(h + 1) * DH], pov[0:QB, :, 0:DH], rs.ap[0:QB, :].unsqueeze(2).to_broadcast([QB, nsg, DH]), ALU.mult, [po, rs], [ATT[s]])

        def s5_prep(l):
            tb = {}
            Ainv = A.f32([16, 2, 128]); ApT = A.f32([16, 2, 128]); T2 = A.bf([4, 2, 128]); CT = A.bf([4, 2, 128])
            Dd = A.bf([4, 128]); a1 = A.f32([16, 2]); WGLU = A.bf([4, DSSM]); T2f = A.bf([4, 4, 2, 128]); CTn = A.bf([4, 128])
            tb.update(Ainv=Ainv, ApT=ApT, T2=T2, CT=CT, Dd=Dd, a1=a1, WGLU=WGLU, T2f=T2f, CTn=CTn)
            mark = A.off
            tb["mark"] = mark
            are_n = A.f32([128], parts=16); aim_n = A.f32([128], parts=16); ldt = A.f32([2], parts=16)
            DMA("sync", are_n.ap, ssm_a_re[l].rearrange("(a b) p -> a (b p)", b=2), [], [are_n.b])
            DMA("sync", aim_n.ap, ssm_a_im[l].rearrange("(a b) p -> a (b p)", b=2), [], [aim_n.b])
            DMA("sync", ldt.ap, ssm_log_dt[l].rearrange("(a b) -> a b", b=2), [], [ldt.b])
            ACT(ldt.ap, ldt.ap, AF.Exp, [ldt], [ldt])
            al_n = A.f32([128], parts=16); th_n = A.f32([128], parts=16)
            dtb = ldt.ap.unsqueeze(2).to_broadcast([16, 2, 64])
            TT_("dve", al_n.ap.rearrange("p (a b) -> p a b", a=2), are_n.ap.rearrange("p (a b) -> p a b", a=2), dtb, ALU.mult, [are_n, ldt], [al_n])
            TT_("dve", th_n.ap.rearrange("p (a b) -> p a b", a=2), aim_n.ap.rearrange("p (a b) -> p a b", a=2), dtb, ALU.mult, [aim_n, ldt], [th_n])
            sm = A.f32([4, 16])
            pt = PS()
            for i, src in enumerate([al_n, th_n, are_n, aim_n]):
                TR(pt.ap[:, i * 16:(i + 1) * 16], src.ap, identf, [src], [pt])
            CP("dve", sm.ap.rearrange("p a b -> p (a b)"), pt.ap[:, 0:64], [pt], [sm])

            def sincos(ang, shape_elems, cosv, sinv, tmpk):
                TS_("dve", tmpk.ap, ang.ap, 1.0 / TWO_PI, MAGIC, ALU.mult, ALU.add, [ang], [tmpk])
                TS_("dve", tmpk.ap, tmpk.ap, -MAGIC, None, ALU.add, None, [tmpk], [tmpk])
                STT(ang.ap, tmpk.ap, -CW1, ang.ap, ALU.mult, ALU.add, [tmpk, ang], [ang])
                STT(ang.ap, tmpk.ap, -CW2, ang.ap, ALU.mult, ALU.add, [tmpk, ang], [ang])
                TS_("dve", ang.ap, ang.ap, PI_LO, -PI_LO, ALU.min, ALU.max, [ang], [ang])
                ACT(sinv.ap, ang.ap, AF.Sin, [ang], [sinv])
                ACT(tmpk.ap, ang.ap, AF.Abs, [ang], [tmpk])
                ACT(cosv.ap, tmpk.ap, AF.Sin, [tmpk, halfpi], [cosv], bias=halfpi.ap[0:cosv.ap.shape[0]], scale=-1.0)

            angS = A.f32([16, 128]); kS = A.f32([16, 128]); cS = A.f32([16, 128]); sS = A.f32([16, 128]); eS = A.f32([16, 128])
            trb = trow.ap.unsqueeze(1).to_broadcast([128, 16, 128])
            TT_("dve", angS.ap, sm.ap[:, 1, :].unsqueeze(2).to_broadcast([128, 16, 128]), trb, ALU.mult, [sm, trow], [angS])
            TT_("dve", eS.ap, sm.ap[:, 0, :].unsqueeze(2).to_broadcast([128, 16, 128]), trb, ALU.mult, [sm, trow], [eS])
            ACT(eS.ap, eS.ap, AF.Exp, [eS], [eS])
            sincos(angS, 2048, cS, sS, kS)
            TT_("dve", ApT.ap[:, :, 0, :], eS.ap, cS.ap, ALU.mult, [eS, cS], [ApT])
            TT_("dve", ApT.ap[:, :, 1, :], eS.ap, sS.ap, ALU.mult, [eS, sS], [ApT])
            CP("dve", a1.ap, ApT.ap[:, :, :, 1], [ApT], [a1])
            if stop_after == "Ca":
                return tb
            cf = A.f32([8, 16])
            are = sm.ap[:, 2, :]; aim = sm.ap[:, 3, :]
            TS_("dve", cf.ap[:, 0, :], a1.ap[:, :, 0], -1.0, None, ALU.add, None, [a1], [cf])
            TT_("dve", cf.ap[:, 1, :], are, are, ALU.mult, [sm], [cf])
            TT_("dve", cf.ap[:, 2, :], aim, aim, ALU.mult, [sm], [cf])
            TT_("dve", cf.ap[:, 1, :], cf.ap[:, 1, :], cf.ap[:, 2, :], ALU.add, [cf], [cf])
            P.add("dve", lambda e: e.reciprocal(out=cf.ap[:, 1, :], in_=cf.ap[:, 1, :]), [cf.b], [cf.b])
            TT_("dve", cf.ap[:, 2, :], cf.ap[:, 0, :], are, ALU.mult, [cf, sm], [cf])
            TT_("dve", cf.ap[:, 3, :], a1.ap[:, :, 1], aim, ALU.mult, [a1, sm], [cf])
            TT_("dve", cf.ap[:, 2, :], cf.ap[:, 2, :], cf.ap[:, 3, :], ALU.add, [cf], [cf])
            TT_("dve", cf.ap[:, 4, :], cf.ap[:, 2, :], cf.ap[:, 1, :], ALU.mult, [cf], [cf])
            TT_("dve", cf.ap[:, 2, :], a1.ap[:, :, 1], are, ALU.mult, [a1, sm], [cf])
            TT_("dve", cf.ap[:, 3, :], cf.ap[:, 0, :], aim, ALU.mult, [cf, sm], [cf])
            TT_("dve", cf.ap[:, 2, :], cf.ap[:, 2, :], cf.ap[:, 3, :], ALU.subtract, [cf], [cf])
            TT_("dve", cf.ap[:, 5, :], cf.ap[:, 2, :], cf.ap[:, 1, :], ALU.mult, [cf], [cf])
            if stop_after == "Cb":
                return tb
            bre = A.f32([16, 16]); bim = A.f32([16, 16]); Z = A.f32([16, 2, 2, 16]); tq = A.f32([16, 16])
            for g2 in range(2):
                DMA("sync", bre.ap[g2 * 64:(g2 + 1) * 64], ssm_b_re[l].rearrange("(a b) p m -> b p a m", b=2)[g2], [], [bre.b])
                DMA("sync", bim.ap[g2 * 64:(g2 + 1) * 64], ssm_b_im[l].rearrange("(a b) p m -> b p a m", b=2)[g2], [], [bim.b])
            MSET("pool", Z, 0.0)
            cre_b = cf.ap[:, 4, :].unsqueeze(2).to_broadcast([128, 16, 16])
            cim_b = cf.ap[:, 5, :].unsqueeze(2).to_broadcast([128, 16, 16])
            for g2 in range(2):
                ps_ = slice(g2 * 64, (g2 + 1) * 64)
                TT_("dve", tq.ap[ps_], bim.ap[ps_], cim_b[ps_], ALU.mult, [bim, cf], [tq])
                TT_("dve", Z.ap[ps_, :, 0, g2, :], bre.ap[ps_], cre_b[ps_], ALU.mult, [bre, cf], [Z])
                TT_("dve", Z.ap[ps_, :, 0, g2, :], Z.ap[ps_, :, 0, g2, :], tq.ap[ps_], ALU.subtract, [Z, tq], [Z])
                TT_("dve", tq.ap[ps_], bre.ap[ps_], cim_b[ps_], ALU.mult, [bre, cf], [tq])
                TT_("dve", Z.ap[ps_, :, 1, g2, :], bim.ap[ps_], cre_b[ps_], ALU.mult, [bim, cf], [Z])
                TT_("dve", Z.ap[ps_, :, 1, g2, :], Z.ap[ps_, :, 1, g2, :], tq.ap[ps_], ALU.add, [Z, tq], [Z])
            for ch in range(4):
                for ri in range(2):
                    pt = PS()
                    zin = A.f32([128])
                    CP("pool", zin.ap.rearrange("p (a b c) -> p a b c", a=4, b=2), Z.ap[:, ch * 4:(ch + 1) * 4, ri, :, :], [Z], [zin])
                    TR(pt.ap[:, 0:128], zin.ap, identf, [zin], [pt])
                    CP("dve", T2.ap[:, ch, ri, :], pt.ap[:, 0:128], [pt], [T2])
            MSET("pool", T2f, 0.0)
            for i4 in range(4):
                CP("pool", T2f.ap[i4 * 32:(i4 + 1) * 32, :, i4, :, :], T2.ap[i4 * 32:(i4 + 1) * 32, :, :, :], [T2], [T2f])
            if stop_after == "Cc":
                return tb
            Zc = A.f32([4, 2, 128])
            MSET("pool", Zc, 0.0)
            for ri, csrc in enumerate([ssm_c_re, ssm_c_im]):
                cview = csrc[l].rearrange("(ch i b) m p -> i b m ch p", i=4, b=2)
                for i4 in range(4):
                    for g2 in range(2):
                        r0 = i4 * 32 + g2 * 16
                        DMA("sync", Zc.ap[r0:r0 + 16, :, ri, g2 * 64:(g2 + 1) * 64], cview[i4, g2], [], [Zc.b])
            for ch in range(4):
                for ri in range(2):
                    pt = PS()
                    TR(pt.ap[:, 0:128], Zc.ap[:, ch, ri, :], identf, [Zc], [pt])
                    if ri == 0:
                        CP("dve", CT.ap[:, ch, ri, :], pt.ap[:, 0:128], [pt], [CT])
                        TS_("dve", CTn.ap[:, ch, :], pt.ap[:, 0:128], -1.0, None, ALU.mult, None, [pt], [CTn])
                    else:
                        TS_("dve", CT.ap[:, ch, ri, :], pt.ap[:, 0:128], -1.0, None, ALU.mult, None, [pt], [CT])
            dcol = A.f32([4]); dnat = A.f32([128], parts=4)
            DMA("sync", dnat.ap, ssm_d[l].rearrange("(c p) -> c p", p=128), [], [dnat.b])
            pt = PS()
            TR(pt.ap[:, 0:4], dnat.ap, identf, [dnat], [pt])
            CP("dve", dcol.ap, pt.ap[:, 0:4], [pt], [dcol])
            for ch in range(4):
                TS_("dve", Dd.ap[:, ch, :], identf.ap, dcol.ap[:, ch:ch + 1], None, ALU.mult, None, [identf, dcol], [Dd])
            if stop_after == "Cd":
                return tb
            arb = A.f32([2048]); aib = A.f32([2048]); ldb = A.f32([32])
            DMA("sync", arb.ap, ssm_a_re[l].rearrange("g p -> (g p)").partition_broadcast(128), [], [arb.b])
            DMA("sync", aib.ap, ssm_a_im[l].rearrange("g p -> (g p)").partition_broadcast(128), [], [aib.b])
            DMA("sync", ldb.ap, ssm_log_dt[l].partition_broadcast(128), [], [ldb.b])
            ACT(ldb.ap, ldb.ap, AF.Exp, [ldb], [ldb])
            dtbb = ldb.ap.unsqueeze(2).to_broadcast([128, 32, 64])
            TT_("dve", arb.ap.rearrange("p (g q) -> p g q", q=64), arb.ap.rearrange("p (g q) -> p g q", q=64), dtbb, ALU.mult, [arb, ldb], [arb])
            TT_("dve", aib.ap.rearrange("p (g q) -> p g q", q=64), aib.ap.rearrange("p (g q) -> p g q", q=64), dtbb, ALU.mult, [aib, ldb], [aib])
            angT = angS; kT_ = kS; cT_ = cS; sT_ = sS; eT = eS
            fl = lambda t: t.ap.rearrange("p a b -> p (a b)")
            TS_("dve", fl(angT), aib.ap, tcol.ap, None, ALU.mult, None, [aib, tcol], [angT])
            ACT(fl(eT), arb.ap, AF.Exp, [arb, ntcol], [eT], scale=ntcol.ap)
            sincos(angT, 2048, cT_, sT_, kT_)
            TT_("dve", Ainv.ap[:, :, 0, :], eT.ap, cT_.ap, ALU.mult, [eT, cT_], [Ainv])
            STT(Ainv.ap[:, :, 1, :], eT.ap, -1.0, sT_.ap, ALU.mult, ALU.mult, [eT, sT_], [Ainv])
            DMA("pool", WGLU.ap, w_glu[l].rearrange("(kc p) n -> p kc n", p=128), [], [WGLU.b])
            tb["mark"] = mark
            return tb

        def phase_C(S, l, tb, hc_init_from=None):
            s = S.s; QB = S.QB; nblk = S.T // QB
            Ainv = tb["Ainv"]; ApT = tb["ApT"]; T2f = tb["T2f"]; CT = tb["CT"]; Dd = tb["Dd"]; a1 = tb["a1"]; WGLU = tb["WGLU"]; CTn = tb["CTn"]
            uTb = [A.bf([4, QB]), A.bf([4, QB])]
            bus = [A.f32([16, 2, 128]), A.f32([16, 2, 128])]
            cus = A.f32([16, 2, 128])
            t1 = A.bf([16, 128]); t2 = A.bf([16, 128]); t3 = A.bf([16, 128]); t4 = A.bf([16, 128])
            w1 = A.bf([16, 128]); w2 = A.bf([16, 128]); w3 = A.bf([16, 128]); w4 = A.bf([16, 128])
            hl = A.f32([16, 2]); hc = A.f32([16, 2]); hq = A.f32([16, 2])
            y2 = A.f32([DSSM]); yin = A.f32([DSSM]); zt = A.bf([DSSM]); zT = A.bf([4, 128]); sg = A.f32([4, 128]); soT = A.bf([4, 128])
            hn = A.f32([128], parts=16)

            def carry_from_hl():
                TT_("dve", hq.ap[:, :, 0], a1.ap[:, :, 0], hl.ap[:, :, 0], ALU.mult, [a1, hl], [hq])
                TT_("dve", hq.ap[:, :, 1], a1.ap[:, :, 1], hl.ap[:, :, 1], ALU.mult, [a1, hl], [hq])
                TT_("dve", hc.ap[:, :, 0], hq.ap[:, :, 0], hq.ap[:, :, 1], ALU.subtract, [hq], [hc])
                TT_("dve", hq.ap[:, :, 0], a1.ap[:, :, 0], hl.ap[:, :, 1], ALU.mult, [a1, hl], [hq])
                TT_("dve", hq.ap[:, :, 1], a1.ap[:, :, 1], hl.ap[:, :, 0], ALU.mult, [a1, hl], [hq])
                TT_("dve", hc.ap[:, :, 1], hq.ap[:, :, 0], hq.ap[:, :, 1], ALU.add, [hq], [hc])

            if S.koff > 0:
                for ri, src in enumerate([sre, sim]):
                    DMA("sync", hn.ap, src[l].rearrange("(a b) p -> a (b p)", b=2), [], [hn.b])
                    pt = PSB[5]
                    TR(pt.ap[:, 0:16], hn.ap, identf, [hn], [pt])
                    CP("dve", hl.ap[:, :, ri], pt.ap[:, 0:16], [pt], [hl])
                carry_from_hl()
            else:
                MSET("dve", hc, 0.0)

            def stage1(blk):
                t0 = blk * QB
                ut = uTb[blk % 2]; bu = bus[blk % 2]
                DMA("sync", ut.ap, S.uT.rearrange("(c p) t -> p c t", p=128)[:, :, t0:t0 + QB], [S.B_uT], [ut.b])
                for ch in range(4):
                    pb = [PSB[(ch % 2) * 2], PSB[(ch % 2) * 2 + 1]]
                    for hf_ in range(2):
                        MM(pb[hf_].ap[0:QB, :], ut.ap[:, ch, :], T2f.ap[:, ch, hf_ * 2:hf_ * 2 + 2, :, :].rearrange("p i r q -> p (i r q)"), True, True, [ut, T2f], [pb[hf_]])
                        CP("act", bu.ap[0:QB, ch * 4 + hf_ * 2:ch * 4 + hf_ * 2 + 2].rearrange("p i r q -> p (i r q)"), pb[hf_].ap[0:QB, :], [pb[hf_]], [bu])

            def stage2a(blk):
                t0 = blk * QB
                ut = uTb[blk % 2]; bu = bus[blk % 2]
                TT_("dve", w1.ap[0:QB], bu.ap[0:QB, :, 0, :], Ainv.ap[0:QB, :, 0, :], ALU.mult, [bu, Ainv], [w1])
                TT_("pool", w4.ap[0:QB], bu.ap[0:QB, :, 1, :], Ainv.ap[0:QB, :, 0, :], ALU.mult, [bu, Ainv], [w4])
                TT_("dve", w2.ap[0:QB], bu.ap[0:QB, :, 1, :], Ainv.ap[0:QB, :, 1, :], ALU.mult, [bu, Ainv], [w2])
                TT_("dve", w3.ap[0:QB], bu.ap[0:QB, :, 0, :], Ainv.ap[0:QB, :, 1, :], ALU.mult, [bu, Ainv], [w3])

            def stageC(blk):
                t0 = blk * QB
                ut = uTb[blk % 2]; bu = bus[blk % 2]
                for ch in range(4):
                    pc = [PSB[4], PSB[5]]
                    for i4 in range(4):
                        pr = ch * 4 + i4
                        tgt = pc[i4 // 2]
                        cre_ = ((i4 % 2) * 2 + 0) * 128; cim_ = ((i4 % 2) * 2 + 1) * 128
                        MM(tgt.ap[:, cre_:cre_ + QB], w1.ap[0:QB, pr, :], tri.ap[0:QB, 0:QB], True, False, [w1, tri], [tgt])
                        MM(tgt.ap[:, cre_:cre_ + QB], w2.ap[0:QB, pr, :], ntri.ap[0:QB, 0:QB], False, True, [w2, ntri], [tgt])
                        MM(tgt.ap[:, cim_:cim_ + QB], w3.ap[0:QB, pr, :], tri.ap[0:QB, 0:QB], True, False, [w3, tri], [tgt])
                        MM(tgt.ap[:, cim_:cim_ + QB], w4.ap[0:QB, pr, :], tri.ap[0:QB, 0:QB], False, True, [w4, tri], [tgt])
                    for hf_ in range(2):
                        CP("act", cus.ap[:, ch * 4 + hf_ * 2:ch * 4 + hf_ * 2 + 2, :, 0:QB], pc[hf_].ap.rearrange("p (i r q) -> p i r q", i=2, r=2)[:, :, :, 0:QB], [pc[hf_]], [cus])

            def stageH(blk):
                t0 = blk * QB
                ut = uTb[blk % 2]; bu = bus[blk % 2]
                TT_("dve", cus.ap[:, :, 0, 0:QB], cus.ap[:, :, 0, 0:QB], hc.ap[:, :, 0].unsqueeze(2).to_broadcast([128, 16, QB]), ALU.add, [cus, hc], [cus])
                TT_("pool", cus.ap[:, :, 1, 0:QB], cus.ap[:, :, 1, 0:QB], hc.ap[:, :, 1].unsqueeze(2).to_broadcast([128, 16, QB]), ALU.add, [cus, hc], [cus])
                apr = ApT.ap[:, :, 0, 0:QB]; api = ApT.ap[:, :, 1, 0:QB]
                cre = cus.ap[:, :, 0, 0:QB]; cim = cus.ap[:, :, 1, 0:QB]
                L = QB - 1
                TT_("dve", hq.ap[:, :, 0], ApT.ap[:, :, 0, L], cus.ap[:, :, 0, L], ALU.mult, [ApT, cus], [hq])
                TT_("dve", hq.ap[:, :, 1], ApT.ap[:, :, 1, L], cus.ap[:, :, 1, L], ALU.mult, [ApT, cus], [hq])
                TT_("dve", hl.ap[:, :, 0], hq.ap[:, :, 0], hq.ap[:, :, 1], ALU.subtract, [hq], [hl])
                TT_("dve", hq.ap[:, :, 0], ApT.ap[:, :, 0, L], cus.ap[:, :, 1, L], ALU.mult, [ApT, cus], [hq])
                TT_("dve", hq.ap[:, :, 1], ApT.ap[:, :, 1, L], cus.ap[:, :, 0, L], ALU.mult, [ApT, cus], [hq])
                TT_("dve", hl.ap[:, :, 1], hq.ap[:, :, 0], hq.ap[:, :, 1], ALU.add, [hq], [hl])
                carry_from_hl()
                TT_("dve", t1.ap[:, :, 0:QB], apr, cre, ALU.mult, [ApT, cus], [t1])
                TT_("pool", t4.ap[:, :, 0:QB], api, cre, ALU.mult, [ApT, cus], [t4])
                TT_("dve", t2.ap[:, :, 0:QB], api, cim, ALU.mult, [ApT, cus], [t2])
                TT_("dve", t3.ap[:, :, 0:QB], apr, cim, ALU.mult, [ApT, cus], [t3])
                py = PSB[6]
                for ch in range(4):
                    MM(py.ap[0:QB, ch * 128:(ch + 1) * 128], ut.ap[:, ch, :], Dd.ap[:, ch, :], ch == 0, False, [ut, Dd], [py])
                for ch in range(4):
                    for i4 in range(4):
                        pr = ch * 4 + i4
                        o_ = py.ap[0:QB, pr * 32:(pr + 1) * 32]
                        cs_ = slice(i4 * 32, (i4 + 1) * 32)
                        MM(o_, t1.ap[:, pr, 0:QB], CT.ap[:, ch, 0, cs_], False, False, [t1, CT], [py])
                        MM(o_, t2.ap[:, pr, 0:QB], CTn.ap[:, ch, cs_], False, False, [t2, CTn], [py])
                        MM(o_, t3.ap[:, pr, 0:QB], CT.ap[:, ch, 1, cs_], False, False, [t3, CT], [py])
                        MM(o_, t4.ap[:, pr, 0:QB], CT.ap[:, ch, 1, cs_], False, pr == 15, [t4, CT], [py])

            def stage2c(blk):
                t0 = blk * QB
                ut = uTb[blk % 2]; bu = bus[blk % 2]
                py = PSB[6]
                ACT(y2.ap[0:QB], py.ap[0:QB, :], AF.Square, [py], [y2])
                TS_("dve", y2.ap[0:QB], y2.ap[0:QB], 0.044715, 1.0, ALU.mult, ALU.add, [y2], [y2])
                TT_("dve", yin.ap[0:QB], y2.ap[0:QB], py.ap[0:QB, :], ALU.mult, [y2, py], [yin])
                ACT(yin.ap[0:QB], yin.ap[0:QB], AF.Sigmoid, [yin], [yin], scale=1.5957691216057308)
                TT_("dve", zt.ap[0:QB], yin.ap[0:QB], py.ap[0:QB, :], ALU.mult, [yin, py], [zt])
                pz = PSB[7]
                pzb = pz.ap.bitcast(BF16)
                for c in range(4):
                    TR(pzb[:, c * 128:c * 128 + QB], zt.ap[0:QB, c * 128:(c + 1) * 128], identb, [zt], [pz])
                CP("act", zT.ap[:, :, 0:QB], pzb[:, 0:512].rearrange("p (a b) -> p a b", b=128)[:, :, 0:QB], [pz], [zT])
                pg = PSB[7]
                for m in range(4):
                    for kc in range(4):
                        MM(pg.ap[:, m * 128:m * 128 + QB], WGLU.ap[:, kc, m * 128:(m + 1) * 128], zT.ap[:, kc, 0:QB], kc == 0, kc == 3, [WGLU, zT], [pg])
                ACT(sg.ap[:, :, 0:QB], pg.ap.rearrange("p (a b) -> p a b", b=128)[:, :, 0:QB], AF.Sigmoid, [pg], [sg])
                TT_("dve", soT.ap[:, :, 0:QB], zT.ap[:, :, 0:QB], sg.ap[:, :, 0:QB], ALU.mult, [zT, sg], [soT])
                DMA("sync", S.soT.rearrange("(c p) t -> p c t", p=128)[:, :, t0:t0 + QB], soT.ap[:, :, 0:QB], [soT.b], [S.B_soT])

            stage1(0)
            stage2a(0)
            stageC(0)
            for blk in range(nblk):
                if blk + 1 < nblk:
                    stage1(blk + 1)
                stageH(blk)
                if blk + 1 < nblk:
                    stage2a(blk + 1)
                    stageC(blk + 1)
                stage2c(blk)
            for ri, dst in enumerate([S.nr_o, S.ni_o]):
                pt = PSB[4 + ri]
                hlc = A.f32([16])
                CP("dve", hlc.ap, hl.ap[:, :, ri], [hl], [hlc])
                TR(pt.ap[0:16, 0:128], hlc.ap, identf, [hlc], [pt])
                ho = A.f32([128], parts=16)
                CP("dve", ho.ap, pt.ap[0:16, 0:128], [pt], [ho])
                DMA("sync", dst[l].rearrange("(a b) p -> a (b p)", b=2), ho.ap, [ho.b], [])

        def post_norm_residual(S, l, oT, osq, xt, xn, gslot, rstd, tmp2):
            TT = S.TT
            pss = PS()
            for c in range(8):
                MM(pss.ap[:, 0:TT], onesM.ap, osq.ap[:, c, :], c == 0, c == 7, [onesM, osq], [pss])
            ACT(rstd.ap, pss.ap[:, 0:TT], AF.Sqrt, [pss, epsc], [rstd], bias=epsc.ap, scale=1.0)
            P.add("dve", lambda e: e.reciprocal(out=rstd.ap, in_=rstd.ap), [rstd.b], [rstd.b])
            for c in range(8):
                tm = tmp2[c % 2]
                STT(tm.ap, oT.ap[:, c, :], PAR.ap[:, l, S.s, gslot, c:c + 1], rstd.ap, ALU.mult, ALU.mult, [oT, PAR, rstd], [tm])
                TT_("dve", xn.ap[:, c, :], xt.ap[:, c, :], tm.ap, ALU.add, [xt, tm], [xn])

        def phase_DE(S, l, WOUT):
            s = S.s; TT = S.TT; QB = S.QB; nsub = TT // QB
            xt = A.f32([8, TT])
            xn = A.f32([8, TT]); oT = A.f32([8, TT]); sq = A.bf([8, TT]); osq = sq
            rstd = A.f32([TT]); tmp2 = [A.f32([TT]), A.f32([TT])]
            mixT = A.bf([8, TT]); hf = A.bf([8, TT]); hT = A.bf([NJ, TT]); sgt = A.f32([TT])
            WGs = [A.bf([8, 256]), A.bf([8, 256])]; WUs = [A.bf([8, 256]), A.bf([8, 256])]
            WDq = [A.bf([NJ, 256]), A.bf([NJ, 256])]
            wgi = 0; wdi = 0
            for i in range(S.nt):
                t0 = i * TT
                DMA("sync", xt.ap, S.xT.rearrange("(c p) t -> p c t", p=128)[:, :, t0:t0 + TT], [S.B_xT[i]], [xt.b])
                DMA("sync", mixT.ap[:, 4:8, :], S.soT.rearrange("(c p) t -> p c t", p=128)[:, :, t0:t0 + TT], [S.B_soT], [mixT.b])
                for j in range(nsub):
                    g = (t0 // QB) + j
                    pa = PS(); pab = pa.ap.bitcast(BF16)
                    for c in range(4):
                        TR(pab[:, c * 128:c * 128 + QB], ATT[s].ap[0:QB, g, c * 128:(c + 1) * 128], identb, [ATT[s]], [pa])
                    CP("act", mixT.ap[:, 0:4, j * QB:(j + 1) * QB], pab[:, 0:512].rearrange("p (a b) -> p a b", b=128)[:, :, 0:QB], [pa], [mixT])
                for m in range(8):
                    po = PS()
                    for kc in range(8):
                        MM(po.ap[:, 0:TT], WOUT.ap[:, kc, m * 128:(m + 1) * 128], mixT.ap[:, kc, :], kc == 0, kc == 7, [WOUT, mixT], [po])
                    CP("dve", oT.ap[:, m, :], po.ap[:, 0:TT], [po], [oT])
                    ACT(osq.ap[:, m, :], po.ap[:, 0:TT], AF.Square, [po], [osq])
                post_norm_residual(S, l, oT, osq, xt, xn, 2, rstd, tmp2)
                rmsnorm_mod(S, l, xn, 3, hf, tmp2, sq, rstd)
                for n0 in range(0, DFF, 256):
                    wg = WGs[wgi % 2]; wu = WUs[wgi % 2]; wgi += 1
                    DMA("sync", wg.ap, WG2[l][:, :, n0:n0 + 256], [B_WFF[l]], [wg.b])
                    DMA("sync", wu.ap, WU2[l][:, :, n0:n0 + 256], [B_WFF[l]], [wu.b])
                    for jj in range(2):
                        j = n0 // 128 + jj
                        pg = PS(); pu = PS()
                        for kc in range(8):
                            MM(pg.ap[:, 0:TT], wg.ap[:, kc, jj * 128:(jj + 1) * 128], hf.ap[:, kc, :], kc == 0, kc == 7, [wg, hf], [pg])
                        for kc in range(8):
                            MM(pu.ap[:, 0:TT], wu.ap[:, kc, jj * 128:(jj + 1) * 128], hf.ap[:, kc, :], kc == 0, kc == 7, [wu, hf], [pu])
                        ACT(sgt.ap, pg.ap[:, 0:TT], AF.Silu, [pg], [sgt])
                        TT_("dve", hT.ap[:, j, :], sgt.ap, pu.ap[:, 0:TT], ALU.mult, [sgt, pu], [hT])
                for q4 in range(4):
                    WD = WDq[wdi % 2]; wdi += 1
                    DMA("sync", WD.ap, WD2[l][:, :, q4 * 256:(q4 + 1) * 256], [B_WFF[l]], [WD.b])
                    for mm in range(2):
                        m = q4 * 2 + mm
                        pf = PS()
                        for j in range(NJ):
                            MM(pf.ap[:, 0:TT], WD.ap[:, j, mm * 128:(mm + 1) * 128], hT.ap[:, j, :], j == 0, j == NJ - 1, [WD, hT], [pf])
                        CP("dve", oT.ap[:, m, :], pf.ap[:, 0:TT], [pf], [oT])
                        ACT(osq.ap[:, m, :], pf.ap[:, 0:TT], AF.Square, [pf], [osq])
                post_norm_residual(S, l, oT, osq, xn, xt, 5, rstd, tmp2)
                DMA("sync", S.xT.rearrange("(c p) t -> p c t", p=128)[:, :, t0:t0 + TT], xt.ap, [xt.b], [S.B_xT[i]])

        def phase_final():
            A.reset()
            xin_ = [A.f32([8, 128]), A.f32([8, 128])]
            xo_ = [A.f32([D]), A.f32([D])]
            cnt = 0
            for S in STREAMS:
                qb = S.QB
                for i in range(S.T // qb):
                    xi = xin_[cnt % 2]; xo = xo_[cnt % 2]; cnt += 1
                    ti = (i * qb) // S.TT
                    DMA("sync", xi.ap[:, :, 0:qb], S.xT.rearrange("(c p) t -> p c t", p=128)[:, :, i * qb:(i + 1) * qb], [S.B_xT[ti]], [xi.b])
                    for half in range(2):
                        pt = PS()
                        for c4 in range(4):
                            TR(pt.ap[0:qb, c4 * 128:(c4 + 1) * 128], xi.ap[:, half * 4 + c4, 0:qb], identf, [xi], [pt])
                        CP("act" if half == 0 else "dve", xo.ap[0:qb, half * 512:(half + 1) * 512], pt.ap[0:qb, :], [pt], [xo])
                    DMA("sync", S.y_out[i * qb:(i + 1) * qb, :], xo.ap[0:qb, :], [xo.b], [])

        for l in range(DEPTH if stop_after != "XF" else 0):
            P.barrier(); A.reset()
            WIN = A.bf([8, DIN])
            DMA("pool", WIN.ap, w_in[l].rearrange("(kc p) n -> p kc n", p=128), [], [WIN.b])
            mark = A.off
            for S in STREAMS:
                A.off = mark
                phase_A(S, l, WIN)
                P.barrier()
            if stop_after is not None and stop_after.startswith("A"):
                break
            P.barrier(); A.reset()
            for S in STREAMS:
                A.reset()
                phase_B(S, l)
                P.barrier()
            if stop_after == "B":
                break
            P.barrier(); A.reset()
            tb = s5_prep(l)
            P.barrier()
            if stop_after in ("C0", "Ca", "Cb", "Cc", "Cd"):
                break
            A.off = tb["mark"]
            mark = A.off
            for S in STREAMS:
                A.off = mark
                phase_C(S, l, tb)
                P.barrier()
            if stop_after is not None and stop_after.startswith("C"):
                break
            P.barrier(); A.reset()
            WOUT = A.bf([8, D])
            DMA("pool", WOUT.ap, w_out[l].rearrange("(kc p) n -> p kc n", p=128), [], [WOUT.b])
            mark = A.off
            for S in STREAMS:
                A.off = mark
                phase_DE(S, l, WOUT)
                P.barrier()
        P.barrier()
        phase_final()


    body_fn()

    with nc.allow_low_precision("bf16 matmul operands, fp32 accumulate"):
        P.emit()
    P.close()
    return nc


_CACHE = {}


def kernel(x_prompt, x_sample, c_prompt, c_sample, cache_k, cache_v, cache_logf,
           state_ssm_re, state_ssm_im, w_ada, b_ada, g_pre_mix, g_post_mix, g_pre_ffn,
           g_post_ffn, w_in, b_forget, ssm_a_re, ssm_a_im, ssm_log_dt, ssm_b_re, ssm_b_im,
           ssm_c_re, ssm_c_im, ssm_d, w_glu, w_out, w_gate, w_up, w_down, _stop_after=None):
    f = lambda a: np.ascontiguousarray(np.asarray(a, dtype=np.float32))
    x_prompt = f(x_prompt); x_sample = f(x_sample)
    B, T, _ = x_prompt.shape
    BS, TS, _ = x_sample.shape
    DEPTH = w_in.shape[0]
    PAST = cache_k.shape[3]
    NC = 8
    key = (T, TS, PAST, DEPTH, _stop_after)
    if key not in _CACHE:
        _CACHE[key] = build(T, TS, PAST, DEPTH, _stop_after)
    nc = _CACHE[key]
    shared = dict(w_ada=f(w_ada), b_ada=f(b_ada), g_pre_mix=f(g_pre_mix), g_post_mix=f(g_post_mix),
                  g_pre_ffn=f(g_pre_ffn), g_post_ffn=f(g_post_ffn), w_in=f(w_in), b_forget=f(b_forget),
                  ssm_a_re=f(ssm_a_re), ssm_a_im=f(ssm_a_im), ssm_log_dt=f(ssm_log_dt),
                  ssm_b_re=f(ssm_b_re), ssm_b_im=f(ssm_b_im), ssm_c_re=f(ssm_c_re), ssm_c_im=f(ssm_c_im),
                  ssm_d=f(ssm_d), w_glu=f(w_glu), w_out=f(w_out), w_gate=f(w_gate), w_up=f(w_up), w_down=f(w_down))
    cache_k = f(cache_k); cache_v = f(cache_v); cache_logf = f(cache_logf)
    state_ssm_re = f(state_ssm_re); state_ssm_im = f(state_ssm_im)
    c_prompt = f(c_prompt); c_sample = f(c_sample)
    in_maps = []
    for c in range(NC):
        bp = c % B
        bs = c % BS
        m = dict(shared)
        m["xp"] = x_prompt[bp]; m["xs"] = x_sample[bs]
        m["c2"] = np.ascontiguousarray(np.stack([c_prompt[bp], c_sample[bs]], axis=0))
        m["ck"] = np.ascontiguousarray(cache_k[:, bs]); m["cv"] = np.ascontiguousarray(cache_v[:, bs])
        m["clf"] = np.ascontiguousarray(cache_logf[:, bs])
        m["sre"] = np.ascontiguousarray(state_ssm_re[:, bs]); m["sim"] = np.ascontiguousarray(state_ssm_im[:, bs])
        in_maps.append(m)
    res = run_bass_kernel_spmd(nc, in_maps, core_ids=list(range(NC)))
    R = res.results
    yp = np.stack([R[b]["yp"] for b in range(B)], axis=0)
    ys = np.stack([R[b]["ys"] for b in range(BS)], axis=0)
    st = lambda name, n: np.stack([R[b][name] for b in range(n)], axis=1)
    return (yp, ys, st("nkp", B), st("nvp", B), st("nlp", B), st("nrp", B), st("nip", B),
            st("nks", BS), st("nvs", BS), st("nls", BS), st("nrs", BS), st("nis", BS))
```

```python
import contextlib
import math
import numpy as np
import concourse.bass as bass
import concourse.mybir as mybir
from concourse.bass_utils import run_bass_kernel_spmd

F32 = mybir.dt.float32
BF16 = mybir.dt.bfloat16
AF = mybir.ActivationFunctionType
ALU = mybir.AluOpType

ENGS = ["pe", "act", "dve", "pool", "sync"]
COMPUTE = {"pe", "act", "dve", "pool"}
RING = 8
EPOCH = 20000


class Buf:
    __slots__ = ("name", "last_w", "readers", "psum")

    def __init__(self, name="", psum=False):
        self.name = name
        self.last_w = None
        self.readers = []
        self.psum = psum


class Op:
    __slots__ = ("eng", "fn", "dma", "deps", "idx", "signal", "sem", "val", "ringwait")

    def __init__(self, eng, fn, dma):
        self.eng = eng
        self.fn = fn
        self.dma = dma
        self.deps = set()
        self.signal = dma
        self.sem = None
        self.val = 0
        self.ringwait = None


class Prog:
    def __init__(self, nc):
        self.nc = nc
        self.ops = {e: [] for e in ENGS}
        self.allops = []
        self.stack = contextlib.ExitStack()
        self.pending_barrier = {e: None for e in ENGS}

    def sb(self, name, shape, dtype=F32):
        return self.stack.enter_context(self.nc.sbuf_tensor(name, list(shape), dtype))

    def ps(self, name, shape, dtype=F32):
        return self.stack.enter_context(self.nc.psum_tensor(name, list(shape), dtype))

    def buf(self, name=""):
        return Buf(name)

    def barrier(self):
        snap = set()
        for e in ENGS:
            lst = self.ops[e]
            if not lst:
                continue
            snap.add((e, len(lst) - 1))
            cnt = 0
            for i in range(len(lst) - 1, -1, -1):
                if lst[i].dma:
                    snap.add((e, i))
                    cnt += 1
                    if cnt >= RING:
                        break
                if len(lst) - i > 4 * RING + 64:
                    break
        for e in ENGS:
            self.pending_barrier[e] = snap

    def add(self, eng, fn, reads=(), writes=(), dma=False):
        op = Op(eng, fn, dma)
        lst = self.ops[eng]
        op.idx = len(lst)
        me = (eng, op.idx)
        same_ok = (eng == "pe") and not dma
        pb = self.pending_barrier[eng]
        if pb is not None:
            for d in pb:
                if d != me:
                    op.deps.add(d)
            self.pending_barrier[eng] = None
        for b in reads:
            w = b.last_w
            if w is not None and w != me:
                op.deps.add(w)
            if b.psum:
                for r in b.readers:
                    if r[0] != eng:
                        op.deps.add(r)
        for b in writes:
            w = b.last_w
            if w is not None and w != me:
                wop = self.ops[w[0]][w[1]]
                if not (same_ok and w[0] == eng and not wop.dma):
                    op.deps.add(w)
            for r in b.readers:
                if r == me:
                    continue
                rop = self.ops[r[0]][r[1]]
                if same_ok and r[0] == eng and not rop.dma:
                    continue
                op.deps.add(r)
        for b in reads:
            b.readers.append(me)
        for b in writes:
            b.last_w = me
            b.readers = []
        lst.append(op)
        self.allops.append(op)
        return op

    def emit(self):
        nc = self.nc
        for op in self.allops:
            for (e, i) in op.deps:
                self.ops[e][i].signal = True
        for e in ENGS:
            if e in COMPUTE:
                for op in reversed(self.ops[e]):
                    if not op.dma:
                        op.signal = True
                        break
        nsem = [0]

        def newsem(tag):
            nsem[0] += 1
            return self.stack.enter_context(nc.semaphore(f"s_{tag}_{nsem[0]}"))

        for e in ENGS:
            cur = None
            cnt = 0
            ring = None
            ringcnt = None
            ringprev = None
            nd = 0
            for op in self.ops[e]:
                if op.dma:
                    if ring is None or nd >= (EPOCH // 16) * RING:
                        ring = [newsem(e + "r") for _ in range(RING)]
                        ringcnt = [0] * RING
                        ringprev = [None] * RING
                        nd = 0
                    slot = nd % RING
                    if ringprev[slot] is not None:
                        op.ringwait = ringprev[slot]
                    ringcnt[slot] += 16
                    op.sem = ring[slot]
                    op.val = ringcnt[slot]
                    ringprev[slot] = (op.sem, op.val)
                    nd += 1
                elif op.signal:
                    if cur is None or cnt >= EPOCH:
                        cur = newsem(e)
                        cnt = 0
                    cnt += 1
                    op.sem = cur
                    op.val = cnt
        self.nsem = nsem[0]
        finals = {}
        for e in ENGS:
            for op in self.ops[e]:
                if op.dma:
                    k = id(op.sem)
                    if k not in finals or finals[k][1] < op.val:
                        finals[k] = (op.sem, op.val)
            if e in COMPUTE:
                for op in reversed(self.ops[e]):
                    if not op.dma:
                        finals[id(op.sem)] = (op.sem, op.val)
                        break
        prog = self

        def emit_engine(e, eng):
            known = {}
            for op in prog.ops[e]:
                waits = {}
                for (de, di) in op.deps:
                    dop = prog.ops[de][di]
                    k = id(dop.sem)
                    if k not in waits or waits[k][1] < dop.val:
                        waits[k] = (dop.sem, dop.val)
                if op.ringwait is not None:
                    k = id(op.ringwait[0])
                    if k not in waits or waits[k][1] < op.ringwait[1]:
                        waits[k] = op.ringwait
                for k, (s, v) in waits.items():
                    if known.get(k, 0) >= v:
                        continue
                    known[k] = v
                    eng.wait_ge(s, v)
                ins = op.fn(eng)
                if op.signal:
                    ins.then_inc(op.sem, 16 if op.dma else 1)
            if e == "sync":
                for k, (s, v) in finals.items():
                    if known.get(k, 0) < v:
                        eng.wait_ge(s, v)

        with nc.Block() as block:
            @block.tensor
            def _(eng):
                emit_engine("pe", eng)

            @block.scalar
            def _(eng):
                emit_engine("act", eng)

            @block.vector
            def _(eng):
                emit_engine("dve", eng)

            @block.gpsimd
            def _(eng):
                emit_engine("pool", eng)

            @block.sync
            def _(eng):
                emit_engine("sync", eng)

    def close(self):
        self.stack.close()


D = 1024
NH = 8
DH = 64
DATT = 512
DSSM = 512
DFF = 2816
DIN = 2056
NG = 32
NST = 64
EPS = 1e-6
NJ = DFF // 128
VW = 72
MAGIC = 12582912.0
TWO_PI = 2.0 * math.pi
CW1 = 6.28125
CW2 = float(np.float32(TWO_PI - 6.28125))
PI_LO = 3.1415925


class Tile:
    __slots__ = ("ap", "b")

    def __init__(self, ap, b):
        self.ap = ap
        self.b = b


def build(T, TS, PAST, DEPTH, stop_after=None):
    nc = bass.Bass("TRN2", target_bir_lowering=False)
    P = Prog(nc)

    def din(name, shape):
        return nc.dram_tensor(name, list(shape), F32, kind="ExternalInput").ap()

    def dout(name, shape):
        return nc.dram_tensor(name, list(shape), F32, kind="ExternalOutput").ap()

    def dscr(name, shape, dt=F32):
        return nc.dram_tensor(name, list(shape), dt, kind="Internal").ap()

    xp = din("xp", [T, D]); xs = din("xs", [TS, D]); c2 = din("c2", [2, D])
    ck = din("ck", [DEPTH, NH, PAST, DH]); cv = din("cv", [DEPTH, NH, PAST, DH])
    clf = din("clf", [DEPTH, NH, PAST])
    sre = din("sre", [DEPTH, NG, NST]); sim = din("sim", [DEPTH, NG, NST])
    w_ada = din("w_ada", [DEPTH, D, 6 * D]); b_ada = din("b_ada", [DEPTH, 6 * D])
    g_pre_mix = din("g_pre_mix", [DEPTH, D]); g_post_mix = din("g_post_mix", [DEPTH, D])
    g_pre_ffn = din("g_pre_ffn", [DEPTH, D]); g_post_ffn = din("g_post_ffn", [DEPTH, D])
    w_in = din("w_in", [DEPTH, D, DIN]); b_forget = din("b_forget", [DEPTH, NH])
    ssm_a_re = din("ssm_a_re", [DEPTH, NG, NST]); ssm_a_im = din("ssm_a_im", [DEPTH, NG, NST])
    ssm_log_dt = din("ssm_log_dt", [DEPTH, NG])
    ssm_b_re = din("ssm_b_re", [DEPTH, NG, NST, 16]); ssm_b_im = din("ssm_b_im", [DEPTH, NG, NST, 16])
    ssm_c_re = din("ssm_c_re", [DEPTH, NG, 16, NST]); ssm_c_im = din("ssm_c_im", [DEPTH, NG, 16, NST])
    ssm_d = din("ssm_d", [DEPTH, DSSM]); w_glu = din("w_glu", [DEPTH, DSSM, DSSM])
    w_out = din("w_out", [DEPTH, D, D]); w_gate = din("w_gate", [DEPTH, D, DFF])
    w_up = din("w_up", [DEPTH, D, DFF]); w_down = din("w_down", [DEPTH, DFF, D])

    yp = dout("yp", [T, D]); ys = dout("ys", [TS, D])
    nkp = dout("nkp", [DEPTH, NH, T, DH]); nvp = dout("nvp", [DEPTH, NH, T, DH])
    nlp = dout("nlp", [DEPTH, NH, T])
    nrp = dout("nrp", [DEPTH, NG, NST]); nip = dout("nip", [DEPTH, NG, NST])
    nks = dout("nks", [DEPTH, NH, TS, DH]); nvs = dout("nvs", [DEPTH, NH, TS, DH])
    nls = dout("nls", [DEPTH, NH, TS])
    nrs = dout("nrs", [DEPTH, NG, NST]); nis = dout("nis", [DEPTH, NG, NST])

    WG2 = dscr("WG2", [DEPTH, 128, 8, DFF], BF16)
    WU2 = dscr("WU2", [DEPTH, 128, 8, DFF], BF16)
    WD2 = dscr("WD2", [DEPTH, 128, NJ, D], BF16)
    B_WFF = [P.buf() for _ in range(DEPTH)]

    class Stream:
        pass

    def mk_stream(name, s, Tn, TT, koff, x_in, y_out, nk_o, nv_o, nl_o, nr_o, ni_o):
        S = Stream()
        S.name = name; S.s = s; S.T = Tn; S.TT = TT; S.nt = Tn // TT; S.koff = koff
        S.QB = min(128, Tn)
        S.nqg = Tn // S.QB
        S.QG = min(512, Tn)
        S.ngr = Tn // S.QG
        S.TK = koff + Tn
        S.nkt = (S.TK + 127) // 128
        S.x_in = x_in; S.y_out = y_out
        S.nk_o = nk_o; S.nv_o = nv_o; S.nl_o = nl_o; S.nr_o = nr_o; S.ni_o = ni_o
        S.xT = dscr(name + "_xT", [D, Tn]); S.B_xT = [P.buf() for _ in range(S.nt)]
        S.QT = dscr(name + "_QT", [NH, DH + 3, Tn], BF16)
        S.KT = dscr(name + "_KT", [NH, DH + 3, Tn], BF16)
        S.VX = dscr(name + "_VX", [Tn, NH, VW], BF16)
        S.B_qkv = P.buf(); S.B_kones = P.buf()
        S.uT = dscr(name + "_uT", [DSSM, Tn], BF16); S.B_uT = P.buf()
        S.soT = dscr(name + "_soT", [DSSM, Tn], BF16); S.B_soT = P.buf()
        return S

    SP = mk_stream("p", 0, T, min(512, T), 0, xp, yp, nkp, nvp, nlp, nrp, nip)
    SS = mk_stream("s", 1, TS, TS, PAST, xs, ys, nks, nvs, nls, nrs, nis)
    STREAMS = [SP, SS]

    def ptile(name, shape, dt=F32):
        h = P.sb(name, shape, dt)
        return Tile(h[tuple(slice(None) for _ in shape)], P.buf(name))

    identf = ptile("identf", [128, 128]); identb = ptile("identb", [128, 128], BF16)
    onesM = ptile("onesM", [128, 128], BF16); tri = ptile("tri", [128, 128], BF16)
    maskT = ptile("maskT", [128, 128], BF16); onesbf = ptile("onesbf", [8, 512], BF16)
    ntri = ptile("ntri", [128, 128], BF16)
    ones8 = ptile("ones8", [8, 512]); epsc = ptile("epsc", [128, 1]); one1 = ptile("one1", [128, 1])
    halfpi = ptile("halfpi", [128, 1]); trow = ptile("trow", [128, 128]); tcol = ptile("tcol", [128, 1])
    ntcol = ptile("ntcol", [128, 1]); sel8 = ptile("sel8", [8, NH, 128])
    PAR = ptile("PAR", [128, DEPTH, 2, 6, 8])
    scT = ptile("scT", [128, 8, 2], BF16)
    ATT = [ptile("ATTp", [128, SP.nqg, DATT], BF16), ptile("ATTs", [SS.QB, 1, DATT], BF16)]
    F_T = [ptile("F_Tp", [128, SP.nkt, NH]), ptile("F_Ts", [128, SS.nkt, NH])]
    CBC = [ptile("CBCp", [128, NH, SP.ngr]), ptile("CBCs", [128, NH, SS.ngr])]
    Fcs = [ptile("Fcp", [8, SP.ngr]), ptile("Fcs", [8, SS.ngr])]
    negb = ptile("negb", [8, 1])
    fcarry = ptile("fcarry", [8, 1])

    PSB = [Tile(P.ps(f"psb{i}", [128, 512])[:, :], Buf(f"psb{i}", psum=True)) for i in range(8)]

    ARW = 40960
    AR = P.sb("arena", [128, ARW])

    class Arena:
        def __init__(self):
            self.off = 0

        def reset(self):
            self.off = 0

        def get(self, nwords_f32, dt=F32, shape=None, parts=128):
            n = (nwords_f32 + 7) // 8 * 8
            assert self.off + n <= ARW, ("arena overflow", self.off, n)
            ap = AR[0:parts, self.off:self.off + nwords_f32]
            self.off += n
            if dt != F32:
                ap = ap.bitcast(dt)
            if shape is not None and len(shape) > 1:
                if len(shape) == 2:
                    ap = ap.rearrange("p (a b) -> p a b", a=shape[0])
                elif len(shape) == 3:
                    ap = ap.rearrange("p (a b c) -> p a b c", a=shape[0], b=shape[1])
                elif len(shape) == 4:
                    ap = ap.rearrange("p (a b c d) -> p a b c d", a=shape[0], b=shape[1], c=shape[2])
            return Tile(ap, P.buf())

        def f32(self, shape, parts=128):
            return self.get(int(np.prod(shape)), F32, shape, parts)

        def bf(self, shape, parts=128):
            n = int(np.prod(shape))
            assert n % 2 == 0
            return self.get(n // 2, BF16, shape, parts)

    A = Arena()

    def rb(ts):
        return [t.b for t in ts]

    def MM(out, lhsT, rhs, start, stop, reads, writes, **kw):
        P.add("pe", lambda e: e.matmul(out, lhsT=lhsT, rhs=rhs, start=start, stop=stop, **kw), rb(reads), rb(writes))

    def TR(out, in_, ident, reads, writes):
        P.add("pe", lambda e: e.transpose(out, in_, ident.ap[0:in_.shape[0], 0:in_.shape[0]]), rb(reads) + [ident.b], rb(writes))

    def ACT(out, in_, func, reads, writes, bias=None, scale=None):
        kw = {}
        if bias is not None:
            kw["bias"] = bias
        if scale is not None:
            kw["scale"] = scale
        P.add("act", lambda e: e.activation(out=out, in_=in_, func=func, **kw), rb(reads), rb(writes))

    def TT_(eng, out, in0, in1, op, reads, writes):
        P.add(eng, lambda e: e.tensor_tensor(out=out, in0=in0, in1=in1, op=op), rb(reads), rb(writes))

    def TS_(eng, out, in0, s1, s2, op0, op1, reads, writes):
        if s2 is None:
            P.add(eng, lambda e: e.tensor_scalar(out=out, in0=in0, scalar1=s1, scalar2=None, op0=op0), rb(reads), rb(writes))
        else:
            P.add(eng, lambda e: e.tensor_scalar(out=out, in0=in0, scalar1=s1, scalar2=s2, op0=op0, op1=op1), rb(reads), rb(writes))

    def STT(out, in0, scalar, in1, op0, op1, reads, writes):
        P.add("dve", lambda e: e.scalar_tensor_tensor(out=out, in0=in0, scalar=scalar, in1=in1, op0=op0, op1=op1), rb(reads), rb(writes))

    def CP(eng, out, in_, reads, writes):
        if eng == "act":
            P.add("act", lambda e: e.copy(out=out, in_=in_), rb(reads), rb(writes))
        else:
            P.add(eng, lambda e: e.tensor_copy(out=out, in_=in_), rb(reads), rb(writes))

    def MSET(eng, t, val):
        P.add(eng, lambda e: e.memset(t.ap, val), [], [t.b])

    def DMA(q, out, in_, reads, writes):
        P.add(q, lambda e: e.dma_start(out=out, in_=in_), reads, writes, dma=True)

    psrr = [0]

    def PS():
        t = PSB[psrr[0] % 8]
        psrr[0] += 1
        return t

    def body_fn():
        MSET("pool", identf, 1.0)
        P.add("pool", lambda e: e.affine_select(out=identf.ap, in_=identf.ap, pattern=[[1, 128]], compare_op=ALU.is_equal, fill=0.0, base=0, channel_multiplier=-1), [identf.b], [identf.b])
        MSET("pool", identb, 1.0)
        P.add("pool", lambda e: e.affine_select(out=identb.ap, in_=identb.ap, pattern=[[1, 128]], compare_op=ALU.is_equal, fill=0.0, base=0, channel_multiplier=-1), [identb.b], [identb.b])
        MSET("pool", tri, 1.0)
        P.add("pool", lambda e: e.affine_select(out=tri.ap, in_=tri.ap, pattern=[[1, 128]], compare_op=ALU.is_ge, fill=0.0, base=0, channel_multiplier=-1), [tri.b], [tri.b])
        MSET("pool", maskT, -30000.0)
        P.add("pool", lambda e: e.affine_select(out=maskT.ap, in_=maskT.ap, pattern=[[1, 128]], compare_op=ALU.is_gt, fill=0.0, base=0, channel_multiplier=-1), [maskT.b], [maskT.b])
        MSET("dve", onesbf, 1.0)
        for S_ in STREAMS:
            for r in range(3):
                for t0_ in range(0, S_.T, 512):
                    tw = min(512, S_.T - t0_)
                    DMA("sync", S_.KT[:, DH + r, t0_:t0_ + tw], onesbf.ap[:, 0:tw], [onesbf.b], [S_.B_kones])
        TS_("dve", ntri.ap, tri.ap, -1.0, None, ALU.mult, None, [tri], [ntri])
        MSET("dve", onesM, 1.0 / 1024.0)
        MSET("dve", ones8, 1.0)
        MSET("dve", epsc, EPS)
        MSET("dve", one1, 1.0)
        MSET("dve", halfpi, math.pi / 2.0)
        P.add("pool", lambda e: e.iota(trow.ap, pattern=[[1, 128]], base=0, channel_multiplier=0, allow_small_or_imprecise_dtypes=True), [], [trow.b])
        P.add("pool", lambda e: e.iota(tcol.ap, pattern=[[0, 1]], base=0, channel_multiplier=1, allow_small_or_imprecise_dtypes=True), [], [tcol.b])
        TS_("dve", ntcol.ap, tcol.ap, -1.0, None, ALU.mult, None, [tcol], [ntcol])
        MSET("pool", sel8, 1.0)
        P.add("pool", lambda e: e.affine_select(out=sel8.ap, in_=sel8.ap, pattern=[[-1, NH], [0, 128]], compare_op=ALU.is_equal, fill=0.0, base=0, channel_multiplier=1), [sel8.b], [sel8.b])

        if stop_after == "P0":
            return
        A.reset()
        c2n = A.f32([D], parts=2)
        DMA("sync", c2n.ap, c2, [], [c2n.b])
        pc = PS()
        for kc in range(8):
            TR(pc.ap[:, kc * 2:kc * 2 + 2], c2n.ap[:, kc * 128:(kc + 1) * 128], identf, [c2n], [pc])
        ACT(scT.ap.rearrange("p a b -> p (a b)"), pc.ap[:, 0:16], AF.Silu, [pc], [scT])
        spn = A.f32([128], parts=80)
        spT = A.f32([80])
        modT = A.f32([48, 2])
        WA = [A.bf([8, 1024]), A.bf([8, 1024])]
        wai = 0
        for l in range(DEPTH):
            DMA("sync", spn.ap[0:48, :], b_ada[l].rearrange("(a b) -> a b", b=128), [], [spn.b])
            for i, g in enumerate([g_pre_mix, g_post_mix, g_pre_ffn, g_post_ffn]):
                DMA("sync", spn.ap[48 + 8 * i:56 + 8 * i, :], g[l].rearrange("(a b) -> a b", b=128), [], [spn.b])
            pt = PS()
            TR(pt.ap[:, 0:80], spn.ap, identf, [spn], [pt])
            CP("dve", spT.ap, pt.ap[:, 0:80], [pt], [spT])
            pm = PS()
            for piece in range(6):
                wa = WA[wai % 2]; wai += 1
                DMA("pool", wa.ap, w_ada[l, :, piece * 1024:(piece + 1) * 1024].rearrange("(kc p) n -> p kc n", p=128), [], [wa.b])
                for j in range(8):
                    cj = piece * 8 + j
                    for kc in range(8):
                        MM(pm.ap[:, cj * 2:cj * 2 + 2], wa.ap[:, kc, j * 128:(j + 1) * 128], scT.ap[:, kc, :], kc == 0, kc == 7, [wa, scT], [pm])
            for s in range(2):
                TT_("dve", modT.ap[:, :, s], pm.ap[:, 0:96].rearrange("p (a b) -> p a b", b=2)[:, :, s], spT.ap[:, 0:48], ALU.add, [pm, spT], [modT])
            for s in range(2):
                par = PAR.ap[:, l, s]
                STT(par[:, 0, :], modT.ap[:, 8:16, s], 1.0, spT.ap[:, 48:56], ALU.add, ALU.mult, [modT, spT], [PAR])
                CP("dve", par[:, 1, :], modT.ap[:, 0:8, s], [modT], [PAR])
                TT_("dve", par[:, 2, :], modT.ap[:, 16:24, s], spT.ap[:, 56:64], ALU.mult, [modT, spT], [PAR])
                STT(par[:, 3, :], modT.ap[:, 32:40, s], 1.0, spT.ap[:, 64:72], ALU.add, ALU.mult, [modT, spT], [PAR])
                CP("dve", par[:, 4, :], modT.ap[:, 24:32, s], [modT], [PAR])
                TT_("dve", par[:, 5, :], modT.ap[:, 40:48, s], spT.ap[:, 72:80], ALU.mult, [modT, spT], [PAR])

        if stop_after == "P1":
            return
        for l in range(DEPTH):
            DMA("pool", WG2[l], w_gate[l].rearrange("(kc p) n -> p kc n", p=128), [], [B_WFF[l]])
            DMA("pool", WU2[l], w_up[l].rearrange("(kc p) n -> p kc n", p=128), [], [B_WFF[l]])
            DMA("pool", WD2[l], w_down[l].rearrange("(j p) n -> p j n", p=128), [], [B_WFF[l]])

        if stop_after == "P2":
            return
        P.barrier()
        A.reset()
        xin = [A.f32([D]), A.f32([D])]
        xtr = [A.f32([8, 128]), A.f32([8, 128])]
        cnt = 0
        for S in STREAMS:
            nb = S.T // S.QB
            for i in range(nb):
                xi = xin[cnt % 2]; xo = xtr[cnt % 2]; cnt += 1
                qb = S.QB
                DMA("sync", xi.ap[0:qb, :], S.x_in[i * qb:(i + 1) * qb, :], [], [xi.b])
                for half in range(2):
                    pt = PS()
                    for c4 in range(4):
                        c = half * 4 + c4
                        TR(pt.ap[:, c4 * 128:c4 * 128 + qb], xi.ap[0:qb, c * 128:(c + 1) * 128], identf, [xi], [pt])
                    CP("act" if half == 0 else "dve", xo.ap[:, half * 4:half * 4 + 4, 0:qb], pt.ap.rearrange("p (a b) -> p a b", b=128)[:, :, 0:qb], [pt], [xo])
                ti = (i * qb) // S.TT
                DMA("sync", S.xT.rearrange("(c p) t -> p c t", p=128)[:, :, i * qb:(i + 1) * qb], xo.ap[:, :, 0:qb], [xo.b], [S.B_xT[ti]])

        if stop_after == "X":
            return
        def rmsnorm_mod(S, l, xt, which, hm, tmp2, sq, rstd):
            TT = S.TT
            ACT(sq.ap, xt.ap, AF.Square, [xt], [sq])
            pss = PS()
            for c in range(8):
                MM(pss.ap[:, 0:TT], onesM.ap, sq.ap[:, c, :], c == 0, c == 7, [onesM, sq], [pss])
            ACT(rstd.ap, pss.ap[:, 0:TT], AF.Sqrt, [pss, epsc], [rstd], bias=epsc.ap, scale=1.0)
            P.add("dve", lambda e: e.reciprocal(out=rstd.ap, in_=rstd.ap), [rstd.b], [rstd.b])
            for c in range(8):
                tm = tmp2[c % 2]
                STT(tm.ap, xt.ap[:, c, :], PAR.ap[:, l, S.s, which, c:c + 1], rstd.ap, ALU.mult, ALU.mult, [xt, PAR, rstd], [tm])
                ACT(hm.ap[:, c, :], tm.ap, AF.Identity, [tm, PAR], [hm], bias=PAR.ap[:, l, S.s, which + 1, c:c + 1], scale=1.0)

        def phase_A(S, l, WIN):
            TT = S.TT; QB = S.QB; nsub = TT // QB
            s = S.s
            xts = [A.f32([8, TT]), A.f32([8, TT])]
            sq = A.bf([8, TT]); rstd = A.f32([TT])
            tmp2 = [A.f32([TT]), A.f32([TT])]
            hms = [A.bf([8, TT]), A.bf([8, TT])]
            qTa = A.bf([NH, TT], parts=64); kTa = A.bf([NH, TT], parts=64)
            ktm = [A.f32([DATT]), A.f32([DATT])]; vtm = [A.f32([DATT]), A.f32([DATT])]
            vx = [A.bf([NH, VW]), A.bf([NH, VW])]
            uTa = A.bf([4, TT])
            gT = A.f32([TT], parts=8); lf = A.f32([TT], parts=8); Fp = A.f32([TT], parts=8)
            Dq = A.f32([TT], parts=8); Dsp = [A.bf([TT], parts=8) for _ in range(3)]
            lfn = 0
            if S.koff > 0:
                MSET("dve", F_T[s], 0.0)
                npast = S.koff
                clt = A.f32([npast], parts=8)
                Fpast = A.f32([npast], parts=8)
                DMA("sync", clt.ap, clf[l], [], [clt.b])
                MSET("dve", fcarry, 0.0)
                cs_ = min(512, npast)
                for i0 in range(0, npast, cs_):
                    P.add("dve", lambda e, i0=i0: e.tensor_tensor_scan(out=Fpast.ap[:, i0:i0 + cs_], data0=ones8.ap[:, 0:cs_], data1=clt.ap[:, i0:i0 + cs_], initial=fcarry.ap, op0=ALU.mult, op1=ALU.add), [ones8.b, clt.b, fcarry.b], [Fpast.b])
                    CP("dve", fcarry.ap, Fpast.ap[:, i0 + cs_ - 1:i0 + cs_], [Fpast], [fcarry])
                pf = PS()
                for kt in range(npast // 128):
                    TR(pf.ap[:, kt * 8:kt * 8 + 8], Fpast.ap[:, kt * 128:(kt + 1) * 128], identf, [Fpast], [pf])
                CP("dve", F_T[s].ap[:, 0:npast // 128, :], pf.ap[:, 0:(npast // 128) * 8].rearrange("p (a b) -> p a b", b=8), [pf], [F_T[s]])
            else:
                MSET("dve", fcarry, 0.0)
            DMA("sync", negb.ap, b_forget[l].rearrange("(a b) -> a b", b=1), [], [negb.b])
            TS_("dve", negb.ap, negb.ap, -1.0, None, ALU.mult, None, [negb], [negb])
            for i in range(S.nt):
                xt = xts[i % 2]; hm = hms[i % 2]
                t0 = i * TT
                DMA("sync", xt.ap, S.xT.rearrange("(c p) t -> p c t", p=128)[:, :, t0:t0 + TT], [S.B_xT[i]], [xt.b])
                rmsnorm_mod(S, l, xt, 0, hm, tmp2, sq, rstd)
                if stop_after == "A1":
                    return
                for h in range(NH):
                    pq = PS()
                    for c in range(8):
                        MM(pq.ap[0:64, 0:TT], WIN.ap[:, c, h * 64:(h + 1) * 64], hm.ap[:, c, :], c == 0, c == 7, [WIN, hm], [pq])
                    ACT(qTa.ap[:, h, :], pq.ap[0:64, 0:TT], AF.Identity, [pq], [qTa], scale=0.125)
                    pk = PS()
                    for c in range(8):
                        MM(pk.ap[0:64, 0:TT], WIN.ap[:, c, DATT + h * 64:DATT + (h + 1) * 64], hm.ap[:, c, :], c == 0, c == 7, [WIN, hm], [pk])
                    CP("dve", kTa.ap[:, h, :], pk.ap[0:64, 0:TT], [pk], [kTa])
                DMA("sync", S.QT.rearrange("h d t -> d h t")[0:DH, :, t0:t0 + TT], qTa.ap, [qTa.b], [S.B_qkv])
                DMA("sync", S.KT.rearrange("h d t -> d h t")[0:DH, :, t0:t0 + TT], kTa.ap, [kTa.b], [S.B_qkv])
                if stop_after == "A2":
                    return
                for j in range(nsub):
                    tb0 = t0 + j * QB
                    kt_ = ktm[j % 2]; vt_ = vtm[j % 2]; vx_ = vx[j % 2]
                    pk = PS()
                    for c in range(8):
                        MM(pk.ap[0:QB, :], hm.ap[:, c, j * QB:(j + 1) * QB], WIN.ap[:, c, DATT:2 * DATT], c == 0, c == 7, [WIN, hm], [pk])
                    CP("act", kt_.ap[0:QB, :], pk.ap[0:QB, :], [pk], [kt_])
                    if stop_after == "A2a":
                        return
                    DMA("sync", S.nk_o[l].rearrange("h t d -> t h d")[tb0:tb0 + QB], kt_.ap[0:QB, :].rearrange("p (h d) -> p h d", d=DH), [kt_.b], [])
                    if stop_after == "A2b":
                        return
                    pv = PS()
                    for c in range(8):
                        MM(pv.ap[0:QB, :], hm.ap[:, c, j * QB:(j + 1) * QB], WIN.ap[:, c, 2 * DATT:3 * DATT], c == 0, c == 7, [WIN, hm], [pv])
                    CP("act", vt_.ap[0:QB, :], pv.ap[0:QB, :], [pv], [vt_])
                    DMA("sync", S.nv_o[l].rearrange("h t d -> t h d")[tb0:tb0 + QB], vt_.ap[0:QB, :].rearrange("p (h d) -> p h d", d=DH), [vt_.b], [])
                    if stop_after == "A2c":
                        return
                    P.add("pool", lambda e, vx_=vx_: e.memset(vx_.ap[0:QB].rearrange("p h d -> p (h d)"), 1.0), [], [vx_.b])
                    CP("dve", vx_.ap[0:QB, :, 0:DH], vt_.ap[0:QB, :].rearrange("p (h d) -> p h d", d=DH), [vt_], [vx_])
                    if stop_after == "A2d":
                        return
                    DMA("sync", S.VX[tb0:tb0 + QB], vx_.ap[0:QB], [vx_.b], [S.B_qkv])
                if stop_after == "A3":
                    return
                for m in range(4):
                    pu = PS()
                    for c in range(8):
                        MM(pu.ap[:, 0:TT], WIN.ap[:, c, 3 * DATT + NH + m * 128:3 * DATT + NH + (m + 1) * 128], hm.ap[:, c, :], c == 0, c == 7, [WIN, hm], [pu])
                    CP("act" if m % 2 else "dve", uTa.ap[:, m, :], pu.ap[:, 0:TT], [pu], [uTa])
                DMA("sync", S.uT.rearrange("(c p) t -> p c t", p=128)[:, :, t0:t0 + TT], uTa.ap, [uTa.b], [S.B_uT])
                if stop_after == "A4":
                    return
                pg = PS()
                for c in range(8):
                    MM(pg.ap[0:8, 0:TT], WIN.ap[:, c, 3 * DATT:3 * DATT + NH], hm.ap[:, c, :], c == 0, c == 7, [WIN, hm], [pg])
                ACT(gT.ap, pg.ap[0:8, 0:TT], AF.Exp, [pg, negb], [gT], bias=negb.ap, scale=-1.0)
                ACT(gT.ap, gT.ap, AF.Ln, [gT, one1], [gT], bias=one1.ap[0:8], scale=1.0)
                TS_("dve", lf.ap, gT.ap, -1.0, None, ALU.mult, None, [gT], [lf])
                DMA("sync", S.nl_o[l][:, t0:t0 + TT], lf.ap, [lf.b], [])
                P.add("dve", lambda e: e.tensor_tensor_scan(out=Fp.ap, data0=ones8.ap[:, 0:TT], data1=lf.ap, initial=fcarry.ap, op0=ALU.mult, op1=ALU.add), [ones8.b, lf.b, fcarry.b], [Fp.b])
                CP("dve", fcarry.ap, Fp.ap[:, TT - 1:TT], [Fp], [fcarry])
                if stop_after == "A5":
                    return
                assert S.TT == S.QG
                CP("dve", Fcs[s].ap[:, i:i + 1], Fp.ap[:, 0:1], [Fp], [Fcs[s]])
                TS_("dve", Dq.ap, Fp.ap, Fcs[s].ap[:, i:i + 1], None, ALU.subtract, None, [Fp, Fcs[s]], [Dq])
                for r in range(3):
                    CP("dve", Dsp[r].ap, Dq.ap, [Dq], [Dsp[r]])
                    if r < 2:
                        TT_("dve", Dq.ap, Dq.ap, Dsp[r].ap, ALU.subtract, [Dq, Dsp[r]], [Dq])
                    DMA("sync", S.QT[:, DH + r, t0:t0 + TT], Dsp[r].ap, [Dsp[r].b], [S.B_qkv])
                pf = PS()
                for j in range(nsub):
                    TR(pf.ap[0:QB, j * 8:j * 8 + 8], Fp.ap[:, j * QB:(j + 1) * QB], identf, [Fp], [pf])
                kt0 = (S.koff + t0) // 128
                CP("dve", F_T[s].ap[0:QB, kt0:kt0 + nsub, :], pf.ap[0:QB, 0:nsub * 8].rearrange("p (a b) -> p a b", b=8), [pf], [F_T[s]])
            pcb = PS()
            for h in range(NH):
                MM(pcb.ap[:, h * S.ngr:(h + 1) * S.ngr], sel8.ap[:, h, :], Fcs[s].ap, True, True, [sel8, Fcs[s]], [pcb])
            CP("dve", CBC[s].ap.rearrange("p a b -> p (a b)"), pcb.ap[:, 0:NH * S.ngr], [pcb], [CBC[s]])

        def phase_B(S, l):
            s = S.s; QB = S.QB; QG = S.QG; nsg = QG // QB; nkt = S.nkt; koff = S.koff; Tn = S.T; TK = S.TK
            npast_t = koff // 128
            KA = DH + 3
            QTh = [A.bf([Tn], parts=KA), A.bf([Tn], parts=KA)]
            KTh = [A.bf([TK], parts=KA), A.bf([TK], parts=KA)]
            VXh = [A.bf([nkt, VW]), A.bf([nkt, VW])]
            NPT = 4
            pTs = [A.bf([QG]) for _ in range(NPT)]
            biasg = [A.f32([nkt]), A.f32([nkt])]
            rsum = [A.f32([nsg]), A.f32([nsg])]
            if koff > 0:
                ckt = [A.f32([npast_t, DH]), A.f32([npast_t, DH])]
            LA = 2
            cnt = {"pt": 0, "bg": 0, "po": 0, "ps": 0}
            for h in range(NH):
                qt = QTh[h % 2]; kt = KTh[h % 2]; vxh = VXh[h % 2]
                DMA("sync", qt.ap, S.QT[h], [S.B_qkv], [qt.b])
                DMA("sync", kt.ap[:, koff:koff + Tn], S.KT[h], [S.B_qkv, S.B_kones], [kt.b])
                if koff > 0:
                    P.add("dve", lambda e, kt=kt: e.memset(kt.ap[DH:DH + 3, 0:koff], 1.0), [], [kt.b])
                ntile_new = (Tn + 127) // 128
                if Tn >= 128:
                    DMA("sync", vxh.ap[:, npast_t:npast_t + ntile_new, :], S.VX.rearrange("(a p) h d -> p a h d", p=128)[:, :, h, :], [S.B_qkv], [vxh.b])
                else:
                    DMA("sync", vxh.ap[0:Tn, npast_t, :], S.VX[:, h, :], [S.B_qkv], [vxh.b])
                if koff > 0:
                    ck_ = ckt[h % 2]
                    DMA("sync", ck_.ap, ck[l, h].rearrange("(a p) d -> p a d", p=128), [], [ck_.b])
                    for a0 in range(0, npast_t, 4):
                        pt = PSB[cnt["ps"] % 4]; cnt["ps"] += 1
                        for a in range(a0, min(a0 + 4, npast_t)):
                            TR(pt.ap[0:64, (a - a0) * 128:(a - a0 + 1) * 128], ck_.ap[:, a, :], identf, [ck_], [pt])
                        na = min(4, npast_t - a0)
                        CP("dve", kt.ap[0:DH, a0 * 128:(a0 + na) * 128], pt.ap[0:64, 0:na * 128], [pt], [kt])
                    P.add("pool", lambda e, vxh=vxh: e.memset(vxh.ap[:, 0:npast_t, :].rearrange("p a d -> p (a d)"), 1.0), [], [vxh.b])
                    DMA("pool", vxh.ap[:, 0:npast_t, 0:DH], cv[l, h].rearrange("(a p) d -> p a d", p=128), [vxh.b], [vxh.b])
                for G in range(S.ngr):
                    q0 = G * QG
                    last_kt = (koff + q0 + QG - 1) // 128
                    first_diag = (koff + q0) // 128
                    bg = biasg[cnt["bg"] % 2]; cnt["bg"] += 1
                    TS_("dve", bg.ap[:, 0:last_kt + 1], F_T[s].ap[:, 0:last_kt + 1, h], -1.0, CBC[s].ap[:, h, G:G + 1], ALU.mult, ALU.add, [F_T[s], CBC[s]], [bg])
                    po = PSB[6 + (cnt["po"] % 2)]; cnt["po"] += 1
                    pov = po.ap[:, 0:nsg * 128].rearrange("p (i d) -> p i d", d=128)
                    blocks = []
                    for k_ in range(last_kt + 1):
                        kp = min(128, TK - k_ * 128)
                        j = max(0, k_ - first_diag)
                        blocks.append((k_, kp, j))

                    def score(bi):
                        k_, kp, j = blocks[bi]
                        psc = PSB[cnt["ps"] % 4]; cnt["ps"] += 1
                        c0 = j * QB
                        diag = k_ >= first_diag
                        MM(psc.ap[0:kp, c0:QG], kt.ap[:, k_ * 128:k_ * 128 + kp], qt.ap[:, q0 + c0:q0 + QG], True, not diag, [kt, qt], [psc])
                        if diag:
                            MM(psc.ap[0:kp, c0:c0 + QB], maskT.ap[0:QB, 0:kp], identb.ap[0:QB, 0:QB], False, True, [maskT, identb], [psc])
                        pT = pTs[cnt["pt"] % NPT]; cnt["pt"] += 1
                        ACT(pT.ap[0:kp, c0:QG], psc.ap[0:kp, c0:QG], AF.Exp, [psc, bg], [pT], bias=bg.ap[0:kp, k_:k_ + 1], scale=1.0)
                        return pT

                    def pv(bi, pT):
                        k_, kp, j = blocks[bi]
                        for i in range(j, nsg):
                            first = (bi == 0 and i == j)
                            last = (bi == len(blocks) - 1 and i == nsg - 1)
                            MM(pov[0:QB, i, 0:DH + 1], pT.ap[0:kp, i * QB:(i + 1) * QB], vxh.ap[0:kp, k_, 0:DH + 1], first, last, [pT, vxh], [po])

                    pend = []
                    nb = len(blocks)
                    for bi in range(nb + LA):
                        if bi < nb:
                            pend.append((bi, score(bi)))
                        if bi >= LA:
                            b2, pT2 = pend.pop(0)
                            pv(b2, pT2)
                    rs = rsum[G % 2]
                    P.add("dve", lambda e, rs=rs, pov=pov: e.reciprocal(out=rs.ap[0:QB, :], in_=pov[0:QB, :, DH]), [po.b], [rs.b])
                    TT_("dve", ATT[s].ap[0:QB, G * nsg:(G + 1) * nsg, h * DH:(h + 1) * DH], pov[0:QB, :, 0:DH], rs.ap[0:QB, :].unsqueeze(2).to_broadcast([QB, nsg, DH]), ALU.mult, [po, rs], [ATT[s]])

        def s5_prep(l):
            tb = {}
            Ainv = A.f32([16, 2, 128]); ApT = A.f32([16, 2, 128]); T2 = A.bf([4, 2, 128]); CT = A.bf([4, 2, 128])
            Dd = A.bf([4, 128]); a1 = A.f32([16, 2]); WGLU = A.bf([4, DSSM]); T2f = A.bf([4, 4, 2, 128]); CTn = A.bf([4, 128])
            tb.update(Ainv=Ainv, ApT=ApT, T2=T2, CT=CT, Dd=Dd, a1=a1, WGLU=WGLU, T2f=T2f, CTn=CTn)
            mark = A.off
            tb["mark"] = mark
            are_n = A.f32([128], parts=16); aim_n = A.f32([128], parts=16); ldt = A.f32([2], parts=16)
            DMA("sync", are_n.ap, ssm_a_re[l].rearrange("(a b) p -> a (b p)", b=2), [], [are_n.b])
            DMA("sync", aim_n.ap, ssm_a_im[l].rearrange("(a b) p -> a (b p)", b=2), [], [aim_n.b])
            DMA("sync", ldt.ap, ssm_log_dt[l].rearrange("(a b) -> a b", b=2), [], [ldt.b])
            ACT(ldt.ap, ldt.ap, AF.Exp, [ldt], [ldt])
            al_n = A.f32([128], parts=16); th_n = A.f32([128], parts=16)
            dtb = ldt.ap.unsqueeze(2).to_broadcast([16, 2, 64])
            TT_("dve", al_n.ap.rearrange("p (a b) -> p a b", a=2), are_n.ap.rearrange("p (a b) -> p a b", a=2), dtb, ALU.mult, [are_n, ldt], [al_n])
            TT_("dve", th_n.ap.rearrange("p (a b) -> p a b", a=2), aim_n.ap.rearrange("p (a b) -> p a b", a=2), dtb, ALU.mult, [aim_n, ldt], [th_n])
            sm = A.f32([4, 16])
            pt = PS()
            for i, src in enumerate([al_n, th_n, are_n, aim_n]):
                TR(pt.ap[:, i * 16:(i + 1) * 16], src.ap, identf, [src], [pt])
            CP("dve", sm.ap.rearrange("p a b -> p (a b)"), pt.ap[:, 0:64], [pt], [sm])

            def sincos(ang, shape_elems, cosv, sinv, tmpk):
                TS_("dve", tmpk.ap, ang.ap, 1.0 / TWO_PI, MAGIC, ALU.mult, ALU.add, [ang], [tmpk])
                TS_("dve", tmpk.ap, tmpk.ap, -MAGIC, None, ALU.add, None, [tmpk], [tmpk])
                STT(ang.ap, tmpk.ap, -CW1, ang.ap, ALU.mult, ALU.add, [tmpk, ang], [ang])
                STT(ang.ap, tmpk.ap, -CW2, ang.ap, ALU.mult, ALU.add, [tmpk, ang], [ang])
                TS_("dve", ang.ap, ang.ap, PI_LO, -PI_LO, ALU.min, ALU.max, [ang], [ang])
                ACT(sinv.ap, ang.ap, AF.Sin, [ang], [sinv])
                ACT(tmpk.ap, ang.ap, AF.Abs, [ang], [tmpk])
                ACT(cosv.ap, tmpk.ap, AF.Sin, [tmpk, halfpi], [cosv], bias=halfpi.ap[0:cosv.ap.shape[0]], scale=-1.0)

            angS = A.f32([16, 128]); kS = A.f32([16, 128]); cS = A.f32([16, 128]); sS = A.f32([16, 128]); eS = A.f32([16, 128])
            trb = trow.ap.unsqueeze(1).to_broadcast([128, 16, 128])
            TT_("dve", angS.ap, sm.ap[:, 1, :].unsqueeze(2).to_broadcast([128, 16, 128]), trb, ALU.mult, [sm, trow], [angS])
            TT_("dve", eS.ap, sm.ap[:, 0, :].unsqueeze(2).to_broadcast([128, 16, 128]), trb, ALU.mult, [sm, trow], [eS])
            ACT(eS.ap, eS.ap, AF.Exp, [eS], [eS])
            sincos(angS, 2048, cS, sS, kS)
            TT_("dve", ApT.ap[:, :, 0, :], eS.ap, cS.ap, ALU.mult, [eS, cS], [ApT])
            TT_("dve", ApT.ap[:, :, 1, :], eS.ap, sS.ap, ALU.mult, [eS, sS], [ApT])
            CP("dve", a1.ap, ApT.ap[:, :, :, 1], [ApT], [a1])
            if stop_after == "Ca":
                return tb
            cf = A.f32([8, 16])
            are = sm.ap[:, 2, :]; aim = sm.ap[:, 3, :]
            TS_("dve", cf.ap[:, 0, :], a1.ap[:, :, 0], -1.0, None, ALU.add, None, [a1], [cf])
            TT_("dve", cf.ap[:, 1, :], are, are, ALU.mult, [sm], [cf])
            TT_("dve", cf.ap[:, 2, :], aim, aim, ALU.mult, [sm], [cf])
            TT_("dve", cf.ap[:, 1, :], cf.ap[:, 1, :], cf.ap[:, 2, :], ALU.add, [cf], [cf])
            P.add("dve", lambda e: e.reciprocal(out=cf.ap[:, 1, :], in_=cf.ap[:, 1, :]), [cf.b], [cf.b])
            TT_("dve", cf.ap[:, 2, :], cf.ap[:, 0, :], are, ALU.mult, [cf, sm], [cf])
            TT_("dve", cf.ap[:, 3, :], a1.ap[:, :, 1], aim, ALU.mult, [a1, sm], [cf])
            TT_("dve", cf.ap[:, 2, :], cf.ap[:, 2, :], cf.ap[:, 3, :], ALU.add, [cf], [cf])
            TT_("dve", cf.ap[:, 4, :], cf.ap[:, 2, :], cf.ap[:, 1, :], ALU.mult, [cf], [cf])
            TT_("dve", cf.ap[:, 2, :], a1.ap[:, :, 1], are, ALU.mult, [a1, sm], [cf])
            TT_("dve", cf.ap[:, 3, :], cf.ap[:, 0, :], aim, ALU.mult, [cf, sm], [cf])
            TT_("dve", cf.ap[:, 2, :], cf.ap[:, 2, :], cf.ap[:, 3, :], ALU.subtract, [cf], [cf])
            TT_("dve", cf.ap[:, 5, :], cf.ap[:, 2, :], cf.ap[:, 1, :], ALU.mult, [cf], [cf])
            if stop_after == "Cb":
                return tb
            bre = A.f32([16, 16]); bim = A.f32([16, 16]); Z = A.f32([16, 2, 2, 16]); tq = A.f32([16, 16])
            for g2 in range(2):
                DMA("sync", bre.ap[g2 * 64:(g2 + 1) * 64], ssm_b_re[l].rearrange("(a b) p m -> b p a m", b=2)[g2], [], [bre.b])
                DMA("sync", bim.ap[g2 * 64:(g2 + 1) * 64], ssm_b_im[l].rearrange("(a b) p m -> b p a m", b=2)[g2], [], [bim.b])
            MSET("pool", Z, 0.0)
            cre_b = cf.ap[:, 4, :].unsqueeze(2).to_broadcast([128, 16, 16])
            cim_b = cf.ap[:, 5, :].unsqueeze(2).to_broadcast([128, 16, 16])
            for g2 in range(2):
                ps_ = slice(g2 * 64, (g2 + 1) * 64)
                TT_("dve", tq.ap[ps_], bim.ap[ps_], cim_b[ps_], ALU.mult, [bim, cf], [tq])
                TT_("dve", Z.ap[ps_, :, 0, g2, :], bre.ap[ps_], cre_b[ps_], ALU.mult, [bre, cf], [Z])
                TT_("dve", Z.ap[ps_, :, 0, g2, :], Z.ap[ps_, :, 0, g2, :], tq.ap[ps_], ALU.subtract, [Z, tq], [Z])
                TT_("dve", tq.ap[ps_], bre.ap[ps_], cim_b[ps_], ALU.mult, [bre, cf], [tq])
                TT_("dve", Z.ap[ps_, :, 1, g2, :], bim.ap[ps_], cre_b[ps_], ALU.mult, [bim, cf], [Z])
                TT_("dve", Z.ap[ps_, :, 1, g2, :], Z.ap[ps_, :, 1, g2, :], tq.ap[ps_], ALU.add, [Z, tq], [Z])
            for ch in range(4):
                for ri in range(2):
                    pt = PS()
                    zin = A.f32([128])
                    CP("pool", zin.ap.rearrange("p (a b c) -> p a b c", a=4, b=2), Z.ap[:, ch * 4:(ch + 1) * 4, ri, :, :], [Z], [zin])
                    TR(pt.ap[:, 0:128], zin.ap, identf, [zin], [pt])
                    CP("dve", T2.ap[:, ch, ri, :], pt.ap[:, 0:128], [pt], [T2])
            MSET("pool", T2f, 0.0)
            for i4 in range(4):
                CP("pool", T2f.ap[i4 * 32:(i4 + 1) * 32, :, i4, :, :], T2.ap[i4 * 32:(i4 + 1) * 32, :, :, :], [T2], [T2f])
            if stop_after == "Cc":
                return tb
            Zc = A.f32([4, 2, 128])
            MSET("pool", Zc, 0.0)
            for ri, csrc in enumerate([ssm_c_re, ssm_c_im]):
                cview = csrc[l].rearrange("(ch i b) m p -> i b m ch p", i=4, b=2)
                for i4 in range(4):
                    for g2 in range(2):
                        r0 = i4 * 32 + g2 * 16
                        DMA("sync", Zc.ap[r0:r0 + 16, :, ri, g2 * 64:(g2 + 1) * 64], cview[i4, g2], [], [Zc.b])
            for ch in range(4):
                for ri in range(2):
                    pt = PS()
                    TR(pt.ap[:, 0:128], Zc.ap[:, ch, ri, :], identf, [Zc], [pt])
                    if ri == 0:
                        CP("dve", CT.ap[:, ch, ri, :], pt.ap[:, 0:128], [pt], [CT])
                        TS_("dve", CTn.ap[:, ch, :], pt.ap[:, 0:128], -1.0, None, ALU.mult, None, [pt], [CTn])
                    else:
                        TS_("dve", CT.ap[:, ch, ri, :], pt.ap[:, 0:128], -1.0, None, ALU.mult, None, [pt], [CT])
            dcol = A.f32([4]); dnat = A.f32([128], parts=4)
            DMA("sync", dnat.ap, ssm_d[l].rearrange("(c p) -> c p", p=128), [], [dnat.b])
            pt = PS()
            TR(pt.ap[:, 0:4], dnat.ap, identf, [dnat], [pt])
            CP("dve", dcol.ap, pt.ap[:, 0:4], [pt], [dcol])
            for ch in range(4):
                TS_("dve", Dd.ap[:, ch, :], identf.ap, dcol.ap[:, ch:ch + 1], None, ALU.mult, None, [identf, dcol], [Dd])
            if stop_after == "Cd":
                return tb
            arb = A.f32([2048]); aib = A.f32([2048]); ldb = A.f32([32])
            DMA("sync", arb.ap, ssm_a_re[l].rearrange("g p -> (g p)").partition_broadcast(128), [], [arb.b])
            DMA("sync", aib.ap, ssm_a_im[l].rearrange("g p -> (g p)").partition_broadcast(128), [], [aib.b])
            DMA("sync", ldb.ap, ssm_log_dt[l].partition_broadcast(128), [], [ldb.b])
            ACT(ldb.ap, ldb.ap, AF.Exp, [ldb], [ldb])
            dtbb = ldb.ap.unsqueeze(2).to_broadcast([128, 32, 64])
            TT_("dve", arb.ap.rearrange("p (g q) -> p g q", q=64), arb.ap.rearrange("p (g q) -> p g q", q=64), dtbb, ALU.mult, [arb, ldb], [arb])
            TT_("dve", aib.ap.rearrange("p (g q) -> p g q", q=64), aib.ap.rearrange("p (g q) -> p g q", q=64), dtbb, ALU.mult, [aib, ldb], [aib])
            angT = angS; kT_ = kS; cT_ = cS; sT_ = sS; eT = eS
            fl = lambda t: t.ap.rearrange("p a b -> p (a b)")
            TS_("dve", fl(angT), aib.ap, tcol.ap, None, ALU.mult, None, [aib, tcol], [angT])
            ACT(fl(eT), arb.ap, AF.Exp, [arb, ntcol], [eT], scale=ntcol.ap)
            sincos(angT, 2048, cT_, sT_, kT_)
            TT_("dve", Ainv.ap[:, :, 0, :], eT.ap, cT_.ap, ALU.mult, [eT, cT_], [Ainv])
            STT(Ainv.ap[:, :, 1, :], eT.ap, -1.0, sT_.ap, ALU.mult, ALU.mult, [eT, sT_], [Ainv])
            DMA("pool", WGLU.ap, w_glu[l].rearrange("(kc p) n -> p kc n", p=128), [], [WGLU.b])
            tb["mark"] = mark
            return tb

        def phase_C(S, l, tb, hc_init_from=None):
            s = S.s; QB = S.QB; nblk = S.T // QB
            Ainv = tb["Ainv"]; ApT = tb["ApT"]; T2f = tb["T2f"]; CT = tb["CT"]; Dd = tb["Dd"]; a1 = tb["a1"]; WGLU = tb["WGLU"]; CTn = tb["CTn"]
            uTb = [A.bf([4, QB]), A.bf([4, QB])]
            bus = [A.f32([16, 2, 128]), A.f32([16, 2, 128])]
            cus = A.f32([16, 2, 128])
            t1 = A.bf([16, 128]); t2 = A.bf([16, 128]); t3 = A.bf([16, 128]); t4 = A.bf([16, 128])
            w1 = A.bf([16, 128]); w2 = A.bf([16, 128]); w3 = A.bf([16, 128]); w4 = A.bf([16, 128])
            hl = A.f32([16, 2]); hc = A.f32([16, 2]); hq = A.f32([16, 2])
            y2 = A.f32([DSSM]); yin = A.f32([DSSM]); zt = A.bf([DSSM]); zT = A.bf([4, 128]); sg = A.f32([4, 128]); soT = A.bf([4, 128])
            hn = A.f32([128], parts=16)

            def carry_from_hl():
                TT_("dve", hq.ap[:, :, 0], a1.ap[:, :, 0], hl.ap[:, :, 0], ALU.mult, [a1, hl], [hq])
                TT_("dve", hq.ap[:, :, 1], a1.ap[:, :, 1], hl.ap[:, :, 1], ALU.mult, [a1, hl], [hq])
                TT_("dve", hc.ap[:, :, 0], hq.ap[:, :, 0], hq.ap[:, :, 1], ALU.subtract, [hq], [hc])
                TT_("dve", hq.ap[:, :, 0], a1.ap[:, :, 0], hl.ap[:, :, 1], ALU.mult, [a1, hl], [hq])
                TT_("dve", hq.ap[:, :, 1], a1.ap[:, :, 1], hl.ap[:, :, 0], ALU.mult, [a1, hl], [hq])
                TT_("dve", hc.ap[:, :, 1], hq.ap[:, :, 0], hq.ap[:, :, 1], ALU.add, [hq], [hc])

            if S.koff > 0:
                for ri, src in enumerate([sre, sim]):
                    DMA("sync", hn.ap, src[l].rearrange("(a b) p -> a (b p)", b=2), [], [hn.b])
                    pt = PSB[5]
                    TR(pt.ap[:, 0:16], hn.ap, identf, [hn], [pt])
                    CP("dve", hl.ap[:, :, ri], pt.ap[:, 0:16], [pt], [hl])
                carry_from_hl()
            else:
                MSET("dve", hc, 0.0)

            def stage1(blk):
                t0 = blk * QB
                ut = uTb[blk % 2]; bu = bus[blk % 2]
                DMA("sync", ut.ap, S.uT.rearrange("(c p) t -> p c t", p=128)[:, :, t0:t0 + QB], [S.B_uT], [ut.b])
                for ch in range(4):
                    pb = [PSB[(ch % 2) * 2], PSB[(ch % 2) * 2 + 1]]
                    for hf_ in range(2):
                        MM(pb[hf_].ap[0:QB, :], ut.ap[:, ch, :], T2f.ap[:, ch, hf_ * 2:hf_ * 2 + 2, :, :].rearrange("p i r q -> p (i r q)"), True, True, [ut, T2f], [pb[hf_]])
                        CP("act", bu.ap[0:QB, ch * 4 + hf_ * 2:ch * 4 + hf_ * 2 + 2].rearrange("p i r q -> p (i r q)"), pb[hf_].ap[0:QB, :], [pb[hf_]], [bu])

            def stage2a(blk):
                t0 = blk * QB
                ut = uTb[blk % 2]; bu = bus[blk % 2]
                TT_("dve", w1.ap[0:QB], bu.ap[0:QB, :, 0, :], Ainv.ap[0:QB, :, 0, :], ALU.mult, [bu, Ainv], [w1])
                TT_("pool", w4.ap[0:QB], bu.ap[0:QB, :, 1, :], Ainv.ap[0:QB, :, 0, :], ALU.mult, [bu, Ainv], [w4])
                TT_("dve", w2.ap[0:QB], bu.ap[0:QB, :, 1, :], Ainv.ap[0:QB, :, 1, :], ALU.mult, [bu, Ainv], [w2])
                TT_("dve", w3.ap[0:QB], bu.ap[0:QB, :, 0, :], Ainv.ap[0:QB, :, 1, :], ALU.mult, [bu, Ainv], [w3])

            def stageC(blk):
                t0 = blk * QB
                ut = uTb[blk % 2]; bu = bus[blk % 2]
                for ch in range(4):
                    pc = [PSB[4], PSB[5]]
                    for i4 in range(4):
                        pr = ch * 4 + i4
                        tgt = pc[i4 // 2]
                        cre_ = ((i4 % 2) * 2 + 0) * 128; cim_ = ((i4 % 2) * 2 + 1) * 128
                        MM(tgt.ap[:, cre_:cre_ + QB], w1.ap[0:QB, pr, :], tri.ap[0:QB, 0:QB], True, False, [w1, tri], [tgt])
                        MM(tgt.ap[:, cre_:cre_ + QB], w2.ap[0:QB, pr, :], ntri.ap[0:QB, 0:QB], False, True, [w2, ntri], [tgt])
                        MM(tgt.ap[:, cim_:cim_ + QB], w3.ap[0:QB, pr, :], tri.ap[0:QB, 0:QB], True, False, [w3, tri], [tgt])
                        MM(tgt.ap[:, cim_:cim_ + QB], w4.ap[0:QB, pr, :], tri.ap[0:QB, 0:QB], False, True, [w4, tri], [tgt])
                    for hf_ in range(2):
                        CP("act", cus.ap[:, ch * 4 + hf_ * 2:ch * 4 + hf_ * 2 + 2, :, 0:QB], pc[hf_].ap.rearrange("p (i r q) -> p i r q", i=2, r=2)[:, :, :, 0:QB], [pc[hf_]], [cus])

            def stageH(blk):
                t0 = blk * QB
                ut = uTb[blk % 2]; bu = bus[blk % 2]
                TT_("dve", cus.ap[:, :, 0, 0:QB], cus.ap[:, :, 0, 0:QB], hc.ap[:, :, 0].unsqueeze(2).to_broadcast([128, 16, QB]), ALU.add, [cus, hc], [cus])
                TT_("pool", cus.ap[:, :, 1, 0:QB], cus.ap[:, :, 1, 0:QB], hc.ap[:, :, 1].unsqueeze(2).to_broadcast([128, 16, QB]), ALU.add, [cus, hc], [cus])
                apr = ApT.ap[:, :, 0, 0:QB]; api = ApT.ap[:, :, 1, 0:QB]
                cre = cus.ap[:, :, 0, 0:QB]; cim = cus.ap[:, :, 1, 0:QB]
                L = QB - 1
                TT_("dve", hq.ap[:, :, 0], ApT.ap[:, :, 0, L], cus.ap[:, :, 0, L], ALU.mult, [ApT, cus], [hq])
                TT_("dve", hq.ap[:, :, 1], ApT.ap[:, :, 1, L], cus.ap[:, :, 1, L], ALU.mult, [ApT, cus], [hq])
                TT_("dve", hl.ap[:, :, 0], hq.ap[:, :, 0], hq.ap[:, :, 1], ALU.subtract, [hq], [hl])
                TT_("dve", hq.ap[:, :, 0], ApT.ap[:, :, 0, L], cus.ap[:, :, 1, L], ALU.mult, [ApT, cus], [hq])
                TT_("dve", hq.ap[:, :, 1], ApT.ap[:, :, 1, L], cus.ap[:, :, 0, L], ALU.mult, [ApT, cus], [hq])
                TT_("dve", hl.ap[:, :, 1], hq.ap[:, :, 0], hq.ap[:, :, 1], ALU.add, [hq], [hl])
                carry_from_hl()
                TT_("dve", t1.ap[:, :, 0:QB], apr, cre, ALU.mult, [ApT, cus], [t1])
                TT_("pool", t4.ap[:, :, 0:QB], api, cre, ALU.mult, [ApT, cus], [t4])
                TT_("dve", t2.ap[:, :, 0:QB], api, cim, ALU.mult, [ApT, cus], [t2])
                TT_("dve", t3.ap[:, :, 0:QB], apr, cim, ALU.mult, [ApT, cus], [t3])
                py = PSB[6]
                for ch in range(4):
                    MM(py.ap[0:QB, ch * 128:(ch + 1) * 128], ut.ap[:, ch, :], Dd.ap[:, ch, :], ch == 0, False, [ut, Dd], [py])
                for ch in range(4):
                    for i4 in range(4):
                        pr = ch * 4 + i4
                        o_ = py.ap[0:QB, pr * 32:(pr + 1) * 32]
                        cs_ = slice(i4 * 32, (i4 + 1) * 32)
                        MM(o_, t1.ap[:, pr, 0:QB], CT.ap[:, ch, 0, cs_], False, False, [t1, CT], [py])
                        MM(o_, t2.ap[:, pr, 0:QB], CTn.ap[:, ch, cs_], False, False, [t2, CTn], [py])
                        MM(o_, t3.ap[:, pr, 0:QB], CT.ap[:, ch, 1, cs_], False, False, [t3, CT], [py])
                        MM(o_, t4.ap[:, pr, 0:QB], CT.ap[:, ch, 1, cs_], False, pr == 15, [t4, CT], [py])

            def stage2c(blk):
                t0 = blk * QB
                ut = uTb[blk % 2]; bu = bus[blk % 2]
                py = PSB[6]
                ACT(y2.ap[0:QB], py.ap[0:QB, :], AF.Square, [py], [y2])
                TS_("dve", y2.ap[0:QB], y2.ap[0:QB], 0.044715, 1.0, ALU.mult, ALU.add, [y2], [y2])
                TT_("dve", yin.ap[0:QB], y2.ap[0:QB], py.ap[0:QB, :], ALU.mult, [y2, py], [yin])
                ACT(yin.ap[0:QB], yin.ap[0:QB], AF.Sigmoid, [yin], [yin], scale=1.5957691216057308)
                TT_("dve", zt.ap[0:QB], yin.ap[0:QB], py.ap[0:QB, :], ALU.mult, [yin, py], [zt])
                pz = PSB[7]
                pzb = pz.ap.bitcast(BF16)
                for c in range(4):
                    TR(pzb[:, c * 128:c * 128 + QB], zt.ap[0:QB, c * 128:(c + 1) * 128], identb, [zt], [pz])
                CP("act", zT.ap[:, :, 0:QB], pzb[:, 0:512].rearrange("p (a b) -> p a b", b=128)[:, :, 0:QB], [pz], [zT])
                pg = PSB[7]
                for m in range(4):
                    for kc in range(4):
                        MM(pg.ap[:, m * 128:m * 128 + QB], WGLU.ap[:, kc, m * 128:(m + 1) * 128], zT.ap[:, kc, 0:QB], kc == 0, kc == 3, [WGLU, zT], [pg])
                ACT(sg.ap[:, :, 0:QB], pg.ap.rearrange("p (a b) -> p a b", b=128)[:, :, 0:QB], AF.Sigmoid, [pg], [sg])
                TT_("dve", soT.ap[:, :, 0:QB], zT.ap[:, :, 0:QB], sg.ap[:, :, 0:QB], ALU.mult, [zT, sg], [soT])
                DMA("sync", S.soT.rearrange("(c p) t -> p c t", p=128)[:, :, t0:t0 + QB], soT.ap[:, :, 0:QB], [soT.b], [S.B_soT])

            stage1(0)
            stage2a(0)
            stageC(0)
            for blk in range(nblk):
                if blk + 1 < nblk:
                    stage1(blk + 1)
                stageH(blk)
                if blk + 1 < nblk:
                    stage2a(blk + 1)
                    stageC(blk + 1)
                stage2c(blk)
            for ri, dst in enumerate([S.nr_o, S.ni_o]):
                pt = PSB[4 + ri]
                hlc = A.f32([16])
                CP("dve", hlc.ap, hl.ap[:, :, ri], [hl], [hlc])
                TR(pt.ap[0:16, 0:128], hlc.ap, identf, [hlc], [pt])
                ho = A.f32([128], parts=16)
                CP("dve", ho.ap, pt.ap[0:16, 0:128], [pt], [ho])
                DMA("sync", dst[l].rearrange("(a b) p -> a (b p)", b=2), ho.ap, [ho.b], [])

        def post_norm_residual(S, l, oT, osq, xt, xn, gslot, rstd, tmp2):
            TT = S.TT
            pss = PS()
            for c in range(8):
                MM(pss.ap[:, 0:TT], onesM.ap, osq.ap[:, c, :], c == 0, c == 7, [onesM, osq], [pss])
            ACT(rstd.ap, pss.ap[:, 0:TT], AF.Sqrt, [pss, epsc], [rstd], bias=epsc.ap, scale=1.0)
            P.add("dve", lambda e: e.reciprocal(out=rstd.ap, in_=rstd.ap), [rstd.b], [rstd.b])
            for c in range(8):
                tm = tmp2[c % 2]
                STT(tm.ap, oT.ap[:, c, :], PAR.ap[:, l, S.s, gslot, c:c + 1], rstd.ap, ALU.mult, ALU.mult, [oT, PAR, rstd], [tm])
                TT_("dve", xn.ap[:, c, :], xt.ap[:, c, :], tm.ap, ALU.add, [xt, tm], [xn])

        def phase_DE(S, l, WOUT):
            s = S.s; TT = S.TT; QB = S.QB; nsub = TT // QB
            xt = A.f32([8, TT])
            xn = A.f32([8, TT]); oT = A.f32([8, TT]); sq = A.bf([8, TT]); osq = sq
            rstd = A.f32([TT]); tmp2 = [A.f32([TT]), A.f32([TT])]
            mixT = A.bf([8, TT]); hf = A.bf([8, TT]); hT = A.bf([NJ, TT]); sgt = A.f32([TT])
            WGs = [A.bf([8, 256]), A.bf([8, 256])]; WUs = [A.bf([8, 256]), A.bf([8, 256])]
            WDq = [A.bf([NJ, 256]), A.bf([NJ, 256])]
            wgi = 0; wdi = 0
            def load_mix(i):
                t0 = i * TT
                DMA("sync", mixT.ap[:, 4:8, :], S.soT.rearrange("(c p) t -> p c t", p=128)[:, :, t0:t0 + TT], [S.B_soT], [mixT.b])
                for j in range(nsub):
                    g = (t0 // QB) + j
                    pa = PS(); pab = pa.ap.bitcast(BF16)
                    for c in range(4):
                        TR(pab[:, c * 128:c * 128 + QB], ATT[s].ap[0:QB, g, c * 128:(c + 1) * 128], identb, [ATT[s]], [pa])
                    CP("act", mixT.ap[:, 0:4, j * QB:(j + 1) * QB], pab[:, 0:512].rearrange("p (a b) -> p a b", b=128)[:, :, 0:QB], [pa], [mixT])

            load_mix(0)
            for i in range(S.nt):
                t0 = i * TT
                DMA("sync", xt.ap, S.xT.rearrange("(c p) t -> p c t", p=128)[:, :, t0:t0 + TT], [S.B_xT[i]], [xt.b])
                pos = []
                for m in range(8):
                    po = PS()
                    for kc in range(8):
                        MM(po.ap[:, 0:TT], WOUT.ap[:, kc, m * 128:(m + 1) * 128], mixT.ap[:, kc, :], kc == 0, kc == 7, [WOUT, mixT], [po])
                    CP("dve", oT.ap[:, m, :], po.ap[:, 0:TT], [po], [oT])
                    ACT(osq.ap[:, m, :], po.ap[:, 0:TT], AF.Square, [po], [osq])
                if i + 1 < S.nt:
                    load_mix(i + 1)
                post_norm_residual(S, l, oT, osq, xt, xn, 2, rstd, tmp2)
                rmsnorm_mod(S, l, xn, 3, hf, tmp2, sq, rstd)
                for n0 in range(0, DFF, 256):
                    wg = WGs[wgi % 2]; wu = WUs[wgi % 2]; wgi += 1
                    DMA("sync", wg.ap, WG2[l][:, :, n0:n0 + 256], [B_WFF[l]], [wg.b])
                    DMA("sync", wu.ap, WU2[l][:, :, n0:n0 + 256], [B_WFF[l]], [wu.b])
                    for jj in range(2):
                        j = n0 // 128 + jj
                        pg = PS(); pu = PS()
                        for kc in range(8):
                            MM(pg.ap[:, 0:TT], wg.ap[:, kc, jj * 128:(jj + 1) * 128], hf.ap[:, kc, :], kc == 0, kc == 7, [wg, hf], [pg])
                        for kc in range(8):
                            MM(pu.ap[:, 0:TT], wu.ap[:, kc, jj * 128:(jj + 1) * 128], hf.ap[:, kc, :], kc == 0, kc == 7, [wu, hf], [pu])
                        ACT(sgt.ap, pg.ap[:, 0:TT], AF.Silu, [pg], [sgt])
                        TT_("dve", hT.ap[:, j, :], sgt.ap, pu.ap[:, 0:TT], ALU.mult, [sgt, pu], [hT])
                for q4 in range(4):
                    WD = WDq[wdi % 2]; wdi += 1
                    DMA("sync", WD.ap, WD2[l][:, :, q4 * 256:(q4 + 1) * 256], [B_WFF[l]], [WD.b])
                    for mm in range(2):
                        m = q4 * 2 + mm
                        pf = PS()
                        for j in range(NJ):
                            MM(pf.ap[:, 0:TT], WD.ap[:, j, mm * 128:(mm + 1) * 128], hT.ap[:, j, :], j == 0, j == NJ - 1, [WD, hT], [pf])
                        CP("dve", oT.ap[:, m, :], pf.ap[:, 0:TT], [pf], [oT])
                        ACT(osq.ap[:, m, :], pf.ap[:, 0:TT], AF.Square, [pf], [osq])
                post_norm_residual(S, l, oT, osq, xn, xt, 5, rstd, tmp2)
                DMA("sync", S.xT.rearrange("(c p) t -> p c t", p=128)[:, :, t0:t0 + TT], xt.ap, [xt.b], [S.B_xT[i]])

        def phase_final():
            A.reset()
            xin_ = [A.f32([8, 128]), A.f32([8, 128])]
            xo_ = [A.f32([D]), A.f32([D])]
            cnt = 0
            for S in STREAMS:
                qb = S.QB
                for i in range(S.T // qb):
                    xi = xin_[cnt % 2]; xo = xo_[cnt % 2]; cnt += 1
                    ti = (i * qb) // S.TT
                    DMA("sync", xi.ap[:, :, 0:qb], S.xT.rearrange("(c p) t -> p c t", p=128)[:, :, i * qb:(i + 1) * qb], [S.B_xT[ti]], [xi.b])
                    for half in range(2):
                        pt = PS()
                        for c4 in range(4):
                            TR(pt.ap[0:qb, c4 * 128:(c4 + 1) * 128], xi.ap[:, half * 4 + c4, 0:qb], identf, [xi], [pt])
                        CP("act" if half == 0 else "dve", xo.ap[0:qb, half * 512:(half + 1) * 512], pt.ap[0:qb, :], [pt], [xo])
                    DMA("sync", S.y_out[i * qb:(i + 1) * qb, :], xo.ap[0:qb, :], [xo.b], [])

        for l in range(DEPTH if stop_after != "XF" else 0):
            P.barrier(); A.reset()
            WIN = A.bf([8, DIN])
            DMA("pool", WIN.ap, w_in[l].rearrange("(kc p) n -> p kc n", p=128), [], [WIN.b])
            mark = A.off
            for S in STREAMS:
                A.off = mark
                phase_A(S, l, WIN)
                P.barrier()
            if stop_after is not None and stop_after.startswith("A"):
                break
            P.barrier(); A.reset()
            for S in STREAMS:
                A.reset()
                phase_B(S, l)
                P.barrier()
            if stop_after == "B":
                break
            P.barrier(); A.reset()
            tb = s5_prep(l)
            P.barrier()
            if stop_after in ("C0", "Ca", "Cb", "Cc", "Cd"):
                break
            A.off = tb["mark"]
            mark = A.off
            for S in STREAMS:
                A.off = mark
                phase_C(S, l, tb)
                P.barrier()
            if stop_after is not None and stop_after.startswith("C"):
                break
            P.barrier(); A.reset()
            WOUT = A.bf([8, D])
            DMA("pool", WOUT.ap, w_out[l].rearrange("(kc p) n -> p kc n", p=128), [], [WOUT.b])
            mark = A.off
            for S in STREAMS:
                A.off = mark
                phase_DE(S, l, WOUT)
                P.barrier()
        P.barrier()
        phase_final()


    body_fn()

    with nc.allow_low_precision("bf16 matmul operands, fp32 accumulate"):
        P.emit()
    P.close()
    return nc


_CACHE = {}


def kernel(x_prompt, x_sample, c_prompt, c_sample, cache_k, cache_v, cache_logf,
           state_ssm_re, state_ssm_im, w_ada, b_ada, g_pre_mix, g_post_mix, g_pre_ffn,
           g_post_ffn, w_in, b_forget, ssm_a_re, ssm_a_im, ssm_log_dt, ssm_b_re, ssm_b_im,
           ssm_c_re, ssm_c_im, ssm_d, w_glu, w_out, w_gate, w_up, w_down, _stop_after=None):
    f = lambda a: np.ascontiguousarray(np.asarray(a, dtype=np.float32))
    x_prompt = f(x_prompt); x_sample = f(x_sample)
    B, T, _ = x_prompt.shape
    BS, TS, _ = x_sample.shape
    DEPTH = w_in.shape[0]
    PAST = cache_k.shape[3]
    NC = 8
    key = (T, TS, PAST, DEPTH, _stop_after)
    if key not in _CACHE:
        _CACHE[key] = build(T, TS, PAST, DEPTH, _stop_after)
    nc = _CACHE[key]
    shared = dict(w_ada=f(w_ada), b_ada=f(b_ada), g_pre_mix=f(g_pre_mix), g_post_mix=f(g_post_mix),
                  g_pre_ffn=f(g_pre_ffn), g_post_ffn=f(g_post_ffn), w_in=f(w_in), b_forget=f(b_forget),
                  ssm_a_re=f(ssm_a_re), ssm_a_im=f(ssm_a_im), ssm_log_dt=f(ssm_log_dt),
                  ssm_b_re=f(ssm_b_re), ssm_b_im=f(ssm_b_im), ssm_c_re=f(ssm_c_re), ssm_c_im=f(ssm_c_im),
                  ssm_d=f(ssm_d), w_glu=f(w_glu), w_out=f(w_out), w_gate=f(w_gate), w_up=f(w_up), w_down=f(w_down))
    cache_k = f(cache_k); cache_v = f(cache_v); cache_logf = f(cache_logf)
    state_ssm_re = f(state_ssm_re); state_ssm_im = f(state_ssm_im)
    c_prompt = f(c_prompt); c_sample = f(c_sample)
    in_maps = []
    for c in range(NC):
        bp = c % B
        bs = c % BS
        m = dict(shared)
        m["xp"] = x_prompt[bp]; m["xs"] = x_sample[bs]
        m["c2"] = np.ascontiguousarray(np.stack([c_prompt[bp], c_sample[bs]], axis=0))
        m["ck"] = np.ascontiguousarray(cache_k[:, bs]); m["cv"] = np.ascontiguousarray(cache_v[:, bs])
        m["clf"] = np.ascontiguousarray(cache_logf[:, bs])
        m["sre"] = np.ascontiguousarray(state_ssm_re[:, bs]); m["sim"] = np.ascontiguousarray(state_ssm_im[:, bs])
        in_maps.append(m)
    res = run_bass_kernel_spmd(nc, in_maps, core_ids=list(range(NC)))
    R = res.results
    yp = np.stack([R[b]["yp"] for b in range(B)], axis=0)
    ys = np.stack([R[b]["ys"] for b in range(BS)], axis=0)
    st = lambda name, n: np.stack([R[b][name] for b in range(n)], axis=1)
    return (yp, ys, st("nkp", B), st("nvp", B), st("nlp", B), st("nrp", B), st("nip", B),
            st("nks", BS), st("nvs", BS), st("nls", BS), st("nrs", BS), st("nis", BS))
```

```python
import contextlib
import math
import numpy as np
import concourse.bass as bass
import concourse.mybir as mybir
from concourse.bass_utils import run_bass_kernel_spmd

F32 = mybir.dt.float32
BF16 = mybir.dt.bfloat16
AF = mybir.ActivationFunctionType
ALU = mybir.AluOpType

ENGS = ["pe", "act", "dve", "pool", "sync"]
COMPUTE = {"pe", "act", "dve", "pool"}
RING = 8
EPOCH = 20000


class Buf:
    __slots__ = ("name", "last_w", "readers", "psum")

    def __init__(self, name="", psum=False):
        self.name = name
        self.last_w = None
        self.readers = []
        self.psum = psum


class Op:
    __slots__ = ("eng", "fn", "dma", "deps", "idx", "signal", "sem", "val", "ringwait")

    def __init__(self, eng, fn, dma):
        self.eng = eng
        self.fn = fn
        self.dma = dma
        self.deps = set()
        self.signal = dma
        self.sem = None
        self.val = 0
        self.ringwait = None


class Prog:
    def __init__(self, nc):
        self.nc = nc
        self.ops = {e: [] for e in ENGS}
        self.allops = []
        self.stack = contextlib.ExitStack()
        self.pending_barrier = {e: None for e in ENGS}

    def sb(self, name, shape, dtype=F32):
        return self.stack.enter_context(self.nc.sbuf_tensor(name, list(shape), dtype))

    def ps(self, name, shape, dtype=F32):
        return self.stack.enter_context(self.nc.psum_tensor(name, list(shape), dtype))

    def buf(self, name=""):
        return Buf(name)

    def barrier(self):
        snap = set()
        for e in ENGS:
            lst = self.ops[e]
            if not lst:
                continue
            snap.add((e, len(lst) - 1))
            cnt = 0
            for i in range(len(lst) - 1, -1, -1):
                if lst[i].dma:
                    snap.add((e, i))
                    cnt += 1
                    if cnt >= RING:
                        break
                if len(lst) - i > 4 * RING + 64:
                    break
        for e in ENGS:
            self.pending_barrier[e] = snap

    def add(self, eng, fn, reads=(), writes=(), dma=False):
        op = Op(eng, fn, dma)
        lst = self.ops[eng]
        op.idx = len(lst)
        me = (eng, op.idx)
        same_ok = (eng == "pe") and not dma
        pb = self.pending_barrier[eng]
        if pb is not None:
            for d in pb:
                if d != me:
                    op.deps.add(d)
            self.pending_barrier[eng] = None
        for b in reads:
            w = b.last_w
            if w is not None and w != me:
                op.deps.add(w)
            if b.psum:
                for r in b.readers:
                    if r[0] != eng:
                        op.deps.add(r)
        for b in writes:
            w = b.last_w
            if w is not None and w != me:
                wop = self.ops[w[0]][w[1]]
                if not (same_ok and w[0] == eng and not wop.dma):
                    op.deps.add(w)
            for r in b.readers:
                if r == me:
                    continue
                rop = self.ops[r[0]][r[1]]
                if same_ok and r[0] == eng and not rop.dma:
                    continue
                op.deps.add(r)
        for b in reads:
            b.readers.append(me)
        for b in writes:
            b.last_w = me
            b.readers = []
        lst.append(op)
        self.allops.append(op)
        return op

    def emit(self):
        nc = self.nc
        for op in self.allops:
            for (e, i) in op.deps:
                self.ops[e][i].signal = True
        for e in ENGS:
            if e in COMPUTE:
                for op in reversed(self.ops[e]):
                    if not op.dma:
                        op.signal = True
                        break
        nsem = [0]

        def newsem(tag):
            nsem[0] += 1
            return self.stack.enter_context(nc.semaphore(f"s_{tag}_{nsem[0]}"))

        for e in ENGS:
            cur = None
            cnt = 0
            ring = None
            ringcnt = None
            ringprev = None
            nd = 0
            for op in self.ops[e]:
                if op.dma:
                    if ring is None or nd >= (EPOCH // 16) * RING:
                        ring = [newsem(e + "r") for _ in range(RING)]
                        ringcnt = [0] * RING
                        ringprev = [None] * RING
                        nd = 0
                    slot = nd % RING
                    if ringprev[slot] is not None:
                        op.ringwait = ringprev[slot]
                    ringcnt[slot] += 16
                    op.sem = ring[slot]
                    op.val = ringcnt[slot]
                    ringprev[slot] = (op.sem, op.val)
                    nd += 1
                elif op.signal:
                    if cur is None or cnt >= EPOCH:
                        cur = newsem(e)
                        cnt = 0
                    cnt += 1
                    op.sem = cur
                    op.val = cnt
        self.nsem = nsem[0]
        finals = {}
        for e in ENGS:
            for op in self.ops[e]:
                if op.dma:
                    k = id(op.sem)
                    if k not in finals or finals[k][1] < op.val:
                        finals[k] = (op.sem, op.val)
            if e in COMPUTE:
                for op in reversed(self.ops[e]):
                    if not op.dma:
                        finals[id(op.sem)] = (op.sem, op.val)
                        break
        prog = self

        def emit_engine(e, eng):
            known = {}
            for op in prog.ops[e]:
                waits = {}
                for (de, di) in op.deps:
                    dop = prog.ops[de][di]
                    k = id(dop.sem)
                    if k not in waits or waits[k][1] < dop.val:
                        waits[k] = (dop.sem, dop.val)
                if op.ringwait is not None:
                    k = id(op.ringwait[0])
                    if k not in waits or waits[k][1] < op.ringwait[1]:
                        waits[k] = op.ringwait
                for k, (s, v) in waits.items():
                    if known.get(k, 0) >= v:
                        continue
                    known[k] = v
                    eng.wait_ge(s, v)
                ins = op.fn(eng)
                if op.signal:
                    ins.then_inc(op.sem, 16 if op.dma else 1)
            if e == "sync":
                for k, (s, v) in finals.items():
                    if known.get(k, 0) < v:
                        eng.wait_ge(s, v)

        with nc.Block() as block:
            @block.tensor
            def _(eng):
                emit_engine("pe", eng)

            @block.scalar
            def _(eng):
                emit_engine("act", eng)

            @block.vector
            def _(eng):
                emit_engine("dve", eng)

            @block.gpsimd
            def _(eng):
                emit_engine("pool", eng)

            @block.sync
            def _(eng):
                emit_engine("sync", eng)

    def close(self):
        self.stack.close()


D = 1024
NH = 8
DH = 64
DATT = 512
DSSM = 512
DFF = 2816
DIN = 2056
NG = 32
NST = 64
EPS = 1e-6
NJ = DFF // 128
VW = 72
MAGIC = 12582912.0
TWO_PI = 2.0 * math.pi
CW1 = 6.28125
CW2 = float(np.float32(TWO_PI - 6.28125))
PI_LO = 3.1415925


class Tile:
    __slots__ = ("ap", "b")

    def __init__(self, ap, b):
        self.ap = ap
        self.b = b


def build(T, TS, PAST, DEPTH, stop_after=None):
    nc = bass.Bass("TRN2", target_bir_lowering=False)
    P = Prog(nc)

    def din(name, shape):
        return nc.dram_tensor(name, list(shape), F32, kind="ExternalInput").ap()

    def dout(name, shape):
        return nc.dram_tensor(name, list(shape), F32, kind="ExternalOutput").ap()

    def dscr(name, shape, dt=F32):
        return nc.dram_tensor(name, list(shape), dt, kind="Internal").ap()

    xp = din("xp", [T, D]); xs = din("xs", [TS, D]); c2 = din("c2", [2, D])
    ck = din("ck", [DEPTH, NH, PAST, DH]); cv = din("cv", [DEPTH, NH, PAST, DH])
    clf = din("clf", [DEPTH, NH, PAST])
    sre = din("sre", [DEPTH, NG, NST]); sim = din("sim", [DEPTH, NG, NST])
    w_ada = din("w_ada", [DEPTH, D, 6 * D]); b_ada = din("b_ada", [DEPTH, 6 * D])
    g_pre_mix = din("g_pre_mix", [DEPTH, D]); g_post_mix = din("g_post_mix", [DEPTH, D])
    g_pre_ffn = din("g_pre_ffn", [DEPTH, D]); g_post_ffn = din("g_post_ffn", [DEPTH, D])
    w_in = din("w_in", [DEPTH, D, DIN]); b_forget = din("b_forget", [DEPTH, NH])
    ssm_a_re = din("ssm_a_re", [DEPTH, NG, NST]); ssm_a_im = din("ssm_a_im", [DEPTH, NG, NST])
    ssm_log_dt = din("ssm_log_dt", [DEPTH, NG])
    ssm_b_re = din("ssm_b_re", [DEPTH, NG, NST, 16]); ssm_b_im = din("ssm_b_im", [DEPTH, NG, NST, 16])
    ssm_c_re = din("ssm_c_re", [DEPTH, NG, 16, NST]); ssm_c_im = din("ssm_c_im", [DEPTH, NG, 16, NST])
    ssm_d = din("ssm_d", [DEPTH, DSSM]); w_glu = din("w_glu", [DEPTH, DSSM, DSSM])
    w_out = din("w_out", [DEPTH, D, D]); w_gate = din("w_gate", [DEPTH, D, DFF])
    w_up = din("w_up", [DEPTH, D, DFF]); w_down = din("w_down", [DEPTH, DFF, D])

    yp = dout("yp", [T, D]); ys = dout("ys", [TS, D])
    nkp = dout("nkp", [DEPTH, NH, T, DH]); nvp = dout("nvp", [DEPTH, NH, T, DH])
    nlp = dout("nlp", [DEPTH, NH, T])
    nrp = dout("nrp", [DEPTH, NG, NST]); nip = dout("nip", [DEPTH, NG, NST])
    nks = dout("nks", [DEPTH, NH, TS, DH]); nvs = dout("nvs", [DEPTH, NH, TS, DH])
    nls = dout("nls", [DEPTH, NH, TS])
    nrs = dout("nrs", [DEPTH, NG, NST]); nis = dout("nis", [DEPTH, NG, NST])

    WG2 = dscr("WG2", [DEPTH, 128, 8, DFF], BF16)
    WU2 = dscr("WU2", [DEPTH, 128, 8, DFF], BF16)
    WD2 = dscr("WD2", [DEPTH, 128, NJ, D], BF16)
    B_WFF = [P.buf() for _ in range(DEPTH)]

    class Stream:
        pass

    def mk_stream(name, s, Tn, TT, koff, x_in, y_out, nk_o, nv_o, nl_o, nr_o, ni_o):
        S = Stream()
        S.name = name; S.s = s; S.T = Tn; S.TT = TT; S.nt = Tn // TT; S.koff = koff
        S.QB = min(128, Tn)
        S.nqg = Tn // S.QB
        S.QG = min(512, Tn)
        S.ngr = Tn // S.QG
        S.TK = koff + Tn
        S.nkt = (S.TK + 127) // 128
        S.x_in = x_in; S.y_out = y_out
        S.nk_o = nk_o; S.nv_o = nv_o; S.nl_o = nl_o; S.nr_o = nr_o; S.ni_o = ni_o
        S.xT = dscr(name + "_xT", [D, Tn]); S.B_xT = [P.buf() for _ in range(S.nt)]
        S.QT = dscr(name + "_QT", [NH, DH + 3, Tn], BF16)
        S.KT = dscr(name + "_KT", [NH, DH + 3, Tn], BF16)
        S.VX = dscr(name + "_VX", [Tn, NH, VW], BF16)
        S.B_qkv = P.buf(); S.B_kones = P.buf()
        S.uT = dscr(name + "_uT", [DSSM, Tn], BF16); S.B_uT = P.buf()
        S.soT = dscr(name + "_soT", [DSSM, Tn], BF16); S.B_soT = P.buf()
        return S

    SP = mk_stream("p", 0, T, min(512, T), 0, xp, yp, nkp, nvp, nlp, nrp, nip)
    SS = mk_stream("s", 1, TS, TS, PAST, xs, ys, nks, nvs, nls, nrs, nis)
    STREAMS = [SP, SS]

    def ptile(name, shape, dt=F32):
        h = P.sb(name, shape, dt)
        return Tile(h[tuple(slice(None) for _ in shape)], P.buf(name))

    identf = ptile("identf", [128, 128]); identb = ptile("identb", [128, 128], BF16)
    onesM = ptile("onesM", [128, 128], BF16); tri = ptile("tri", [128, 128], BF16)
    maskT = ptile("maskT", [128, 128], BF16); onesbf = ptile("onesbf", [8, 512], BF16)
    ntri = ptile("ntri", [128, 128], BF16)
    ones8 = ptile("ones8", [8, 512]); epsc = ptile("epsc", [128, 1]); one1 = ptile("one1", [128, 1])
    halfpi = ptile("halfpi", [128, 1]); trow = ptile("trow", [128, 128]); tcol = ptile("tcol", [128, 1])
    ntcol = ptile("ntcol", [128, 1]); sel8 = ptile("sel8", [8, NH, 128])
    PAR = ptile("PAR", [128, DEPTH, 2, 6, 8])
    scT = ptile("scT", [128, 8, 2], BF16)
    ATT = [ptile("ATTp", [128, SP.nqg, DATT], BF16), ptile("ATTs", [SS.QB, 1, DATT], BF16)]
    F_T = [ptile("F_Tp", [128, SP.nkt, NH]), ptile("F_Ts", [128, SS.nkt, NH])]
    CBC = [ptile("CBCp", [128, NH, SP.ngr]), ptile("CBCs", [128, NH, SS.ngr])]
    Fcs = [ptile("Fcp", [8, SP.ngr]), ptile("Fcs", [8, SS.ngr])]
    negb = ptile("negb", [8, 1])
    fcarry = ptile("fcarry", [8, 1])

    PSB = [Tile(P.ps(f"psb{i}", [128, 512])[:, :], Buf(f"psb{i}", psum=True)) for i in range(8)]

    ARW = 40960
    AR = P.sb("arena", [128, ARW])

    class Arena:
        def __init__(self):
            self.off = 0

        def reset(self):
            self.off = 0

        def get(self, nwords_f32, dt=F32, shape=None, parts=128):
            n = (nwords_f32 + 7) // 8 * 8
            assert self.off + n <= ARW, ("arena overflow", self.off, n)
            ap = AR[0:parts, self.off:self.off + nwords_f32]
            self.off += n
            if dt != F32:
                ap = ap.bitcast(dt)
            if shape is not None and len(shape) > 1:
                if len(shape) == 2:
                    ap = ap.rearrange("p (a b) -> p a b", a=shape[0])
                elif len(shape) == 3:
                    ap = ap.rearrange("p (a b c) -> p a b c", a=shape[0], b=shape[1])
                elif len(shape) == 4:
                    ap = ap.rearrange("p (a b c d) -> p a b c d", a=shape[0], b=shape[1], c=shape[2])
            return Tile(ap, P.buf())

        def f32(self, shape, parts=128):
            return self.get(int(np.prod(shape)), F32, shape, parts)

        def bf(self, shape, parts=128):
            n = int(np.prod(shape))
            assert n % 2 == 0
            return self.get(n // 2, BF16, shape, parts)

    A = Arena()

    def rb(ts):
        return [t.b for t in ts]

    def MM(out, lhsT, rhs, start, stop, reads, writes, **kw):
        P.add("pe", lambda e: e.matmul(out, lhsT=lhsT, rhs=rhs, start=start, stop=stop, **kw), rb(reads), rb(writes))

    def TR(out, in_, ident, reads, writes):
        P.add("pe", lambda e: e.transpose(out, in_, ident.ap[0:in_.shape[0], 0:in_.shape[0]]), rb(reads) + [ident.b], rb(writes))

    def ACT(out, in_, func, reads, writes, bias=None, scale=None):
        kw = {}
        if bias is not None:
            kw["bias"] = bias
        if scale is not None:
            kw["scale"] = scale
        P.add("act", lambda e: e.activation(out=out, in_=in_, func=func, **kw), rb(reads), rb(writes))

    def TT_(eng, out, in0, in1, op, reads, writes):
        P.add(eng, lambda e: e.tensor_tensor(out=out, in0=in0, in1=in1, op=op), rb(reads), rb(writes))

    def TS_(eng, out, in0, s1, s2, op0, op1, reads, writes):
        if s2 is None:
            P.add(eng, lambda e: e.tensor_scalar(out=out, in0=in0, scalar1=s1, scalar2=None, op0=op0), rb(reads), rb(writes))
        else:
            P.add(eng, lambda e: e.tensor_scalar(out=out, in0=in0, scalar1=s1, scalar2=s2, op0=op0, op1=op1), rb(reads), rb(writes))

    def STT(out, in0, scalar, in1, op0, op1, reads, writes):
        P.add("dve", lambda e: e.scalar_tensor_tensor(out=out, in0=in0, scalar=scalar, in1=in1, op0=op0, op1=op1), rb(reads), rb(writes))

    def CP(eng, out, in_, reads, writes):
        if eng == "act":
            P.add("act", lambda e: e.copy(out=out, in_=in_), rb(reads), rb(writes))
        else:
            P.add(eng, lambda e: e.tensor_copy(out=out, in_=in_), rb(reads), rb(writes))

    def MSET(eng, t, val):
        P.add(eng, lambda e: e.memset(t.ap, val), [], [t.b])

    def DMA(q, out, in_, reads, writes):
        P.add(q, lambda e: e.dma_start(out=out, in_=in_), reads, writes, dma=True)

    psrr = [0]

    def PS():
        t = PSB[psrr[0] % 8]
        psrr[0] += 1
        return t

    def body_fn():
        MSET("pool", identf, 1.0)
        P.add("pool", lambda e: e.affine_select(out=identf.ap, in_=identf.ap, pattern=[[1, 128]], compare_op=ALU.is_equal, fill=0.0, base=0, channel_multiplier=-1), [identf.b], [identf.b])
        MSET("pool", identb, 1.0)
        P.add("pool", lambda e: e.affine_select(out=identb.ap, in_=identb.ap, pattern=[[1, 128]], compare_op=ALU.is_equal, fill=0.0, base=0, channel_multiplier=-1), [identb.b], [identb.b])
        MSET("pool", tri, 1.0)
        P.add("pool", lambda e: e.affine_select(out=tri.ap, in_=tri.ap, pattern=[[1, 128]], compare_op=ALU.is_ge, fill=0.0, base=0, channel_multiplier=-1), [tri.b], [tri.b])
        MSET("pool", maskT, -30000.0)
        P.add("pool", lambda e: e.affine_select(out=maskT.ap, in_=maskT.ap, pattern=[[1, 128]], compare_op=ALU.is_gt, fill=0.0, base=0, channel_multiplier=-1), [maskT.b], [maskT.b])
        MSET("dve", onesbf, 1.0)
        for S_ in STREAMS:
            for r in range(3):
                for t0_ in range(0, S_.T, 512):
                    tw = min(512, S_.T - t0_)
                    DMA("sync", S_.KT[:, DH + r, t0_:t0_ + tw], onesbf.ap[:, 0:tw], [onesbf.b], [S_.B_kones])
        TS_("dve", ntri.ap, tri.ap, -1.0, None, ALU.mult, None, [tri], [ntri])
        MSET("dve", onesM, 1.0 / 1024.0)
        MSET("dve", ones8, 1.0)
        MSET("dve", epsc, EPS)
        MSET("dve", one1, 1.0)
        MSET("dve", halfpi, math.pi / 2.0)
        P.add("pool", lambda e: e.iota(trow.ap, pattern=[[1, 128]], base=0, channel_multiplier=0, allow_small_or_imprecise_dtypes=True), [], [trow.b])
        P.add("pool", lambda e: e.iota(tcol.ap, pattern=[[0, 1]], base=0, channel_multiplier=1, allow_small_or_imprecise_dtypes=True), [], [tcol.b])
        TS_("dve", ntcol.ap, tcol.ap, -1.0, None, ALU.mult, None, [tcol], [ntcol])
        MSET("pool", sel8, 1.0)
        P.add("pool", lambda e: e.affine_select(out=sel8.ap, in_=sel8.ap, pattern=[[-1, NH], [0, 128]], compare_op=ALU.is_equal, fill=0.0, base=0, channel_multiplier=1), [sel8.b], [sel8.b])

        if stop_after == "P0":
            return
        A.reset()
        c2n = A.f32([D], parts=2)
        DMA("sync", c2n.ap, c2, [], [c2n.b])
        pc = PS()
        for kc in range(8):
            TR(pc.ap[:, kc * 2:kc * 2 + 2], c2n.ap[:, kc * 128:(kc + 1) * 128], identf, [c2n], [pc])
        ACT(scT.ap.rearrange("p a b -> p (a b)"), pc.ap[:, 0:16], AF.Silu, [pc], [scT])
        spn = A.f32([128], parts=80)
        spT = A.f32([80])
        modT = A.f32([48, 2])
        WA = [A.bf([8, 1024]), A.bf([8, 1024])]
        wai = 0
        for l in range(DEPTH):
            DMA("sync", spn.ap[0:48, :], b_ada[l].rearrange("(a b) -> a b", b=128), [], [spn.b])
            for i, g in enumerate([g_pre_mix, g_post_mix, g_pre_ffn, g_post_ffn]):
                DMA("sync", spn.ap[48 + 8 * i:56 + 8 * i, :], g[l].rearrange("(a b) -> a b", b=128), [], [spn.b])
            pt = PS()
            TR(pt.ap[:, 0:80], spn.ap, identf, [spn], [pt])
            CP("dve", spT.ap, pt.ap[:, 0:80], [pt], [spT])
            pm = PS()
            for piece in range(6):
                wa = WA[wai % 2]; wai += 1
                DMA("pool", wa.ap, w_ada[l, :, piece * 1024:(piece + 1) * 1024].rearrange("(kc p) n -> p kc n", p=128), [], [wa.b])
                for j in range(8):
                    cj = piece * 8 + j
                    for kc in range(8):
                        MM(pm.ap[:, cj * 2:cj * 2 + 2], wa.ap[:, kc, j * 128:(j + 1) * 128], scT.ap[:, kc, :], kc == 0, kc == 7, [wa, scT], [pm])
            for s in range(2):
                TT_("dve", modT.ap[:, :, s], pm.ap[:, 0:96].rearrange("p (a b) -> p a b", b=2)[:, :, s], spT.ap[:, 0:48], ALU.add, [pm, spT], [modT])
            for s in range(2):
                par = PAR.ap[:, l, s]
                STT(par[:, 0, :], modT.ap[:, 8:16, s], 1.0, spT.ap[:, 48:56], ALU.add, ALU.mult, [modT, spT], [PAR])
                CP("dve", par[:, 1, :], modT.ap[:, 0:8, s], [modT], [PAR])
                TT_("dve", par[:, 2, :], modT.ap[:, 16:24, s], spT.ap[:, 56:64], ALU.mult, [modT, spT], [PAR])
                STT(par[:, 3, :], modT.ap[:, 32:40, s], 1.0, spT.ap[:, 64:72], ALU.add, ALU.mult, [modT, spT], [PAR])
                CP("dve", par[:, 4, :], modT.ap[:, 24:32, s], [modT], [PAR])
                TT_("dve", par[:, 5, :], modT.ap[:, 40:48, s], spT.ap[:, 72:80], ALU.mult, [modT, spT], [PAR])

        if stop_after == "P1":
            return
        for l in range(DEPTH):
            DMA("pool", WG2[l], w_gate[l].rearrange("(kc p) n -> p kc n", p=128), [], [B_WFF[l]])
            DMA("pool", WU2[l], w_up[l].rearrange("(kc p) n -> p kc n", p=128), [], [B_WFF[l]])
            DMA("pool", WD2[l], w_down[l].rearrange("(j p) n -> p j n", p=128), [], [B_WFF[l]])

        if stop_after == "P2":
            return
        P.barrier()
        A.reset()
        xin = [A.f32([D]), A.f32([D])]
        xtr = [A.f32([8, 128]), A.f32([8, 128])]
        cnt = 0
        for S in STREAMS:
            nb = S.T // S.QB
            for i in range(nb):
                xi = xin[cnt % 2]; xo = xtr[cnt % 2]; cnt += 1
                qb = S.QB
                DMA("sync", xi.ap[0:qb, :], S.x_in[i * qb:(i + 1) * qb, :], [], [xi.b])
                for half in range(2):
                    pt = PS()
                    for c4 in range(4):
                        c = half * 4 + c4
                        TR(pt.ap[:, c4 * 128:c4 * 128 + qb], xi.ap[0:qb, c * 128:(c + 1) * 128], identf, [xi], [pt])
                    CP("act" if half == 0 else "dve", xo.ap[:, half * 4:half * 4 + 4, 0:qb], pt.ap.rearrange("p (a b) -> p a b", b=128)[:, :, 0:qb], [pt], [xo])
                ti = (i * qb) // S.TT
                DMA("sync", S.xT.rearrange("(c p) t -> p c t", p=128)[:, :, i * qb:(i + 1) * qb], xo.ap[:, :, 0:qb], [xo.b], [S.B_xT[ti]])

        if stop_after == "X":
            return
        def rmsnorm_mod(S, l, xt, which, hm, tmp2, sq, rstd):
            TT = S.TT
            ACT(sq.ap, xt.ap, AF.Square, [xt], [sq])
            pss = PS()
            for c in range(8):
                MM(pss.ap[:, 0:TT], onesM.ap, sq.ap[:, c, :], c == 0, c == 7, [onesM, sq], [pss])
            ACT(rstd.ap, pss.ap[:, 0:TT], AF.Sqrt, [pss, epsc], [rstd], bias=epsc.ap, scale=1.0)
            P.add("dve", lambda e: e.reciprocal(out=rstd.ap, in_=rstd.ap), [rstd.b], [rstd.b])
            for c in range(8):
                tm = tmp2[c % 2]
                STT(tm.ap, xt.ap[:, c, :], PAR.ap[:, l, S.s, which, c:c + 1], rstd.ap, ALU.mult, ALU.mult, [xt, PAR, rstd], [tm])
                ACT(hm.ap[:, c, :], tm.ap, AF.Identity, [tm, PAR], [hm], bias=PAR.ap[:, l, S.s, which + 1, c:c + 1], scale=1.0)

        def phase_A(S, l, WIN):
            TT = S.TT; QB = S.QB; nsub = TT // QB
            s = S.s
            xts = [A.f32([8, TT]), A.f32([8, TT])]
            sq = A.bf([8, TT]); rstd = A.f32([TT])
            tmp2 = [A.f32([TT]), A.f32([TT])]
            hms = [A.bf([8, TT]), A.bf([8, TT])]
            qTa = A.bf([NH, TT], parts=64); kTa = A.bf([NH, TT], parts=64)
            ktm = [A.f32([DATT]), A.f32([DATT])]; vtm = [A.f32([DATT]), A.f32([DATT])]
            vx = [A.bf([NH, VW]), A.bf([NH, VW])]
            uTa = A.bf([4, TT])
            gT = A.f32([TT], parts=8); lf = A.f32([TT], parts=8); Fp = A.f32([TT], parts=8)
            Dq = A.f32([TT], parts=8); Dsp = [A.bf([TT], parts=8) for _ in range(3)]
            lfn = 0
            if S.koff > 0:
                MSET("dve", F_T[s], 0.0)
                npast = S.koff
                clt = A.f32([npast], parts=8)
                Fpast = A.f32([npast], parts=8)
                DMA("sync", clt.ap, clf[l], [], [clt.b])
                MSET("dve", fcarry, 0.0)
                cs_ = min(512, npast)
                for i0 in range(0, npast, cs_):
                    P.add("dve", lambda e, i0=i0: e.tensor_tensor_scan(out=Fpast.ap[:, i0:i0 + cs_], data0=ones8.ap[:, 0:cs_], data1=clt.ap[:, i0:i0 + cs_], initial=fcarry.ap, op0=ALU.mult, op1=ALU.add), [ones8.b, clt.b, fcarry.b], [Fpast.b])
                    CP("dve", fcarry.ap, Fpast.ap[:, i0 + cs_ - 1:i0 + cs_], [Fpast], [fcarry])
                pf = PS()
                for kt in range(npast // 128):
                    TR(pf.ap[:, kt * 8:kt * 8 + 8], Fpast.ap[:, kt * 128:(kt + 1) * 128], identf, [Fpast], [pf])
                CP("dve", F_T[s].ap[:, 0:npast // 128, :], pf.ap[:, 0:(npast // 128) * 8].rearrange("p (a b) -> p a b", b=8), [pf], [F_T[s]])
            else:
                MSET("dve", fcarry, 0.0)
            DMA("sync", negb.ap, b_forget[l].rearrange("(a b) -> a b", b=1), [], [negb.b])
            TS_("dve", negb.ap, negb.ap, -1.0, None, ALU.mult, None, [negb], [negb])
            def load_norm(i):
                xt_ = xts[i % 2]; hm_ = hms[i % 2]
                DMA("sync", xt_.ap, S.xT.rearrange("(c p) t -> p c t", p=128)[:, :, i * TT:(i + 1) * TT], [S.B_xT[i]], [xt_.b])
                rmsnorm_mod(S, l, xt_, 0, hm_, tmp2, sq, rstd)

            load_norm(0)
            for i in range(S.nt):
                xt = xts[i % 2]; hm = hms[i % 2]
                t0 = i * TT
                if i + 1 < S.nt:
                    load_norm(i + 1)
                if stop_after == "A1":
                    return
                for h in range(NH):
                    pq = PS()
                    for c in range(8):
                        MM(pq.ap[0:64, 0:TT], WIN.ap[:, c, h * 64:(h + 1) * 64], hm.ap[:, c, :], c == 0, c == 7, [WIN, hm], [pq])
                    ACT(qTa.ap[:, h, :], pq.ap[0:64, 0:TT], AF.Identity, [pq], [qTa], scale=0.125)
                    pk = PS()
                    for c in range(8):
                        MM(pk.ap[0:64, 0:TT], WIN.ap[:, c, DATT + h * 64:DATT + (h + 1) * 64], hm.ap[:, c, :], c == 0, c == 7, [WIN, hm], [pk])
                    CP("dve", kTa.ap[:, h, :], pk.ap[0:64, 0:TT], [pk], [kTa])
                DMA("sync", S.QT.rearrange("h d t -> d h t")[0:DH, :, t0:t0 + TT], qTa.ap, [qTa.b], [S.B_qkv])
                DMA("sync", S.KT.rearrange("h d t -> d h t")[0:DH, :, t0:t0 + TT], kTa.ap, [kTa.b], [S.B_qkv])
                if stop_after == "A2":
                    return
                for j in range(nsub):
                    tb0 = t0 + j * QB
                    kt_ = ktm[j % 2]; vt_ = vtm[j % 2]; vx_ = vx[j % 2]
                    pk = PS()
                    for c in range(8):
                        MM(pk.ap[0:QB, :], hm.ap[:, c, j * QB:(j + 1) * QB], WIN.ap[:, c, DATT:2 * DATT], c == 0, c == 7, [WIN, hm], [pk])
                    CP("act", kt_.ap[0:QB, :], pk.ap[0:QB, :], [pk], [kt_])
                    if stop_after == "A2a":
                        return
                    DMA("sync", S.nk_o[l].rearrange("h t d -> t h d")[tb0:tb0 + QB], kt_.ap[0:QB, :].rearrange("p (h d) -> p h d", d=DH), [kt_.b], [])
                    if stop_after == "A2b":
                        return
                    pv = PS()
                    for c in range(8):
                        MM(pv.ap[0:QB, :], hm.ap[:, c, j * QB:(j + 1) * QB], WIN.ap[:, c, 2 * DATT:3 * DATT], c == 0, c == 7, [WIN, hm], [pv])
                    CP("act", vt_.ap[0:QB, :], pv.ap[0:QB, :], [pv], [vt_])
                    DMA("sync", S.nv_o[l].rearrange("h t d -> t h d")[tb0:tb0 + QB], vt_.ap[0:QB, :].rearrange("p (h d) -> p h d", d=DH), [vt_.b], [])
                    if stop_after == "A2c":
                        return
                    P.add("pool", lambda e, vx_=vx_: e.memset(vx_.ap[0:QB].rearrange("p h d -> p (h d)"), 1.0), [], [vx_.b])
                    CP("dve", vx_.ap[0:QB, :, 0:DH], vt_.ap[0:QB, :].rearrange("p (h d) -> p h d", d=DH), [vt_], [vx_])
                    if stop_after == "A2d":
                        return
                    DMA("sync", S.VX[tb0:tb0 + QB], vx_.ap[0:QB], [vx_.b], [S.B_qkv])
                if stop_after == "A3":
                    return
                for m in range(4):
                    pu = PS()
                    for c in range(8):
                        MM(pu.ap[:, 0:TT], WIN.ap[:, c, 3 * DATT + NH + m * 128:3 * DATT + NH + (m + 1) * 128], hm.ap[:, c, :], c == 0, c == 7, [WIN, hm], [pu])
                    CP("act" if m % 2 else "dve", uTa.ap[:, m, :], pu.ap[:, 0:TT], [pu], [uTa])
                DMA("sync", S.uT.rearrange("(c p) t -> p c t", p=128)[:, :, t0:t0 + TT], uTa.ap, [uTa.b], [S.B_uT])
                if stop_after == "A4":
                    return
                pg = PS()
                for c in range(8):
                    MM(pg.ap[0:8, 0:TT], WIN.ap[:, c, 3 * DATT:3 * DATT + NH], hm.ap[:, c, :], c == 0, c == 7, [WIN, hm], [pg])
                ACT(gT.ap, pg.ap[0:8, 0:TT], AF.Exp, [pg, negb], [gT], bias=negb.ap, scale=-1.0)
                ACT(gT.ap, gT.ap, AF.Ln, [gT, one1], [gT], bias=one1.ap[0:8], scale=1.0)
                TS_("dve", lf.ap, gT.ap, -1.0, None, ALU.mult, None, [gT], [lf])
                DMA("sync", S.nl_o[l][:, t0:t0 + TT], lf.ap, [lf.b], [])
                P.add("dve", lambda e: e.tensor_tensor_scan(out=Fp.ap, data0=ones8.ap[:, 0:TT], data1=lf.ap, initial=fcarry.ap, op0=ALU.mult, op1=ALU.add), [ones8.b, lf.b, fcarry.b], [Fp.b])
                CP("dve", fcarry.ap, Fp.ap[:, TT - 1:TT], [Fp], [fcarry])
                if stop_after == "A5":
                    return
                assert S.TT == S.QG
                CP("dve", Fcs[s].ap[:, i:i + 1], Fp.ap[:, 0:1], [Fp], [Fcs[s]])
                TS_("dve", Dq.ap, Fp.ap, Fcs[s].ap[:, i:i + 1], None, ALU.subtract, None, [Fp, Fcs[s]], [Dq])
                for r in range(3):
                    CP("dve", Dsp[r].ap, Dq.ap, [Dq], [Dsp[r]])
                    if r < 2:
                        TT_("dve", Dq.ap, Dq.ap, Dsp[r].ap, ALU.subtract, [Dq, Dsp[r]], [Dq])
                    DMA("sync", S.QT[:, DH + r, t0:t0 + TT], Dsp[r].ap, [Dsp[r].b], [S.B_qkv])
                pf = PS()
                for j in range(nsub):
                    TR(pf.ap[0:QB, j * 8:j * 8 + 8], Fp.ap[:, j * QB:(j + 1) * QB], identf, [Fp], [pf])
                kt0 = (S.koff + t0) // 128
                CP("dve", F_T[s].ap[0:QB, kt0:kt0 + nsub, :], pf.ap[0:QB, 0:nsub * 8].rearrange("p (a b) -> p a b", b=8), [pf], [F_T[s]])
            pcb = PS()
            for h in range(NH):
                MM(pcb.ap[:, h * S.ngr:(h + 1) * S.ngr], sel8.ap[:, h, :], Fcs[s].ap, True, True, [sel8, Fcs[s]], [pcb])
            CP("dve", CBC[s].ap.rearrange("p a b -> p (a b)"), pcb.ap[:, 0:NH * S.ngr], [pcb], [CBC[s]])

        def phase_B(S, l):
            s = S.s; QB = S.QB; QG = S.QG; nsg = QG // QB; nkt = S.nkt; koff = S.koff; Tn = S.T; TK = S.TK
            npast_t = koff // 128
            KA = DH + 3
            QTh = [A.bf([Tn], parts=KA), A.bf([Tn], parts=KA)]
            KTh = [A.bf([TK], parts=KA), A.bf([TK], parts=KA)]
            VXh = [A.bf([nkt, VW]), A.bf([nkt, VW])]
            NPT = 4
            pTs = [A.bf([QG]) for _ in range(NPT)]
            biasg = [A.f32([nkt]), A.f32([nkt])]
            rsum = [A.f32([nsg]), A.f32([nsg])]
            if koff > 0:
                ckt = [A.f32([npast_t, DH]), A.f32([npast_t, DH])]
            LA = 2
            cnt = {"pt": 0, "bg": 0, "po": 0, "ps": 0}
            for h in range(NH):
                qt = QTh[h % 2]; kt = KTh[h % 2]; vxh = VXh[h % 2]
                DMA("sync", qt.ap, S.QT[h], [S.B_qkv], [qt.b])
                DMA("sync", kt.ap[:, koff:koff + Tn], S.KT[h], [S.B_qkv, S.B_kones], [kt.b])
                if koff > 0:
                    P.add("dve", lambda e, kt=kt: e.memset(kt.ap[DH:DH + 3, 0:koff], 1.0), [], [kt.b])
                ntile_new = (Tn + 127) // 128
                if Tn >= 128:
                    DMA("sync", vxh.ap[:, npast_t:npast_t + ntile_new, :], S.VX.rearrange("(a p) h d -> p a h d", p=128)[:, :, h, :], [S.B_qkv], [vxh.b])
                else:
                    DMA("sync", vxh.ap[0:Tn, npast_t, :], S.VX[:, h, :], [S.B_qkv], [vxh.b])
                if koff > 0:
                    ck_ = ckt[h % 2]
                    DMA("sync", ck_.ap, ck[l, h].rearrange("(a p) d -> p a d", p=128), [], [ck_.b])
                    for a0 in range(0, npast_t, 4):
                        pt = PSB[cnt["ps"] % 4]; cnt["ps"] += 1
                        for a in range(a0, min(a0 + 4, npast_t)):
                            TR(pt.ap[0:64, (a - a0) * 128:(a - a0 + 1) * 128], ck_.ap[:, a, :], identf, [ck_], [pt])
                        na = min(4, npast_t - a0)
                        CP("dve", kt.ap[0:DH, a0 * 128:(a0 + na) * 128], pt.ap[0:64, 0:na * 128], [pt], [kt])
                    P.add("pool", lambda e, vxh=vxh: e.memset(vxh.ap[:, 0:npast_t, :].rearrange("p a d -> p (a d)"), 1.0), [], [vxh.b])
                    DMA("pool", vxh.ap[:, 0:npast_t, 0:DH], cv[l, h].rearrange("(a p) d -> p a d", p=128), [vxh.b], [vxh.b])
                for G in range(S.ngr):
                    q0 = G * QG
                    last_kt = (koff + q0 + QG - 1) // 128
                    first_diag = (koff + q0) // 128
                    bg = biasg[cnt["bg"] % 2]; cnt["bg"] += 1
                    TS_("dve", bg.ap[:, 0:last_kt + 1], F_T[s].ap[:, 0:last_kt + 1, h], -1.0, CBC[s].ap[:, h, G:G + 1], ALU.mult, ALU.add, [F_T[s], CBC[s]], [bg])
                    po = PSB[6 + (cnt["po"] % 2)]; cnt["po"] += 1
                    pov = po.ap[:, 0:nsg * 128].rearrange("p (i d) -> p i d", d=128)
                    blocks = []
                    for k_ in range(last_kt + 1):
                        kp = min(128, TK - k_ * 128)
                        j = max(0, k_ - first_diag)
                        blocks.append((k_, kp, j))

                    def score(bi):
                        k_, kp, j = blocks[bi]
                        psc = PSB[cnt["ps"] % 4]; cnt["ps"] += 1
                        c0 = j * QB
                        diag = k_ >= first_diag
                        MM(psc.ap[0:kp, c0:QG], kt.ap[:, k_ * 128:k_ * 128 + kp], qt.ap[:, q0 + c0:q0 + QG], True, not diag, [kt, qt], [psc])
                        if diag:
                            MM(psc.ap[0:kp, c0:c0 + QB], maskT.ap[0:QB, 0:kp], identb.ap[0:QB, 0:QB], False, True, [maskT, identb], [psc])
                        pT = pTs[cnt["pt"] % NPT]; cnt["pt"] += 1
                        ACT(pT.ap[0:kp, c0:QG], psc.ap[0:kp, c0:QG], AF.Exp, [psc, bg], [pT], bias=bg.ap[0:kp, k_:k_ + 1], scale=1.0)
                        return pT

                    def pv(bi, pT):
                        k_, kp, j = blocks[bi]
                        for i in range(j, nsg):
                            first = (bi == 0 and i == j)
                            last = (bi == len(blocks) - 1 and i == nsg - 1)
                            MM(pov[0:QB, i, 0:DH + 1], pT.ap[0:kp, i * QB:(i + 1) * QB], vxh.ap[0:kp, k_, 0:DH + 1], first, last, [pT, vxh], [po])

                    pend = []
                    nb = len(blocks)
                    for bi in range(nb + LA):
                        if bi < nb:
                            pend.append((bi, score(bi)))
                        if bi >= LA:
                            b2, pT2 = pend.pop(0)
                            pv(b2, pT2)
                    rs = rsum[G % 2]
                    P.add("dve", lambda e, rs=rs, pov=pov: e.reciprocal(out=rs.ap[0:QB, :], in_=pov[0:QB, :, DH]), [po.b], [rs.b])
                    TT_("dve", ATT[s].ap[0:QB, G * nsg:(G + 1) * nsg, h * DH:(h + 1) * DH], pov[0:QB, :, 0:DH], rs.ap[0:QB, :].unsqueeze(2).to_broadcast([QB, nsg, DH]), ALU.mult, [po, rs], [ATT[s]])

        def s5_prep(l):
            tb = {}
            Ainv = A.f32([16, 2, 128]); ApT = A.f32([16, 2, 128]); T2 = A.bf([4, 2, 128]); CT = A.bf([4, 2, 128])
            Dd = A.bf([4, 128]); a1 = A.f32([16, 2]); WGLU = A.bf([4, DSSM]); T2f = A.bf([4, 4, 2, 128]); CTn = A.bf([4, 128])
            tb.update(Ainv=Ainv, ApT=ApT, T2=T2, CT=CT, Dd=Dd, a1=a1, WGLU=WGLU, T2f=T2f, CTn=CTn)
            mark = A.off
            tb["mark"] = mark
            are_n = A.f32([128], parts=16); aim_n = A.f32([128], parts=16); ldt = A.f32([2], parts=16)
            DMA("sync", are_n.ap, ssm_a_re[l].rearrange("(a b) p -> a (b p)", b=2), [], [are_n.b])
            DMA("sync", aim_n.ap, ssm_a_im[l].rearrange("(a b) p -> a (b p)", b=2), [], [aim_n.b])
            DMA("sync", ldt.ap, ssm_log_dt[l].rearrange("(a b) -> a b", b=2), [], [ldt.b])
            ACT(ldt.ap, ldt.ap, AF.Exp, [ldt], [ldt])
            al_n = A.f32([128], parts=16); th_n = A.f32([128], parts=16)
            dtb = ldt.ap.unsqueeze(2).to_broadcast([16, 2, 64])
            TT_("dve", al_n.ap.rearrange("p (a b) -> p a b", a=2), are_n.ap.rearrange("p (a b) -> p a b", a=2), dtb, ALU.mult, [are_n, ldt], [al_n])
            TT_("dve", th_n.ap.rearrange("p (a b) -> p a b", a=2), aim_n.ap.rearrange("p (a b) -> p a b", a=2), dtb, ALU.mult, [aim_n, ldt], [th_n])
            sm = A.f32([4, 16])
            pt = PS()
            for i, src in enumerate([al_n, th_n, are_n, aim_n]):
                TR(pt.ap[:, i * 16:(i + 1) * 16], src.ap, identf, [src], [pt])
            CP("dve", sm.ap.rearrange("p a b -> p (a b)"), pt.ap[:, 0:64], [pt], [sm])

            def sincos(ang, shape_elems, cosv, sinv, tmpk):
                TS_("dve", tmpk.ap, ang.ap, 1.0 / TWO_PI, MAGIC, ALU.mult, ALU.add, [ang], [tmpk])
                TS_("dve", tmpk.ap, tmpk.ap, -MAGIC, None, ALU.add, None, [tmpk], [tmpk])
                STT(ang.ap, tmpk.ap, -CW1, ang.ap, ALU.mult, ALU.add, [tmpk, ang], [ang])
                STT(ang.ap, tmpk.ap, -CW2, ang.ap, ALU.mult, ALU.add, [tmpk, ang], [ang])
                TS_("dve", ang.ap, ang.ap, PI_LO, -PI_LO, ALU.min, ALU.max, [ang], [ang])
                ACT(sinv.ap, ang.ap, AF.Sin, [ang], [sinv])
                ACT(tmpk.ap, ang.ap, AF.Abs, [ang], [tmpk])
                ACT(cosv.ap, tmpk.ap, AF.Sin, [tmpk, halfpi], [cosv], bias=halfpi.ap[0:cosv.ap.shape[0]], scale=-1.0)

            angS = A.f32([16, 128]); kS = A.f32([16, 128]); cS = A.f32([16, 128]); sS = A.f32([16, 128]); eS = A.f32([16, 128])
            trb = trow.ap.unsqueeze(1).to_broadcast([128, 16, 128])
            TT_("dve", angS.ap, sm.ap[:, 1, :].unsqueeze(2).to_broadcast([128, 16, 128]), trb, ALU.mult, [sm, trow], [angS])
            TT_("dve", eS.ap, sm.ap[:, 0, :].unsqueeze(2).to_broadcast([128, 16, 128]), trb, ALU.mult, [sm, trow], [eS])
            ACT(eS.ap, eS.ap, AF.Exp, [eS], [eS])
            sincos(angS, 2048, cS, sS, kS)
            TT_("dve", ApT.ap[:, :, 0, :], eS.ap, cS.ap, ALU.mult, [eS, cS], [ApT])
            TT_("dve", ApT.ap[:, :, 1, :], eS.ap, sS.ap, ALU.mult, [eS, sS], [ApT])
            CP("dve", a1.ap, ApT.ap[:, :, :, 1], [ApT], [a1])
            if stop_after == "Ca":
                return tb
            cf = A.f32([8, 16])
            are = sm.ap[:, 2, :]; aim = sm.ap[:, 3, :]
            TS_("dve", cf.ap[:, 0, :], a1.ap[:, :, 0], -1.0, None, ALU.add, None, [a1], [cf])
            TT_("dve", cf.ap[:, 1, :], are, are, ALU.mult, [sm], [cf])
            TT_("dve", cf.ap[:, 2, :], aim, aim, ALU.mult, [sm], [cf])
            TT_("dve", cf.ap[:, 1, :], cf.ap[:, 1, :], cf.ap[:, 2, :], ALU.add, [cf], [cf])
            P.add("dve", lambda e: e.reciprocal(out=cf.ap[:, 1, :], in_=cf.ap[:, 1, :]), [cf.b], [cf.b])
            TT_("dve", cf.ap[:, 2, :], cf.ap[:, 0, :], are, ALU.mult, [cf, sm], [cf])
            TT_("dve", cf.ap[:, 3, :], a1.ap[:, :, 1], aim, ALU.mult, [a1, sm], [cf])
            TT_("dve", cf.ap[:, 2, :], cf.ap[:, 2, :], cf.ap[:, 3, :], ALU.add, [cf], [cf])
            TT_("dve", cf.ap[:, 4, :], cf.ap[:, 2, :], cf.ap[:, 1, :], ALU.mult, [cf], [cf])
            TT_("dve", cf.ap[:, 2, :], a1.ap[:, :, 1], are, ALU.mult, [a1, sm], [cf])
            TT_("dve", cf.ap[:, 3, :], cf.ap[:, 0, :], aim, ALU.mult, [cf, sm], [cf])
            TT_("dve", cf.ap[:, 2, :], cf.ap[:, 2, :], cf.ap[:, 3, :], ALU.subtract, [cf], [cf])
            TT_("dve", cf.ap[:, 5, :], cf.ap[:, 2, :], cf.ap[:, 1, :], ALU.mult, [cf], [cf])
            if stop_after == "Cb":
                return tb
            bre = A.f32([16, 16]); bim = A.f32([16, 16]); Z = A.f32([16, 2, 2, 16]); tq = A.f32([16, 16])
            for g2 in range(2):
                DMA("sync", bre.ap[g2 * 64:(g2 + 1) * 64], ssm_b_re[l].rearrange("(a b) p m -> b p a m", b=2)[g2], [], [bre.b])
                DMA("sync", bim.ap[g2 * 64:(g2 + 1) * 64], ssm_b_im[l].rearrange("(a b) p m -> b p a m", b=2)[g2], [], [bim.b])
            MSET("pool", Z, 0.0)
            cre_b = cf.ap[:, 4, :].unsqueeze(2).to_broadcast([128, 16, 16])
            cim_b = cf.ap[:, 5, :].unsqueeze(2).to_broadcast([128, 16, 16])
            for g2 in range(2):
                ps_ = slice(g2 * 64, (g2 + 1) * 64)
                TT_("dve", tq.ap[ps_], bim.ap[ps_], cim_b[ps_], ALU.mult, [bim, cf], [tq])
                TT_("dve", Z.ap[ps_, :, 0, g2, :], bre.ap[ps_], cre_b[ps_], ALU.mult, [bre, cf], [Z])
                TT_("dve", Z.ap[ps_, :, 0, g2, :], Z.ap[ps_, :, 0, g2, :], tq.ap[ps_], ALU.subtract, [Z, tq], [Z])
                TT_("dve", tq.ap[ps_], bre.ap[ps_], cim_b[ps_], ALU.mult, [bre, cf], [tq])
                TT_("dve", Z.ap[ps_, :, 1, g2, :], bim.ap[ps_], cre_b[ps_], ALU.mult, [bim, cf], [Z])
                TT_("dve", Z.ap[ps_, :, 1, g2, :], Z.ap[ps_, :, 1, g2, :], tq.ap[ps_], ALU.add, [Z, tq], [Z])
            for ch in range(4):
                for ri in range(2):
                    pt = PS()
                    zin = A.f32([128])
                    CP("pool", zin.ap.rearrange("p (a b c) -> p a b c", a=4, b=2), Z.ap[:, ch * 4:(ch + 1) * 4, ri, :, :], [Z], [zin])
                    TR(pt.ap[:, 0:128], zin.ap, identf, [zin], [pt])
                    CP("dve", T2.ap[:, ch, ri, :], pt.ap[:, 0:128], [pt], [T2])
            MSET("pool", T2f, 0.0)
            for i4 in range(4):
                CP("pool", T2f.ap[i4 * 32:(i4 + 1) * 32, :, i4, :, :], T2.ap[i4 * 32:(i4 + 1) * 32, :, :, :], [T2], [T2f])
            if stop_after == "Cc":
                return tb
            Zc = A.f32([4, 2, 128])
            MSET("pool", Zc, 0.0)
            for ri, csrc in enumerate([ssm_c_re, ssm_c_im]):
                cview = csrc[l].rearrange("(ch i b) m p -> i b m ch p", i=4, b=2)
                for i4 in range(4):
                    for g2 in range(2):
                        r0 = i4 * 32 + g2 * 16
                        DMA("sync", Zc.ap[r0:r0 + 16, :, ri, g2 * 64:(g2 + 1) * 64], cview[i4, g2], [], [Zc.b])
            for ch in range(4):
                for ri in range(2):
                    pt = PS()
                    TR(pt.ap[:, 0:128], Zc.ap[:, ch, ri, :], identf, [Zc], [pt])
                    if ri == 0:
                        CP("dve", CT.ap[:, ch, ri, :], pt.ap[:, 0:128], [pt], [CT])
                        TS_("dve", CTn.ap[:, ch, :], pt.ap[:, 0:128], -1.0, None, ALU.mult, None, [pt], [CTn])
                    else:
                        TS_("dve", CT.ap[:, ch, ri, :], pt.ap[:, 0:128], -1.0, None, ALU.mult, None, [pt], [CT])
            dcol = A.f32([4]); dnat = A.f32([128], parts=4)
            DMA("sync", dnat.ap, ssm_d[l].rearrange("(c p) -> c p", p=128), [], [dnat.b])
            pt = PS()
            TR(pt.ap[:, 0:4], dnat.ap, identf, [dnat], [pt])
            CP("dve", dcol.ap, pt.ap[:, 0:4], [pt], [dcol])
            for ch in range(4):
                TS_("dve", Dd.ap[:, ch, :], identf.ap, dcol.ap[:, ch:ch + 1], None, ALU.mult, None, [identf, dcol], [Dd])
            if stop_after == "Cd":
                return tb
            arb = A.f32([2048]); aib = A.f32([2048]); ldb = A.f32([32])
            DMA("sync", arb.ap, ssm_a_re[l].rearrange("g p -> (g p)").partition_broadcast(128), [], [arb.b])
            DMA("sync", aib.ap, ssm_a_im[l].rearrange("g p -> (g p)").partition_broadcast(128), [], [aib.b])
            DMA("sync", ldb.ap, ssm_log_dt[l].partition_broadcast(128), [], [ldb.b])
            ACT(ldb.ap, ldb.ap, AF.Exp, [ldb], [ldb])
            dtbb = ldb.ap.unsqueeze(2).to_broadcast([128, 32, 64])
            TT_("dve", arb.ap.rearrange("p (g q) -> p g q", q=64), arb.ap.rearrange("p (g q) -> p g q", q=64), dtbb, ALU.mult, [arb, ldb], [arb])
            TT_("dve", aib.ap.rearrange("p (g q) -> p g q", q=64), aib.ap.rearrange("p (g q) -> p g q", q=64), dtbb, ALU.mult, [aib, ldb], [aib])
            angT = angS; kT_ = kS; cT_ = cS; sT_ = sS; eT = eS
            fl = lambda t: t.ap.rearrange("p a b -> p (a b)")
            TS_("dve", fl(angT), aib.ap, tcol.ap, None, ALU.mult, None, [aib, tcol], [angT])
            ACT(fl(eT), arb.ap, AF.Exp, [arb, ntcol], [eT], scale=ntcol.ap)
            sincos(angT, 2048, cT_, sT_, kT_)
            TT_("dve", Ainv.ap[:, :, 0, :], eT.ap, cT_.ap, ALU.mult, [eT, cT_], [Ainv])
            STT(Ainv.ap[:, :, 1, :], eT.ap, -1.0, sT_.ap, ALU.mult, ALU.mult, [eT, sT_], [Ainv])
            DMA("pool", WGLU.ap, w_glu[l].rearrange("(kc p) n -> p kc n", p=128), [], [WGLU.b])
            tb["mark"] = mark
            return tb

        def phase_C(S, l, tb, hc_init_from=None):
            s = S.s; QB = S.QB; nblk = S.T // QB
            Ainv = tb["Ainv"]; ApT = tb["ApT"]; T2f = tb["T2f"]; CT = tb["CT"]; Dd = tb["Dd"]; a1 = tb["a1"]; WGLU = tb["WGLU"]; CTn = tb["CTn"]
            uTb = [A.bf([4, QB]), A.bf([4, QB])]
            bus = [A.f32([16, 2, 128]), A.f32([16, 2, 128])]
            cus = A.f32([16, 2, 128])
            t1 = A.bf([16, 128]); t2 = A.bf([16, 128]); t3 = A.bf([16, 128]); t4 = A.bf([16, 128])
            w1 = A.bf([16, 128]); w2 = A.bf([16, 128]); w3 = A.bf([16, 128]); w4 = A.bf([16, 128])
            hl = A.f32([16, 2]); hc = A.f32([16, 2]); hq = A.f32([16, 2])
            y2 = A.f32([DSSM]); yin = A.f32([DSSM]); zt = A.bf([DSSM]); zT = A.bf([4, 128]); sg = A.f32([4, 128]); soT = A.bf([4, 128])
            hn = A.f32([128], parts=16)

            def carry_from_hl():
                TT_("dve", hq.ap[:, :, 0], a1.ap[:, :, 0], hl.ap[:, :, 0], ALU.mult, [a1, hl], [hq])
                TT_("dve", hq.ap[:, :, 1], a1.ap[:, :, 1], hl.ap[:, :, 1], ALU.mult, [a1, hl], [hq])
                TT_("dve", hc.ap[:, :, 0], hq.ap[:, :, 0], hq.ap[:, :, 1], ALU.subtract, [hq], [hc])
                TT_("dve", hq.ap[:, :, 0], a1.ap[:, :, 0], hl.ap[:, :, 1], ALU.mult, [a1, hl], [hq])
                TT_("dve", hq.ap[:, :, 1], a1.ap[:, :, 1], hl.ap[:, :, 0], ALU.mult, [a1, hl], [hq])
                TT_("dve", hc.ap[:, :, 1], hq.ap[:, :, 0], hq.ap[:, :, 1], ALU.add, [hq], [hc])

            if S.koff > 0:
                for ri, src in enumerate([sre, sim]):
                    DMA("sync", hn.ap, src[l].rearrange("(a b) p -> a (b p)", b=2), [], [hn.b])
                    pt = PSB[5]
                    TR(pt.ap[:, 0:16], hn.ap, identf, [hn], [pt])
                    CP("dve", hl.ap[:, :, ri], pt.ap[:, 0:16], [pt], [hl])
                carry_from_hl()
            else:
                MSET("dve", hc, 0.0)

            def stage1(blk):
                t0 = blk * QB
                ut = uTb[blk % 2]; bu = bus[blk % 2]
                DMA("sync", ut.ap, S.uT.rearrange("(c p) t -> p c t", p=128)[:, :, t0:t0 + QB], [S.B_uT], [ut.b])
                for ch in range(4):
                    pb = [PSB[(ch % 2) * 2], PSB[(ch % 2) * 2 + 1]]
                    for hf_ in range(2):
                        MM(pb[hf_].ap[0:QB, :], ut.ap[:, ch, :], T2f.ap[:, ch, hf_ * 2:hf_ * 2 + 2, :, :].rearrange("p i r q -> p (i r q)"), True, True, [ut, T2f], [pb[hf_]])
                        CP("act", bu.ap[0:QB, ch * 4 + hf_ * 2:ch * 4 + hf_ * 2 + 2].rearrange("p i r q -> p (i r q)"), pb[hf_].ap[0:QB, :], [pb[hf_]], [bu])

            def stage2a(blk):
                t0 = blk * QB
                ut = uTb[blk % 2]; bu = bus[blk % 2]
                TT_("dve", w1.ap[0:QB], bu.ap[0:QB, :, 0, :], Ainv.ap[0:QB, :, 0, :], ALU.mult, [bu, Ainv], [w1])
                TT_("pool", w4.ap[0:QB], bu.ap[0:QB, :, 1, :], Ainv.ap[0:QB, :, 0, :], ALU.mult, [bu, Ainv], [w4])
                TT_("dve", w2.ap[0:QB], bu.ap[0:QB, :, 1, :], Ainv.ap[0:QB, :, 1, :], ALU.mult, [bu, Ainv], [w2])
                TT_("dve", w3.ap[0:QB], bu.ap[0:QB, :, 0, :], Ainv.ap[0:QB, :, 1, :], ALU.mult, [bu, Ainv], [w3])

            def stageC(blk):
                t0 = blk * QB
                ut = uTb[blk % 2]; bu = bus[blk % 2]
                for ch in range(4):
                    pc = [PSB[4], PSB[5]]
                    for i4 in range(4):
                        pr = ch * 4 + i4
                        tgt = pc[i4 // 2]
                        cre_ = ((i4 % 2) * 2 + 0) * 128; cim_ = ((i4 % 2) * 2 + 1) * 128
                        MM(tgt.ap[:, cre_:cre_ + QB], w1.ap[0:QB, pr, :], tri.ap[0:QB, 0:QB], True, False, [w1, tri], [tgt])
                        MM(tgt.ap[:, cre_:cre_ + QB], w2.ap[0:QB, pr, :], ntri.ap[0:QB, 0:QB], False, True, [w2, ntri], [tgt])
                        MM(tgt.ap[:, cim_:cim_ + QB], w3.ap[0:QB, pr, :], tri.ap[0:QB, 0:QB], True, False, [w3, tri], [tgt])
                        MM(tgt.ap[:, cim_:cim_ + QB], w4.ap[0:QB, pr, :], tri.ap[0:QB, 0:QB], False, True, [w4, tri], [tgt])
                    for hf_ in range(2):
                        CP("act", cus.ap[:, ch * 4 + hf_ * 2:ch * 4 + hf_ * 2 + 2, :, 0:QB], pc[hf_].ap.rearrange("p (i r q) -> p i r q", i=2, r=2)[:, :, :, 0:QB], [pc[hf_]], [cus])

            def stageH(blk):
                t0 = blk * QB
                ut = uTb[blk % 2]; bu = bus[blk % 2]
                TT_("dve", cus.ap[:, :, 0, 0:QB], cus.ap[:, :, 0, 0:QB], hc.ap[:, :, 0].unsqueeze(2).to_broadcast([128, 16, QB]), ALU.add, [cus, hc], [cus])
                TT_("pool", cus.ap[:, :, 1, 0:QB], cus.ap[:, :, 1, 0:QB], hc.ap[:, :, 1].unsqueeze(2).to_broadcast([128, 16, QB]), ALU.add, [cus, hc], [cus])
                apr = ApT.ap[:, :, 0, 0:QB]; api = ApT.ap[:, :, 1, 0:QB]
                cre = cus.ap[:, :, 0, 0:QB]; cim = cus.ap[:, :, 1, 0:QB]
                L = QB - 1
                TT_("dve", hq.ap[:, :, 0], ApT.ap[:, :, 0, L], cus.ap[:, :, 0, L], ALU.mult, [ApT, cus], [hq])
                TT_("dve", hq.ap[:, :, 1], ApT.ap[:, :, 1, L], cus.ap[:, :, 1, L], ALU.mult, [ApT, cus], [hq])
                TT_("dve", hl.ap[:, :, 0], hq.ap[:, :, 0], hq.ap[:, :, 1], ALU.subtract, [hq], [hl])
                TT_("dve", hq.ap[:, :, 0], ApT.ap[:, :, 0, L], cus.ap[:, :, 1, L], ALU.mult, [ApT, cus], [hq])
                TT_("dve", hq.ap[:, :, 1], ApT.ap[:, :, 1, L], cus.ap[:, :, 0, L], ALU.mult, [ApT, cus], [hq])
                TT_("dve", hl.ap[:, :, 1], hq.ap[:, :, 0], hq.ap[:, :, 1], ALU.add, [hq], [hl])
                carry_from_hl()
                TT_("dve", t1.ap[:, :, 0:QB], apr, cre, ALU.mult, [ApT, cus], [t1])
                TT_("pool", t4.ap[:, :, 0:QB], api, cre, ALU.mult, [ApT, cus], [t4])
                TT_("dve", t2.ap[:, :, 0:QB], api, cim, ALU.mult, [ApT, cus], [t2])
                TT_("dve", t3.ap[:, :, 0:QB], apr, cim, ALU.mult, [ApT, cus], [t3])
                py = PSB[6]
                for ch in range(4):
                    MM(py.ap[0:QB, ch * 128:(ch + 1) * 128], ut.ap[:, ch, :], Dd.ap[:, ch, :], ch == 0, False, [ut, Dd], [py])
                for ch in range(4):
                    for i4 in range(4):
                        pr = ch * 4 + i4
                        o_ = py.ap[0:QB, pr * 32:(pr + 1) * 32]
                        cs_ = slice(i4 * 32, (i4 + 1) * 32)
                        MM(o_, t1.ap[:, pr, 0:QB], CT.ap[:, ch, 0, cs_], False, False, [t1, CT], [py])
                        MM(o_, t2.ap[:, pr, 0:QB], CTn.ap[:, ch, cs_], False, False, [t2, CTn], [py])
                        MM(o_, t3.ap[:, pr, 0:QB], CT.ap[:, ch, 1, cs_], False, False, [t3, CT], [py])
                        MM(o_, t4.ap[:, pr, 0:QB], CT.ap[:, ch, 1, cs_], False, pr == 15, [t4, CT], [py])

            def stage2c(blk):
                t0 = blk * QB
                ut = uTb[blk % 2]; bu = bus[blk % 2]
                py = PSB[6]
                ACT(y2.ap[0:QB], py.ap[0:QB, :], AF.Square, [py], [y2])
                TS_("dve", y2.ap[0:QB], y2.ap[0:QB], 0.044715, 1.0, ALU.mult, ALU.add, [y2], [y2])
                TT_("dve", yin.ap[0:QB], y2.ap[0:QB], py.ap[0:QB, :], ALU.mult, [y2, py], [yin])
                ACT(yin.ap[0:QB], yin.ap[0:QB], AF.Sigmoid, [yin], [yin], scale=1.5957691216057308)
                TT_("dve", zt.ap[0:QB], yin.ap[0:QB], py.ap[0:QB, :], ALU.mult, [yin, py], [zt])
                pz = PSB[7]
                pzb = pz.ap.bitcast(BF16)
                for c in range(4):
                    TR(pzb[:, c * 128:c * 128 + QB], zt.ap[0:QB, c * 128:(c + 1) * 128], identb, [zt], [pz])
                CP("act", zT.ap[:, :, 0:QB], pzb[:, 0:512].rearrange("p (a b) -> p a b", b=128)[:, :, 0:QB], [pz], [zT])
                pg = PSB[7]
                for m in range(4):
                    for kc in range(4):
                        MM(pg.ap[:, m * 128:m * 128 + QB], WGLU.ap[:, kc, m * 128:(m + 1) * 128], zT.ap[:, kc, 0:QB], kc == 0, kc == 3, [WGLU, zT], [pg])
                ACT(sg.ap[:, :, 0:QB], pg.ap.rearrange("p (a b) -> p a b", b=128)[:, :, 0:QB], AF.Sigmoid, [pg], [sg])
                TT_("dve", soT.ap[:, :, 0:QB], zT.ap[:, :, 0:QB], sg.ap[:, :, 0:QB], ALU.mult, [zT, sg], [soT])
                DMA("sync", S.soT.rearrange("(c p) t -> p c t", p=128)[:, :, t0:t0 + QB], soT.ap[:, :, 0:QB], [soT.b], [S.B_soT])

            stage1(0)
            stage2a(0)
            stageC(0)
            for blk in range(nblk):
                if blk + 1 < nblk:
                    stage1(blk + 1)
                stageH(blk)
                if blk + 1 < nblk:
                    stage2a(blk + 1)
                    stageC(blk + 1)
                stage2c(blk)
            for ri, dst in enumerate([S.nr_o, S.ni_o]):
                pt = PSB[4 + ri]
                hlc = A.f32([16])
                CP("dve", hlc.ap, hl.ap[:, :, ri], [hl], [hlc])
                TR(pt.ap[0:16, 0:128], hlc.ap, identf, [hlc], [pt])
                ho = A.f32([128], parts=16)
                CP("dve", ho.ap, pt.ap[0:16, 0:128], [pt], [ho])
                DMA("sync", dst[l].rearrange("(a b) p -> a (b p)", b=2), ho.ap, [ho.b], [])

        def post_norm_residual(S, l, oT, osq, xt, xn, gslot, rstd, tmp2):
            TT = S.TT
            pss = PS()
            for c in range(8):
                MM(pss.ap[:, 0:TT], onesM.ap, osq.ap[:, c, :], c == 0, c == 7, [onesM, osq], [pss])
            ACT(rstd.ap, pss.ap[:, 0:TT], AF.Sqrt, [pss, epsc], [rstd], bias=epsc.ap, scale=1.0)
            P.add("dve", lambda e: e.reciprocal(out=rstd.ap, in_=rstd.ap), [rstd.b], [rstd.b])
            for c in range(8):
                tm = tmp2[c % 2]
                STT(tm.ap, oT.ap[:, c, :], PAR.ap[:, l, S.s, gslot, c:c + 1], rstd.ap, ALU.mult, ALU.mult, [oT, PAR, rstd], [tm])
                TT_("dve", xn.ap[:, c, :], xt.ap[:, c, :], tm.ap, ALU.add, [xt, tm], [xn])

        def phase_DE(S, l, WOUT):
            s = S.s; TT = S.TT; QB = S.QB; nsub = TT // QB
            xt = A.f32([8, TT])
            xn = A.f32([8, TT]); oT = A.f32([8, TT]); sq = A.bf([8, TT]); osq = sq
            rstd = A.f32([TT]); tmp2 = [A.f32([TT]), A.f32([TT])]
            mixT = A.bf([8, TT]); hf = A.bf([8, TT]); hT = A.bf([NJ, TT]); sgt = A.f32([TT])
            WGs = [A.bf([8, 256]), A.bf([8, 256])]; WUs = [A.bf([8, 256]), A.bf([8, 256])]
            WDq = [A.bf([NJ, 256]), A.bf([NJ, 256])]
            wgi = 0; wdi = 0
            def load_mix(i):
                t0 = i * TT
                DMA("sync", mixT.ap[:, 4:8, :], S.soT.rearrange("(c p) t -> p c t", p=128)[:, :, t0:t0 + TT], [S.B_soT], [mixT.b])
                for j in range(nsub):
                    g = (t0 // QB) + j
                    pa = PS(); pab = pa.ap.bitcast(BF16)
                    for c in range(4):
                        TR(pab[:, c * 128:c * 128 + QB], ATT[s].ap[0:QB, g, c * 128:(c + 1) * 128], identb, [ATT[s]], [pa])
                    CP("act", mixT.ap[:, 0:4, j * QB:(j + 1) * QB], pab[:, 0:512].rearrange("p (a b) -> p a b", b=128)[:, :, 0:QB], [pa], [mixT])

            load_mix(0)
            for i in range(S.nt):
                t0 = i * TT
                DMA("sync", xt.ap, S.xT.rearrange("(c p) t -> p c t", p=128)[:, :, t0:t0 + TT], [S.B_xT[i]], [xt.b])
                pos = []
                for m in range(8):
                    po = PS()
                    for kc in range(8):
                        MM(po.ap[:, 0:TT], WOUT.ap[:, kc, m * 128:(m + 1) * 128], mixT.ap[:, kc, :], kc == 0, kc == 7, [WOUT, mixT], [po])
                    CP("dve", oT.ap[:, m, :], po.ap[:, 0:TT], [po], [oT])
                    ACT(osq.ap[:, m, :], po.ap[:, 0:TT], AF.Square, [po], [osq])
                if i + 1 < S.nt:
                    load_mix(i + 1)
                post_norm_residual(S, l, oT, osq, xt, xn, 2, rstd, tmp2)
                rmsnorm_mod(S, l, xn, 3, hf, tmp2, sq, rstd)
                for n0 in range(0, DFF, 256):
                    wg = WGs[wgi % 2]; wu = WUs[wgi % 2]; wgi += 1
                    DMA("sync", wg.ap, WG2[l][:, :, n0:n0 + 256], [B_WFF[l]], [wg.b])
                    DMA("sync", wu.ap, WU2[l][:, :, n0:n0 + 256], [B_WFF[l]], [wu.b])
                    for jj in range(2):
                        j = n0 // 128 + jj
                        pg = PS(); pu = PS()
                        for kc in range(8):
                            MM(pg.ap[:, 0:TT], wg.ap[:, kc, jj * 128:(jj + 1) * 128], hf.ap[:, kc, :], kc == 0, kc == 7, [wg, hf], [pg])
                        for kc in range(8):
                            MM(pu.ap[:, 0:TT], wu.ap[:, kc, jj * 128:(jj + 1) * 128], hf.ap[:, kc, :], kc == 0, kc == 7, [wu, hf], [pu])
                        ACT(sgt.ap, pg.ap[:, 0:TT], AF.Silu, [pg], [sgt])
                        TT_("dve", hT.ap[:, j, :], sgt.ap, pu.ap[:, 0:TT], ALU.mult, [sgt, pu], [hT])
                for q4 in range(4):
                    WD = WDq[wdi % 2]; wdi += 1
                    DMA("sync", WD.ap, WD2[l][:, :, q4 * 256:(q4 + 1) * 256], [B_WFF[l]], [WD.b])
                    for mm in range(2):
                        m = q4 * 2 + mm
                        pf = PS()
                        for j in range(NJ):
                            MM(pf.ap[:, 0:TT], WD.ap[:, j, mm * 128:(mm + 1) * 128], hT.ap[:, j, :], j == 0, j == NJ - 1, [WD, hT], [pf])
                        CP("dve", oT.ap[:, m, :], pf.ap[:, 0:TT], [pf], [oT])
                        ACT(osq.ap[:, m, :], pf.ap[:, 0:TT], AF.Square, [pf], [osq])
                post_norm_residual(S, l, oT, osq, xn, xt, 5, rstd, tmp2)
                DMA("sync", S.xT.rearrange("(c p) t -> p c t", p=128)[:, :, t0:t0 + TT], xt.ap, [xt.b], [S.B_xT[i]])

        def phase_final():
            A.reset()
            xin_ = [A.f32([8, 128]), A.f32([8, 128])]
            xo_ = [A.f32([D]), A.f32([D])]
            cnt = 0
            for S in STREAMS:
                qb = S.QB
                for i in range(S.T // qb):
                    xi = xin_[cnt % 2]; xo = xo_[cnt % 2]; cnt += 1
                    ti = (i * qb) // S.TT
                    DMA("sync", xi.ap[:, :, 0:qb], S.xT.rearrange("(c p) t -> p c t", p=128)[:, :, i * qb:(i + 1) * qb], [S.B_xT[ti]], [xi.b])
                    for half in range(2):
                        pt = PS()
                        for c4 in range(4):
                            TR(pt.ap[0:qb, c4 * 128:(c4 + 1) * 128], xi.ap[:, half * 4 + c4, 0:qb], identf, [xi], [pt])
                        CP("act" if half == 0 else "dve", xo.ap[0:qb, half * 512:(half + 1) * 512], pt.ap[0:qb, :], [pt], [xo])
                    DMA("sync", S.y_out[i * qb:(i + 1) * qb, :], xo.ap[0:qb, :], [xo.b], [])

        for l in range(DEPTH if stop_after != "XF" else 0):
            P.barrier(); A.reset()
            WIN = A.bf([8, DIN])
            DMA("pool", WIN.ap, w_in[l].rearrange("(kc p) n -> p kc n", p=128), [], [WIN.b])
            mark = A.off
            for S in STREAMS:
                A.off = mark
                phase_A(S, l, WIN)
                P.barrier()
            if stop_after is not None and stop_after.startswith("A"):
                break
            P.barrier(); A.reset()
            for S in STREAMS:
                A.reset()
                phase_B(S, l)
                P.barrier()
            if stop_after == "B":
                break
            P.barrier(); A.reset()
            tb = s5_prep(l)
            P.barrier()
            if stop_after in ("C0", "Ca", "Cb", "Cc", "Cd"):
                break
            A.off = tb["mark"]
            mark = A.off
            for S in STREAMS:
                A.off = mark
                phase_C(S, l, tb)
                P.barrier()
            if stop_after is not None and stop_after.startswith("C"):
                break
            P.barrier(); A.reset()
            WOUT = A.bf([8, D])
            DMA("pool", WOUT.ap, w_out[l].rearrange("(kc p) n -> p kc n", p=128), [], [WOUT.b])
            mark = A.off
            for S in STREAMS:
                A.off = mark
                phase_DE(S, l, WOUT)
                P.barrier()
        P.barrier()
        phase_final()


    body_fn()

    with nc.allow_low_precision("bf16 matmul operands, fp32 accumulate"):
        P.emit()
    P.close()
    return nc


_CACHE = {}


def kernel(x_prompt, x_sample, c_prompt, c_sample, cache_k, cache_v, cache_logf,
           state_ssm_re, state_ssm_im, w_ada, b_ada, g_pre_mix, g_post_mix, g_pre_ffn,
           g_post_ffn, w_in, b_forget, ssm_a_re, ssm_a_im, ssm_log_dt, ssm_b_re, ssm_b_im,
           ssm_c_re, ssm_c_im, ssm_d, w_glu, w_out, w_gate, w_up, w_down, _stop_after=None):
    f = lambda a: np.ascontiguousarray(np.asarray(a, dtype=np.float32))
    x_prompt = f(x_prompt); x_sample = f(x_sample)
    B, T, _ = x_prompt.shape
    BS, TS, _ = x_sample.shape
    DEPTH = w_in.shape[0]
    PAST = cache_k.shape[3]
    NC = 8
    key = (T, TS, PAST, DEPTH, _stop_after)
    if key not in _CACHE:
        _CACHE[key] = build(T, TS, PAST, DEPTH, _stop_after)
    nc = _CACHE[key]
    shared = dict(w_ada=f(w_ada), b_ada=f(b_ada), g_pre_mix=f(g_pre_mix), g_post_mix=f(g_post_mix),
                  g_pre_ffn=f(g_pre_ffn), g_post_ffn=f(g_post_ffn), w_in=f(w_in), b_forget=f(b_forget),
                  ssm_a_re=f(ssm_a_re), ssm_a_im=f(ssm_a_im), ssm_log_dt=f(ssm_log_dt),
                  ssm_b_re=f(ssm_b_re), ssm_b_im=f(ssm_b_im), ssm_c_re=f(ssm_c_re), ssm_c_im=f(ssm_c_im),
                  ssm_d=f(ssm_d), w_glu=f(w_glu), w_out=f(w_out), w_gate=f(w_gate), w_up=f(w_up), w_down=f(w_down))
    cache_k = f(cache_k); cache_v = f(cache_v); cache_logf = f(cache_logf)
    state_ssm_re = f(state_ssm_re); state_ssm_im = f(state_ssm_im)
    c_prompt = f(c_prompt); c_sample = f(c_sample)
    in_maps = []
    for c in range(NC):
        bp = c % B
        bs = c % BS
        m = dict(shared)
        m["xp"] = x_prompt[bp]; m["xs"] = x_sample[bs]
        m["c2"] = np.ascontiguousarray(np.stack([c_prompt[bp], c_sample[bs]], axis=0))
        m["ck"] = np.ascontiguousarray(cache_k[:, bs]); m["cv"] = np.ascontiguousarray(cache_v[:, bs])
        m["clf"] = np.ascontiguousarray(cache_logf[:, bs])
        m["sre"] = np.ascontiguousarray(state_ssm_re[:, bs]); m["sim"] = np.ascontiguousarray(state_ssm_im[:, bs])
        in_maps.append(m)
    res = run_bass_kernel_spmd(nc, in_maps, core_ids=list(range(NC)))
    R = res.results
    yp = np.stack([R[b]["yp"] for b in range(B)], axis=0)
    ys = np.stack([R[b]["ys"] for b in range(BS)], axis=0)
    st = lambda name, n: np.stack([R[b][name] for b in range(n)], axis=1)
    return (yp, ys, st("nkp", B), st("nvp", B), st("nlp", B), st("nrp", B), st("nip", B),
            st("nks", BS), st("nvs", BS), st("nls", BS), st("nrs", BS), st("nis", BS))
```

```python
import contextlib
import math
import numpy as np
import concourse.bass as bass
import concourse.mybir as mybir
from concourse.bass_utils import run_bass_kernel_spmd

F32 = mybir.dt.float32
BF16 = mybir.dt.bfloat16
AF = mybir.ActivationFunctionType
ALU = mybir.AluOpType

ENGS = ["pe", "act", "dve", "pool", "sync"]
COMPUTE = {"pe", "act", "dve", "pool"}
RING = 8
EPOCH = 20000


class Buf:
    __slots__ = ("name", "last_w", "readers", "psum")

    def __init__(self, name="", psum=False):
        self.name = name
        self.last_w = None
        self.readers = []
        self.psum = psum


class Op:
    __slots__ = ("eng", "fn", "dma", "deps", "idx", "signal", "sem", "val", "ringwait")

    def __init__(self, eng, fn, dma):
        self.eng = eng
        self.fn = fn
        self.dma = dma
        self.deps = set()
        self.signal = dma
        self.sem = None
        self.val = 0
        self.ringwait = None


class Prog:
    def __init__(self, nc):
        self.nc = nc
        self.ops = {e: [] for e in ENGS}
        self.allops = []
        self.stack = contextlib.ExitStack()
        self.pending_barrier = {e: None for e in ENGS}

    def sb(self, name, shape, dtype=F32):
        return self.stack.enter_context(self.nc.sbuf_tensor(name, list(shape), dtype))

    def ps(self, name, shape, dtype=F32):
        return self.stack.enter_context(self.nc.psum_tensor(name, list(shape), dtype))

    def buf(self, name=""):
        return Buf(name)

    def barrier(self):
        snap = set()
        for e in ENGS:
            lst = self.ops[e]
            if not lst:
                continue
            snap.add((e, len(lst) - 1))
            cnt = 0
            for i in range(len(lst) - 1, -1, -1):
                if lst[i].dma:
                    snap.add((e, i))
                    cnt += 1
                    if cnt >= RING:
                        break
                if len(lst) - i > 4 * RING + 64:
                    break
        for e in ENGS:
            self.pending_barrier[e] = snap

    def add(self, eng, fn, reads=(), writes=(), dma=False):
        op = Op(eng, fn, dma)
        lst = self.ops[eng]
        op.idx = len(lst)
        me = (eng, op.idx)
        same_ok = (eng == "pe") and not dma
        pb = self.pending_barrier[eng]
        if pb is not None:
            for d in pb:
                if d != me:
                    op.deps.add(d)
            self.pending_barrier[eng] = None
        for b in reads:
            w = b.last_w
            if w is not None and w != me:
                op.deps.add(w)
            if b.psum:
                for r in b.readers:
                    if r[0] != eng:
                        op.deps.add(r)
        for b in writes:
            w = b.last_w
            if w is not None and w != me:
                wop = self.ops[w[0]][w[1]]
                if not (same_ok and w[0] == eng and not wop.dma):
                    op.deps.add(w)
            for r in b.readers:
                if r == me:
                    continue
                rop = self.ops[r[0]][r[1]]
                if same_ok and r[0] == eng and not rop.dma:
                    continue
                op.deps.add(r)
        for b in reads:
            b.readers.append(me)
        for b in writes:
            b.last_w = me
            b.readers = []
        lst.append(op)
        self.allops.append(op)
        return op

    def emit(self):
        nc = self.nc
        for op in self.allops:
            for (e, i) in op.deps:
                self.ops[e][i].signal = True
        for e in ENGS:
            if e in COMPUTE:
                for op in reversed(self.ops[e]):
                    if not op.dma:
                        op.signal = True
                        break
        nsem = [0]

        def newsem(tag):
            nsem[0] += 1
            return self.stack.enter_context(nc.semaphore(f"s_{tag}_{nsem[0]}"))

        for e in ENGS:
            cur = None
            cnt = 0
            ring = None
            ringcnt = None
            ringprev = None
            nd = 0
            for op in self.ops[e]:
                if op.dma:
                    if ring is None or nd >= (EPOCH // 16) * RING:
                        ring = [newsem(e + "r") for _ in range(RING)]
                        ringcnt = [0] * RING
                        ringprev = [None] * RING
                        nd = 0
                    slot = nd % RING
                    if ringprev[slot] is not None:
                        op.ringwait = ringprev[slot]
                    ringcnt[slot] += 16
                    op.sem = ring[slot]
                    op.val = ringcnt[slot]
                    ringprev[slot] = (op.sem, op.val)
                    nd += 1
                elif op.signal:
                    if cur is None or cnt >= EPOCH:
                        cur = newsem(e)
                        cnt = 0
                    cnt += 1
                    op.sem = cur
                    op.val = cnt
        self.nsem = nsem[0]
        finals = {}
        for e in ENGS:
            for op in self.ops[e]:
                if op.dma:
                    k = id(op.sem)
                    if k not in finals or finals[k][1] < op.val:
                        finals[k] = (op.sem, op.val)
            if e in COMPUTE:
                for op in reversed(self.ops[e]):
                    if not op.dma:
                        finals[id(op.sem)] = (op.sem, op.val)
                        break
        prog = self

        def emit_engine(e, eng):
            known = {}
            for op in prog.ops[e]:
                waits = {}
                for (de, di) in op.deps:
                    dop = prog.ops[de][di]
                    k = id(dop.sem)
                    if k not in waits or waits[k][1] < dop.val:
                        waits[k] = (dop.sem, dop.val)
                if op.ringwait is not None:
                    k = id(op.ringwait[0])
                    if k not in waits or waits[k][1] < op.ringwait[1]:
                        waits[k] = op.ringwait
                for k, (s, v) in waits.items():
                    if known.get(k, 0) >= v:
                        continue
                    known[k] = v
                    eng.wait_ge(s, v)
                ins = op.fn(eng)
                if op.signal:
                    ins.then_inc(op.sem, 16 if op.dma else 1)
            if e == "sync":
                for k, (s, v) in finals.items():
                    if known.get(k, 0) < v:
                        eng.wait_ge(s, v)

        with nc.Block() as block:
            @block.tensor
            def _(eng):
                emit_engine("pe", eng)

            @block.scalar
            def _(eng):
                emit_engine("act", eng)

            @block.vector
            def _(eng):
                emit_engine("dve", eng)

            @block.gpsimd
            def _(eng):
                emit_engine("pool", eng)

            @block.sync
            def _(eng):
                emit_engine("sync", eng)

    def close(self):
        self.stack.close()


D = 1024
NH = 8
DH = 64
DATT = 512
DSSM = 512
DFF = 2816
DIN = 2056
NG = 32
NST = 64
EPS = 1e-6
NJ = DFF // 128
VW = 72
MAGIC = 12582912.0
TWO_PI = 2.0 * math.pi
CW1 = 6.28125
CW2 = float(np.float32(TWO_PI - 6.28125))
PI_LO = 3.1415925


class Tile:
    __slots__ = ("ap", "b")

    def __init__(self, ap, b):
        self.ap = ap
        self.b = b


def build(T, TS, PAST, DEPTH, stop_after=None):
    nc = bass.Bass("TRN2", target_bir_lowering=False)
    P = Prog(nc)

    def din(name, shape):
        return nc.dram_tensor(name, list(shape), F32, kind="ExternalInput").ap()

    def dout(name, shape):
        return nc.dram_tensor(name, list(shape), F32, kind="ExternalOutput").ap()

    def dscr(name, shape, dt=F32):
        return nc.dram_tensor(name, list(shape), dt, kind="Internal").ap()

    xp = din("xp", [T, D]); xs = din("xs", [TS, D]); c2 = din("c2", [2, D])
    ck = din("ck", [DEPTH, NH, PAST, DH]); cv = din("cv", [DEPTH, NH, PAST, DH])
    clf = din("clf", [DEPTH, NH, PAST])
    sre = din("sre", [DEPTH, NG, NST]); sim = din("sim", [DEPTH, NG, NST])
    w_ada = din("w_ada", [DEPTH, D, 6 * D]); b_ada = din("b_ada", [DEPTH, 6 * D])
    g_pre_mix = din("g_pre_mix", [DEPTH, D]); g_post_mix = din("g_post_mix", [DEPTH, D])
    g_pre_ffn = din("g_pre_ffn", [DEPTH, D]); g_post_ffn = din("g_post_ffn", [DEPTH, D])
    w_in = din("w_in", [DEPTH, D, DIN]); b_forget = din("b_forget", [DEPTH, NH])
    ssm_a_re = din("ssm_a_re", [DEPTH, NG, NST]); ssm_a_im = din("ssm_a_im", [DEPTH, NG, NST])
    ssm_log_dt = din("ssm_log_dt", [DEPTH, NG])
    ssm_b_re = din("ssm_b_re", [DEPTH, NG, NST, 16]); ssm_b_im = din("ssm_b_im", [DEPTH, NG, NST, 16])
    ssm_c_re = din("ssm_c_re", [DEPTH, NG, 16, NST]); ssm_c_im = din("ssm_c_im", [DEPTH, NG, 16, NST])
    ssm_d = din("ssm_d", [DEPTH, DSSM]); w_glu = din("w_glu", [DEPTH, DSSM, DSSM])
    w_out = din("w_out", [DEPTH, D, D]); w_gate = din("w_gate", [DEPTH, D, DFF])
    w_up = din("w_up", [DEPTH, D, DFF]); w_down = din("w_down", [DEPTH, DFF, D])

    yp = dout("yp", [T, D]); ys = dout("ys", [TS, D])
    nkp = dout("nkp", [DEPTH, NH, T, DH]); nvp = dout("nvp", [DEPTH, NH, T, DH])
    nlp = dout("nlp", [DEPTH, NH, T])
    nrp = dout("nrp", [DEPTH, NG, NST]); nip = dout("nip", [DEPTH, NG, NST])
    nks = dout("nks", [DEPTH, NH, TS, DH]); nvs = dout("nvs", [DEPTH, NH, TS, DH])
    nls = dout("nls", [DEPTH, NH, TS])
    nrs = dout("nrs", [DEPTH, NG, NST]); nis = dout("nis", [DEPTH, NG, NST])

    WG2 = dscr("WG2", [DEPTH, 128, 8, DFF], BF16)
    WU2 = dscr("WU2", [DEPTH, 128, 8, DFF], BF16)
    WD2 = dscr("WD2", [DEPTH, 128, NJ, D], BF16)
    B_WFF = [P.buf() for _ in range(DEPTH)]

    class Stream:
        pass

    def mk_stream(name, s, Tn, TT, koff, x_in, y_out, nk_o, nv_o, nl_o, nr_o, ni_o):
        S = Stream()
        S.name = name; S.s = s; S.T = Tn; S.TT = TT; S.nt = Tn // TT; S.koff = koff
        S.QB = min(128, Tn)
        S.nqg = Tn // S.QB
        S.QG = min(512, Tn)
        S.ngr = Tn // S.QG
        S.TK = koff + Tn
        S.nkt = (S.TK + 127) // 128
        S.x_in = x_in; S.y_out = y_out
        S.nk_o = nk_o; S.nv_o = nv_o; S.nl_o = nl_o; S.nr_o = nr_o; S.ni_o = ni_o
        S.xT = dscr(name + "_xT", [D, Tn]); S.B_xT = [P.buf() for _ in range(S.nt)]
        S.QT = dscr(name + "_QT", [NH, DH + 3, Tn], BF16)
        S.KT = dscr(name + "_KT", [NH, DH + 3, Tn], BF16)
        S.VX = dscr(name + "_VX", [Tn, NH, VW], BF16)
        S.B_qkv = P.buf(); S.B_kones = P.buf()
        S.uT = dscr(name + "_uT", [DSSM, Tn], BF16); S.B_uT = P.buf()
        S.soT = dscr(name + "_soT", [DSSM, Tn], BF16); S.B_soT = P.buf()
        return S

    SP = mk_stream("p", 0, T, min(512, T), 0, xp, yp, nkp, nvp, nlp, nrp, nip)
    SS = mk_stream("s", 1, TS, TS, PAST, xs, ys, nks, nvs, nls, nrs, nis)
    STREAMS = [SP, SS]

    def ptile(name, shape, dt=F32):
        h = P.sb(name, shape, dt)
        return Tile(h[tuple(slice(None) for _ in shape)], P.buf(name))

    identf = ptile("identf", [128, 128]); identb = ptile("identb", [128, 128], BF16)
    onesM = ptile("onesM", [128, 128], BF16); tri = ptile("tri", [128, 128], BF16)
    maskT = ptile("maskT", [128, 128], BF16); onesbf = ptile("onesbf", [8, 512], BF16)
    ntri = ptile("ntri", [128, 128], BF16)
    ones8 = ptile("ones8", [8, 512]); epsc = ptile("epsc", [128, 1]); one1 = ptile("one1", [128, 1])
    halfpi = ptile("halfpi", [128, 1]); trow = ptile("trow", [128, 128]); tcol = ptile("tcol", [128, 1])
    ntcol = ptile("ntcol", [128, 1]); sel8 = ptile("sel8", [8, NH, 128])
    PAR = ptile("PAR", [128, DEPTH, 2, 6, 8])
    scT = ptile("scT", [128, 8, 2], BF16)
    ATT = [ptile("ATTp", [128, SP.nqg, DATT], BF16), ptile("ATTs", [SS.QB, 1, DATT], BF16)]
    F_T = [ptile("F_Tp", [128, SP.nkt, NH]), ptile("F_Ts", [128, SS.nkt, NH])]
    CBC = [ptile("CBCp", [128, NH, SP.ngr]), ptile("CBCs", [128, NH, SS.ngr])]
    Fcs = [ptile("Fcp", [8, SP.ngr]), ptile("Fcs", [8, SS.ngr])]
    negb = ptile("negb", [8, 1])
    fcarry = ptile("fcarry", [8, 1])

    PSB = [Tile(P.ps(f"psb{i}", [128, 512])[:, :], Buf(f"psb{i}", psum=True)) for i in range(8)]

    ARW = 40960
    AR = P.sb("arena", [128, ARW])

    class Arena:
        def __init__(self):
            self.off = 0

        def reset(self):
            self.off = 0

        def get(self, nwords_f32, dt=F32, shape=None, parts=128):
            n = (nwords_f32 + 7) // 8 * 8
            assert self.off + n <= ARW, ("arena overflow", self.off, n)
            ap = AR[0:parts, self.off:self.off + nwords_f32]
            self.off += n
            if dt != F32:
                ap = ap.bitcast(dt)
            if shape is not None and len(shape) > 1:
                if len(shape) == 2:
                    ap = ap.rearrange("p (a b) -> p a b", a=shape[0])
                elif len(shape) == 3:
                    ap = ap.rearrange("p (a b c) -> p a b c", a=shape[0], b=shape[1])
                elif len(shape) == 4:
                    ap = ap.rearrange("p (a b c d) -> p a b c d", a=shape[0], b=shape[1], c=shape[2])
            return Tile(ap, P.buf())

        def f32(self, shape, parts=128):
            return self.get(int(np.prod(shape)), F32, shape, parts)

        def bf(self, shape, parts=128):
            n = int(np.prod(shape))
            assert n % 2 == 0
            return self.get(n // 2, BF16, shape, parts)

    A = Arena()

    def rb(ts):
        return [t.b for t in ts]

    def MM(out, lhsT, rhs, start, stop, reads, writes, **kw):
        P.add("pe", lambda e: e.matmul(out, lhsT=lhsT, rhs=rhs, start=start, stop=stop, **kw), rb(reads), rb(writes))

    def TR(out, in_, ident, reads, writes):
        P.add("pe", lambda e: e.transpose(out, in_, ident.ap[0:in_.shape[0], 0:in_.shape[0]]), rb(reads) + [ident.b], rb(writes))

    def ACT(out, in_, func, reads, writes, bias=None, scale=None):
        kw = {}
        if bias is not None:
            kw["bias"] = bias
        if scale is not None:
            kw["scale"] = scale
        P.add("act", lambda e: e.activation(out=out, in_=in_, func=func, **kw), rb(reads), rb(writes))

    def TT_(eng, out, in0, in1, op, reads, writes):
        P.add(eng, lambda e: e.tensor_tensor(out=out, in0=in0, in1=in1, op=op), rb(reads), rb(writes))

    def TS_(eng, out, in0, s1, s2, op0, op1, reads, writes):
        if s2 is None:
            P.add(eng, lambda e: e.tensor_scalar(out=out, in0=in0, scalar1=s1, scalar2=None, op0=op0), rb(reads), rb(writes))
        else:
            P.add(eng, lambda e: e.tensor_scalar(out=out, in0=in0, scalar1=s1, scalar2=s2, op0=op0, op1=op1), rb(reads), rb(writes))

    def STT(out, in0, scalar, in1, op0, op1, reads, writes):
        P.add("dve", lambda e: e.scalar_tensor_tensor(out=out, in0=in0, scalar=scalar, in1=in1, op0=op0, op1=op1), rb(reads), rb(writes))

    def CP(eng, out, in_, reads, writes):
        if eng == "act":
            P.add("act", lambda e: e.copy(out=out, in_=in_), rb(reads), rb(writes))
        else:
            P.add(eng, lambda e: e.tensor_copy(out=out, in_=in_), rb(reads), rb(writes))

    def MSET(eng, t, val):
        P.add(eng, lambda e: e.memset(t.ap, val), [], [t.b])

    def DMA(q, out, in_, reads, writes):
        P.add(q, lambda e: e.dma_start(out=out, in_=in_), reads, writes, dma=True)

    psrr = [0]

    def PS():
        t = PSB[psrr[0] % 8]
        psrr[0] += 1
        return t

    def body_fn():
        MSET("pool", identf, 1.0)
        P.add("pool", lambda e: e.affine_select(out=identf.ap, in_=identf.ap, pattern=[[1, 128]], compare_op=ALU.is_equal, fill=0.0, base=0, channel_multiplier=-1), [identf.b], [identf.b])
        MSET("pool", identb, 1.0)
        P.add("pool", lambda e: e.affine_select(out=identb.ap, in_=identb.ap, pattern=[[1, 128]], compare_op=ALU.is_equal, fill=0.0, base=0, channel_multiplier=-1), [identb.b], [identb.b])
        MSET("pool", tri, 1.0)
        P.add("pool", lambda e: e.affine_select(out=tri.ap, in_=tri.ap, pattern=[[1, 128]], compare_op=ALU.is_ge, fill=0.0, base=0, channel_multiplier=-1), [tri.b], [tri.b])
        MSET("pool", maskT, -30000.0)
        P.add("pool", lambda e: e.affine_select(out=maskT.ap, in_=maskT.ap, pattern=[[1, 128]], compare_op=ALU.is_gt, fill=0.0, base=0, channel_multiplier=-1), [maskT.b], [maskT.b])
        MSET("dve", onesbf, 1.0)
        for S_ in STREAMS:
            for r in range(3):
                for t0_ in range(0, S_.T, 512):
                    tw = min(512, S_.T - t0_)
                    DMA("sync", S_.KT[:, DH + r, t0_:t0_ + tw], onesbf.ap[:, 0:tw], [onesbf.b], [S_.B_kones])
        TS_("dve", ntri.ap, tri.ap, -1.0, None, ALU.mult, None, [tri], [ntri])
        MSET("dve", onesM, 1.0 / 1024.0)
        MSET("dve", ones8, 1.0)
        MSET("dve", epsc, EPS)
        MSET("dve", one1, 1.0)
        MSET("dve", halfpi, math.pi / 2.0)
        P.add("pool", lambda e: e.iota(trow.ap, pattern=[[1, 128]], base=0, channel_multiplier=0, allow_small_or_imprecise_dtypes=True), [], [trow.b])
        P.add("pool", lambda e: e.iota(tcol.ap, pattern=[[0, 1]], base=0, channel_multiplier=1, allow_small_or_imprecise_dtypes=True), [], [tcol.b])
        TS_("dve", ntcol.ap, tcol.ap, -1.0, None, ALU.mult, None, [tcol], [ntcol])
        MSET("pool", sel8, 1.0)
        P.add("pool", lambda e: e.affine_select(out=sel8.ap, in_=sel8.ap, pattern=[[-1, NH], [0, 128]], compare_op=ALU.is_equal, fill=0.0, base=0, channel_multiplier=1), [sel8.b], [sel8.b])

        if stop_after == "P0":
            return
        A.reset()
        c2n = A.f32([D], parts=2)
        DMA("sync", c2n.ap, c2, [], [c2n.b])
        pc = PS()
        for kc in range(8):
            TR(pc.ap[:, kc * 2:kc * 2 + 2], c2n.ap[:, kc * 128:(kc + 1) * 128], identf, [c2n], [pc])
        ACT(scT.ap.rearrange("p a b -> p (a b)"), pc.ap[:, 0:16], AF.Silu, [pc], [scT])
        spn = A.f32([128], parts=80)
        spT = A.f32([80])
        modT = A.f32([48, 2])
        WA = [A.bf([8, 1024]), A.bf([8, 1024])]
        wai = 0
        for l in range(DEPTH):
            DMA("sync", spn.ap[0:48, :], b_ada[l].rearrange("(a b) -> a b", b=128), [], [spn.b])
            for i, g in enumerate([g_pre_mix, g_post_mix, g_pre_ffn, g_post_ffn]):
                DMA("sync", spn.ap[48 + 8 * i:56 + 8 * i, :], g[l].rearrange("(a b) -> a b", b=128), [], [spn.b])
            pt = PS()
            TR(pt.ap[:, 0:80], spn.ap, identf, [spn], [pt])
            CP("dve", spT.ap, pt.ap[:, 0:80], [pt], [spT])
            pm = PS()
            for piece in range(6):
                wa = WA[wai % 2]; wai += 1
                DMA("pool", wa.ap, w_ada[l, :, piece * 1024:(piece + 1) * 1024].rearrange("(kc p) n -> p kc n", p=128), [], [wa.b])
                for j in range(8):
                    cj = piece * 8 + j
                    for kc in range(8):
                        MM(pm.ap[:, cj * 2:cj * 2 + 2], wa.ap[:, kc, j * 128:(j + 1) * 128], scT.ap[:, kc, :], kc == 0, kc == 7, [wa, scT], [pm])
            for s in range(2):
                TT_("dve", modT.ap[:, :, s], pm.ap[:, 0:96].rearrange("p (a b) -> p a b", b=2)[:, :, s], spT.ap[:, 0:48], ALU.add, [pm, spT], [modT])
            for s in range(2):
                par = PAR.ap[:, l, s]
                STT(par[:, 0, :], modT.ap[:, 8:16, s], 1.0, spT.ap[:, 48:56], ALU.add, ALU.mult, [modT, spT], [PAR])
                CP("dve", par[:, 1, :], modT.ap[:, 0:8, s], [modT], [PAR])
                TT_("dve", par[:, 2, :], modT.ap[:, 16:24, s], spT.ap[:, 56:64], ALU.mult, [modT, spT], [PAR])
                STT(par[:, 3, :], modT.ap[:, 32:40, s], 1.0, spT.ap[:, 64:72], ALU.add, ALU.mult, [modT, spT], [PAR])
                CP("dve", par[:, 4, :], modT.ap[:, 24:32, s], [modT], [PAR])
                TT_("dve", par[:, 5, :], modT.ap[:, 40:48, s], spT.ap[:, 72:80], ALU.mult, [modT, spT], [PAR])

        if stop_after == "P1":
            return
        for l in range(DEPTH):
            DMA("pool", WG2[l], w_gate[l].rearrange("(kc p) n -> p kc n", p=128), [], [B_WFF[l]])
            DMA("pool", WU2[l], w_up[l].rearrange("(kc p) n -> p kc n", p=128), [], [B_WFF[l]])
            DMA("pool", WD2[l], w_down[l].rearrange("(j p) n -> p j n", p=128), [], [B_WFF[l]])

        if stop_after == "P2":
            return
        P.barrier()
        A.reset()
        xin = [A.f32([D]), A.f32([D])]
        xtr = [A.f32([8, 128]), A.f32([8, 128])]
        cnt = 0
        for S in STREAMS:
            nb = S.T // S.QB
            for i in range(nb):
                xi = xin[cnt % 2]; xo = xtr[cnt % 2]; cnt += 1
                qb = S.QB
                DMA("sync", xi.ap[0:qb, :], S.x_in[i * qb:(i + 1) * qb, :], [], [xi.b])
                for half in range(2):
                    pt = PS()
                    for c4 in range(4):
                        c = half * 4 + c4
                        TR(pt.ap[:, c4 * 128:c4 * 128 + qb], xi.ap[0:qb, c * 128:(c + 1) * 128], identf, [xi], [pt])
                    CP("act" if half == 0 else "dve", xo.ap[:, half * 4:half * 4 + 4, 0:qb], pt.ap.rearrange("p (a b) -> p a b", b=128)[:, :, 0:qb], [pt], [xo])
                ti = (i * qb) // S.TT
                DMA("sync", S.xT.rearrange("(c p) t -> p c t", p=128)[:, :, i * qb:(i + 1) * qb], xo.ap[:, :, 0:qb], [xo.b], [S.B_xT[ti]])

        if stop_after == "X":
            return
        def rmsnorm_mod(S, l, xt, which, hm, tmp2, sq, rstd):
            TT = S.TT
            ACT(sq.ap, xt.ap, AF.Square, [xt], [sq])
            pss = PS()
            for c in range(8):
                MM(pss.ap[:, 0:TT], onesM.ap, sq.ap[:, c, :], c == 0, c == 7, [onesM, sq], [pss])
            ACT(rstd.ap, pss.ap[:, 0:TT], AF.Sqrt, [pss, epsc], [rstd], bias=epsc.ap, scale=1.0)
            P.add("dve", lambda e: e.reciprocal(out=rstd.ap, in_=rstd.ap), [rstd.b], [rstd.b])
            for c in range(8):
                tm = tmp2[c % 2]
                STT(tm.ap, xt.ap[:, c, :], PAR.ap[:, l, S.s, which, c:c + 1], rstd.ap, ALU.mult, ALU.mult, [xt, PAR, rstd], [tm])
                ACT(hm.ap[:, c, :], tm.ap, AF.Identity, [tm, PAR], [hm], bias=PAR.ap[:, l, S.s, which + 1, c:c + 1], scale=1.0)

        def phase_A(S, l, WIN):
            TT = S.TT; QB = S.QB; nsub = TT // QB
            s = S.s
            xts = [A.f32([8, TT]), A.f32([8, TT])]
            sq = A.bf([8, TT]); rstd = A.f32([TT])
            tmp2 = [A.f32([TT]), A.f32([TT])]
            hms = [A.bf([8, TT]), A.bf([8, TT])]
            qTa = A.bf([NH, TT], parts=64); kTa = A.bf([NH, TT], parts=64)
            ktm = [A.f32([DATT]), A.f32([DATT])]; vtm = [A.f32([DATT]), A.f32([DATT])]
            vx = [A.bf([NH, VW]), A.bf([NH, VW])]
            uTa = A.bf([4, TT])
            gT = A.f32([TT], parts=8); lf = A.f32([TT], parts=8); Fp = A.f32([TT], parts=8)
            Dq = A.f32([TT], parts=8); Dsp = [A.bf([TT], parts=8) for _ in range(3)]
            lfn = 0
            if S.koff > 0:
                MSET("dve", F_T[s], 0.0)
                npast = S.koff
                clt = A.f32([npast], parts=8)
                Fpast = A.f32([npast], parts=8)
                DMA("sync", clt.ap, clf[l], [], [clt.b])
                MSET("dve", fcarry, 0.0)
                cs_ = min(512, npast)
                for i0 in range(0, npast, cs_):
                    P.add("dve", lambda e, i0=i0: e.tensor_tensor_scan(out=Fpast.ap[:, i0:i0 + cs_], data0=ones8.ap[:, 0:cs_], data1=clt.ap[:, i0:i0 + cs_], initial=fcarry.ap, op0=ALU.mult, op1=ALU.add), [ones8.b, clt.b, fcarry.b], [Fpast.b])
                    CP("dve", fcarry.ap, Fpast.ap[:, i0 + cs_ - 1:i0 + cs_], [Fpast], [fcarry])
                pf = PS()
                for kt in range(npast // 128):
                    TR(pf.ap[:, kt * 8:kt * 8 + 8], Fpast.ap[:, kt * 128:(kt + 1) * 128], identf, [Fpast], [pf])
                CP("dve", F_T[s].ap[:, 0:npast // 128, :], pf.ap[:, 0:(npast // 128) * 8].rearrange("p (a b) -> p a b", b=8), [pf], [F_T[s]])
            else:
                MSET("dve", fcarry, 0.0)
            DMA("sync", negb.ap, b_forget[l].rearrange("(a b) -> a b", b=1), [], [negb.b])
            TS_("dve", negb.ap, negb.ap, -1.0, None, ALU.mult, None, [negb], [negb])
            def load_norm(i):
                xt_ = xts[i % 2]; hm_ = hms[i % 2]
                DMA("sync", xt_.ap, S.xT.rearrange("(c p) t -> p c t", p=128)[:, :, i * TT:(i + 1) * TT], [S.B_xT[i]], [xt_.b])
                rmsnorm_mod(S, l, xt_, 0, hm_, tmp2, sq, rstd)

            load_norm(0)
            for i in range(S.nt):
                xt = xts[i % 2]; hm = hms[i % 2]
                t0 = i * TT
                if i + 1 < S.nt:
                    load_norm(i + 1)
                if stop_after == "A1":
                    return
                for h in range(NH):
                    pq = PS()
                    for c in range(8):
                        MM(pq.ap[0:64, 0:TT], WIN.ap[:, c, h * 64:(h + 1) * 64], hm.ap[:, c, :], c == 0, c == 7, [WIN, hm], [pq])
                    ACT(qTa.ap[:, h, :], pq.ap[0:64, 0:TT], AF.Identity, [pq], [qTa], scale=0.125)
                    pk = PS()
                    for c in range(8):
                        MM(pk.ap[0:64, 0:TT], WIN.ap[:, c, DATT + h * 64:DATT + (h + 1) * 64], hm.ap[:, c, :], c == 0, c == 7, [WIN, hm], [pk])
                    CP("dve", kTa.ap[:, h, :], pk.ap[0:64, 0:TT], [pk], [kTa])
                DMA("sync", S.QT.rearrange("h d t -> d h t")[0:DH, :, t0:t0 + TT], qTa.ap, [qTa.b], [S.B_qkv])
                DMA("sync", S.KT.rearrange("h d t -> d h t")[0:DH, :, t0:t0 + TT], kTa.ap, [kTa.b], [S.B_qkv])
                if stop_after == "A2":
                    return
                for j in range(nsub):
                    tb0 = t0 + j * QB
                    kt_ = ktm[j % 2]; vt_ = vtm[j % 2]; vx_ = vx[j % 2]
                    pk = PS()
                    for c in range(8):
                        MM(pk.ap[0:QB, :], hm.ap[:, c, j * QB:(j + 1) * QB], WIN.ap[:, c, DATT:2 * DATT], c == 0, c == 7, [WIN, hm], [pk])
                    CP("act", kt_.ap[0:QB, :], pk.ap[0:QB, :], [pk], [kt_])
                    if stop_after == "A2a":
                        return
                    DMA("sync", S.nk_o[l].rearrange("h t d -> t h d")[tb0:tb0 + QB], kt_.ap[0:QB, :].rearrange("p (h d) -> p h d", d=DH), [kt_.b], [])
                    if stop_after == "A2b":
                        return
                    pv = PS()
                    for c in range(8):
                        MM(pv.ap[0:QB, :], hm.ap[:, c, j * QB:(j + 1) * QB], WIN.ap[:, c, 2 * DATT:3 * DATT], c == 0, c == 7, [WIN, hm], [pv])
                    CP("act", vt_.ap[0:QB, :], pv.ap[0:QB, :], [pv], [vt_])
                    DMA("sync", S.nv_o[l].rearrange("h t d -> t h d")[tb0:tb0 + QB], vt_.ap[0:QB, :].rearrange("p (h d) -> p h d", d=DH), [vt_.b], [])
                    if stop_after == "A2c":
                        return
                    P.add("pool", lambda e, vx_=vx_: e.memset(vx_.ap[0:QB].rearrange("p h d -> p (h d)"), 1.0), [], [vx_.b])
                    CP("dve", vx_.ap[0:QB, :, 0:DH], vt_.ap[0:QB, :].rearrange("p (h d) -> p h d", d=DH), [vt_], [vx_])
                    if stop_after == "A2d":
                        return
                    DMA("sync", S.VX[tb0:tb0 + QB], vx_.ap[0:QB], [vx_.b], [S.B_qkv])
                if stop_after == "A3":
                    return
                for m in range(4):
                    pu = PS()
                    for c in range(8):
                        MM(pu.ap[:, 0:TT], WIN.ap[:, c, 3 * DATT + NH + m * 128:3 * DATT + NH + (m + 1) * 128], hm.ap[:, c, :], c == 0, c == 7, [WIN, hm], [pu])
                    CP("act" if m % 2 else "dve", uTa.ap[:, m, :], pu.ap[:, 0:TT], [pu], [uTa])
                DMA("sync", S.uT.rearrange("(c p) t -> p c t", p=128)[:, :, t0:t0 + TT], uTa.ap, [uTa.b], [S.B_uT])
                if stop_after == "A4":
                    return
                pg = PS()
                for c in range(8):
                    MM(pg.ap[0:8, 0:TT], WIN.ap[:, c, 3 * DATT:3 * DATT + NH], hm.ap[:, c, :], c == 0, c == 7, [WIN, hm], [pg])
                ACT(gT.ap, pg.ap[0:8, 0:TT], AF.Exp, [pg, negb], [gT], bias=negb.ap, scale=-1.0)
                ACT(gT.ap, gT.ap, AF.Ln, [gT, one1], [gT], bias=one1.ap[0:8], scale=1.0)
                TS_("dve", lf.ap, gT.ap, -1.0, None, ALU.mult, None, [gT], [lf])
                DMA("sync", S.nl_o[l][:, t0:t0 + TT], lf.ap, [lf.b], [])
                P.add("dve", lambda e: e.tensor_tensor_scan(out=Fp.ap, data0=ones8.ap[:, 0:TT], data1=lf.ap, initial=fcarry.ap, op0=ALU.mult, op1=ALU.add), [ones8.b, lf.b, fcarry.b], [Fp.b])
                CP("dve", fcarry.ap, Fp.ap[:, TT - 1:TT], [Fp], [fcarry])
                if stop_after == "A5":
                    return
                assert S.TT == S.QG
                CP("dve", Fcs[s].ap[:, i:i + 1], Fp.ap[:, 0:1], [Fp], [Fcs[s]])
                TS_("dve", Dq.ap, Fp.ap, Fcs[s].ap[:, i:i + 1], None, ALU.subtract, None, [Fp, Fcs[s]], [Dq])
                for r in range(3):
                    CP("dve", Dsp[r].ap, Dq.ap, [Dq], [Dsp[r]])
                    if r < 2:
                        TT_("dve", Dq.ap, Dq.ap, Dsp[r].ap, ALU.subtract, [Dq, Dsp[r]], [Dq])
                    DMA("sync", S.QT[:, DH + r, t0:t0 + TT], Dsp[r].ap, [Dsp[r].b], [S.B_qkv])
                pf = PS()
                for j in range(nsub):
                    TR(pf.ap[0:QB, j * 8:j * 8 + 8], Fp.ap[:, j * QB:(j + 1) * QB], identf, [Fp], [pf])
                kt0 = (S.koff + t0) // 128
                CP("dve", F_T[s].ap[0:QB, kt0:kt0 + nsub, :], pf.ap[0:QB, 0:nsub * 8].rearrange("p (a b) -> p a b", b=8), [pf], [F_T[s]])
            pcb = PS()
            for h in range(NH):
                MM(pcb.ap[:, h * S.ngr:(h + 1) * S.ngr], sel8.ap[:, h, :], Fcs[s].ap, True, True, [sel8, Fcs[s]], [pcb])
            CP("dve", CBC[s].ap.rearrange("p a b -> p (a b)"), pcb.ap[:, 0:NH * S.ngr], [pcb], [CBC[s]])

        def phase_B(S, l):
            s = S.s; QB = S.QB; QG = S.QG; nsg = QG // QB; nkt = S.nkt; koff = S.koff; Tn = S.T; TK = S.TK
            npast_t = koff // 128
            KA = DH + 3
            QTh = [A.bf([Tn], parts=KA), A.bf([Tn], parts=KA)]
            KTh = [A.bf([TK], parts=KA), A.bf([TK], parts=KA)]
            VXh = [A.bf([nkt, VW]), A.bf([nkt, VW])]
            NPT = 6
            pTs = [A.bf([QG]) for _ in range(NPT)]
            biasg = [A.f32([nkt]), A.f32([nkt])]
            rsum = [A.f32([nsg]), A.f32([nsg])]
            if koff > 0:
                ckt = [A.f32([npast_t, DH]), A.f32([npast_t, DH])]
            LA = 3
            cnt = {"pt": 0, "bg": 0, "po": 0, "ps": 0}
            for h in range(NH):
                qt = QTh[h % 2]; kt = KTh[h % 2]; vxh = VXh[h % 2]
                DMA("sync", qt.ap, S.QT[h], [S.B_qkv], [qt.b])
                DMA("sync", kt.ap[:, koff:koff + Tn], S.KT[h], [S.B_qkv, S.B_kones], [kt.b])
                if koff > 0:
                    P.add("dve", lambda e, kt=kt: e.memset(kt.ap[DH:DH + 3, 0:koff], 1.0), [], [kt.b])
                ntile_new = (Tn + 127) // 128
                if Tn >= 128:
                    DMA("sync", vxh.ap[:, npast_t:npast_t + ntile_new, :], S.VX.rearrange("(a p) h d -> p a h d", p=128)[:, :, h, :], [S.B_qkv], [vxh.b])
                else:
                    DMA("sync", vxh.ap[0:Tn, npast_t, :], S.VX[:, h, :], [S.B_qkv], [vxh.b])
                if koff > 0:
                    ck_ = ckt[h % 2]
                    DMA("sync", ck_.ap, ck[l, h].rearrange("(a p) d -> p a d", p=128), [], [ck_.b])
                    for a0 in range(0, npast_t, 4):
                        pt = PSB[cnt["ps"] % 6]; cnt["ps"] += 1
                        for a in range(a0, min(a0 + 4, npast_t)):
                            TR(pt.ap[0:64, (a - a0) * 128:(a - a0 + 1) * 128], ck_.ap[:, a, :], identf, [ck_], [pt])
                        na = min(4, npast_t - a0)
                        CP("dve", kt.ap[0:DH, a0 * 128:(a0 + na) * 128], pt.ap[0:64, 0:na * 128], [pt], [kt])
                    P.add("pool", lambda e, vxh=vxh: e.memset(vxh.ap[:, 0:npast_t, :].rearrange("p a d -> p (a d)"), 1.0), [], [vxh.b])
                    DMA("pool", vxh.ap[:, 0:npast_t, 0:DH], cv[l, h].rearrange("(a p) d -> p a d", p=128), [vxh.b], [vxh.b])
                for G in range(S.ngr):
                    q0 = G * QG
                    last_kt = (koff + q0 + QG - 1) // 128
                    first_diag = (koff + q0) // 128
                    bg = biasg[cnt["bg"] % 2]; cnt["bg"] += 1
                    TS_("dve", bg.ap[:, 0:last_kt + 1], F_T[s].ap[:, 0:last_kt + 1, h], -1.0, CBC[s].ap[:, h, G:G + 1], ALU.mult, ALU.add, [F_T[s], CBC[s]], [bg])
                    po = PSB[6 + (cnt["po"] % 2)]; cnt["po"] += 1
                    pov = po.ap[:, 0:nsg * 128].rearrange("p (i d) -> p i d", d=128)
                    blocks = []
                    for k_ in range(last_kt + 1):
                        kp = min(128, TK - k_ * 128)
                        j = max(0, k_ - first_diag)
                        blocks.append((k_, kp, j))

                    def score(bi):
                        k_, kp, j = blocks[bi]
                        psc = PSB[cnt["ps"] % 6]; cnt["ps"] += 1
                        c0 = j * QB
                        diag = k_ >= first_diag
                        MM(psc.ap[0:kp, c0:QG], kt.ap[:, k_ * 128:k_ * 128 + kp], qt.ap[:, q0 + c0:q0 + QG], True, not diag, [kt, qt], [psc])
                        if diag:
                            MM(psc.ap[0:kp, c0:c0 + QB], maskT.ap[0:QB, 0:kp], identb.ap[0:QB, 0:QB], False, True, [maskT, identb], [psc])
                        pT = pTs[cnt["pt"] % NPT]; cnt["pt"] += 1
                        ACT(pT.ap[0:kp, c0:QG], psc.ap[0:kp, c0:QG], AF.Exp, [psc, bg], [pT], bias=bg.ap[0:kp, k_:k_ + 1], scale=1.0)
                        return pT

                    def pv(bi, pT):
                        k_, kp, j = blocks[bi]
                        for i in range(j, nsg):
                            first = (bi == 0 and i == j)
                            last = (bi == len(blocks) - 1 and i == nsg - 1)
                            MM(pov[0:QB, i, 0:DH + 1], pT.ap[0:kp, i * QB:(i + 1) * QB], vxh.ap[0:kp, k_, 0:DH + 1], first, last, [pT, vxh], [po])

                    pend = []
                    nb = len(blocks)
                    for bi in range(nb + LA):
                        if bi < nb:
                            pend.append((bi, score(bi)))
                        if bi >= LA:
                            b2, pT2 = pend.pop(0)
                            pv(b2, pT2)
                    rs = rsum[G % 2]
                    P.add("dve", lambda e, rs=rs, pov=pov: e.reciprocal(out=rs.ap[0:QB, :], in_=pov[0:QB, :, DH]), [po.b], [rs.b])
                    TT_("dve", ATT[s].ap[0:QB, G * nsg:(G + 1) * nsg, h * DH:(h + 1) * DH], pov[0:QB, :, 0:DH], rs.ap[0:QB, :].unsqueeze(2).to_broadcast([QB, nsg, DH]), ALU.mult, [po, rs], [ATT[s]])

        def s5_prep(l):
            tb = {}
            Ainv = A.f32([16, 2, 128]); ApT = A.f32([16, 2, 128]); T2 = A.bf([4, 2, 128]); CT = A.bf([4, 2, 128])
            Dd = A.bf([4, 128]); a1 = A.f32([16, 2]); WGLU = A.bf([4, DSSM]); T2f = A.bf([4, 4, 2, 128]); CTn = A.bf([4, 128])
            tb.update(Ainv=Ainv, ApT=ApT, T2=T2, CT=CT, Dd=Dd, a1=a1, WGLU=WGLU, T2f=T2f, CTn=CTn)
            mark = A.off
            tb["mark"] = mark
            are_n = A.f32([128], parts=16); aim_n = A.f32([128], parts=16); ldt = A.f32([2], parts=16)
            DMA("sync", are_n.ap, ssm_a_re[l].rearrange("(a b) p -> a (b p)", b=2), [], [are_n.b])
            DMA("sync", aim_n.ap, ssm_a_im[l].rearrange("(a b) p -> a (b p)", b=2), [], [aim_n.b])
            DMA("sync", ldt.ap, ssm_log_dt[l].rearrange("(a b) -> a b", b=2), [], [ldt.b])
            ACT(ldt.ap, ldt.ap, AF.Exp, [ldt], [ldt])
            al_n = A.f32([128], parts=16); th_n = A.f32([128], parts=16)
            dtb = ldt.ap.unsqueeze(2).to_broadcast([16, 2, 64])
            TT_("dve", al_n.ap.rearrange("p (a b) -> p a b", a=2), are_n.ap.rearrange("p (a b) -> p a b", a=2), dtb, ALU.mult, [are_n, ldt], [al_n])
            TT_("dve", th_n.ap.rearrange("p (a b) -> p a b", a=2), aim_n.ap.rearrange("p (a b) -> p a b", a=2), dtb, ALU.mult, [aim_n, ldt], [th_n])
            sm = A.f32([4, 16])
            pt = PS()
            for i, src in enumerate([al_n, th_n, are_n, aim_n]):
                TR(pt.ap[:, i * 16:(i + 1) * 16], src.ap, identf, [src], [pt])
            CP("dve", sm.ap.rearrange("p a b -> p (a b)"), pt.ap[:, 0:64], [pt], [sm])

            def sincos(ang, shape_elems, cosv, sinv, tmpk):
                TS_("dve", tmpk.ap, ang.ap, 1.0 / TWO_PI, MAGIC, ALU.mult, ALU.add, [ang], [tmpk])
                TS_("dve", tmpk.ap, tmpk.ap, -MAGIC, None, ALU.add, None, [tmpk], [tmpk])
                STT(ang.ap, tmpk.ap, -CW1, ang.ap, ALU.mult, ALU.add, [tmpk, ang], [ang])
                STT(ang.ap, tmpk.ap, -CW2, ang.ap, ALU.mult, ALU.add, [tmpk, ang], [ang])
                TS_("dve", ang.ap, ang.ap, PI_LO, -PI_LO, ALU.min, ALU.max, [ang], [ang])
                ACT(sinv.ap, ang.ap, AF.Sin, [ang], [sinv])
                ACT(tmpk.ap, ang.ap, AF.Abs, [ang], [tmpk])
                ACT(cosv.ap, tmpk.ap, AF.Sin, [tmpk, halfpi], [cosv], bias=halfpi.ap[0:cosv.ap.shape[0]], scale=-1.0)

            angS = A.f32([16, 128]); kS = A.f32([16, 128]); cS = A.f32([16, 128]); sS = A.f32([16, 128]); eS = A.f32([16, 128])
            trb = trow.ap.unsqueeze(1).to_broadcast([128, 16, 128])
            TT_("dve", angS.ap, sm.ap[:, 1, :].unsqueeze(2).to_broadcast([128, 16, 128]), trb, ALU.mult, [sm, trow], [angS])
            TT_("dve", eS.ap, sm.ap[:, 0, :].unsqueeze(2).to_broadcast([128, 16, 128]), trb, ALU.mult, [sm, trow], [eS])
            ACT(eS.ap, eS.ap, AF.Exp, [eS], [eS])
            sincos(angS, 2048, cS, sS, kS)
            TT_("dve", ApT.ap[:, :, 0, :], eS.ap, cS.ap, ALU.mult, [eS, cS], [ApT])
            TT_("dve", ApT.ap[:, :, 1, :], eS.ap, sS.ap, ALU.mult, [eS, sS], [ApT])
            CP("dve", a1.ap, ApT.ap[:, :, :, 1], [ApT], [a1])
            if stop_after == "Ca":
                return tb
            cf = A.f32([8, 16])
            are = sm.ap[:, 2, :]; aim = sm.ap[:, 3, :]
            TS_("dve", cf.ap[:, 0, :], a1.ap[:, :, 0], -1.0, None, ALU.add, None, [a1], [cf])
            TT_("dve", cf.ap[:, 1, :], are, are, ALU.mult, [sm], [cf])
            TT_("dve", cf.ap[:, 2, :], aim, aim, ALU.mult, [sm], [cf])
            TT_("dve", cf.ap[:, 1, :], cf.ap[:, 1, :], cf.ap[:, 2, :], ALU.add, [cf], [cf])
            P.add("dve", lambda e: e.reciprocal(out=cf.ap[:, 1, :], in_=cf.ap[:, 1, :]), [cf.b], [cf.b])
            TT_("dve", cf.ap[:, 2, :], cf.ap[:, 0, :], are, ALU.mult, [cf, sm], [cf])
            TT_("dve", cf.ap[:, 3, :], a1.ap[:, :, 1], aim, ALU.mult, [a1, sm], [cf])
            TT_("dve", cf.ap[:, 2, :], cf.ap[:, 2, :], cf.ap[:, 3, :], ALU.add, [cf], [cf])
            TT_("dve", cf.ap[:, 4, :], cf.ap[:, 2, :], cf.ap[:, 1, :], ALU.mult, [cf], [cf])
            TT_("dve", cf.ap[:, 2, :], a1.ap[:, :, 1], are, ALU.mult, [a1, sm], [cf])
            TT_("dve", cf.ap[:, 3, :], cf.ap[:, 0, :], aim, ALU.mult, [cf, sm], [cf])
            TT_("dve", cf.ap[:, 2, :], cf.ap[:, 2, :], cf.ap[:, 3, :], ALU.subtract, [cf], [cf])
            TT_("dve", cf.ap[:, 5, :], cf.ap[:, 2, :], cf.ap[:, 1, :], ALU.mult, [cf], [cf])
            if stop_after == "Cb":
                return tb
            bre = A.f32([16, 16]); bim = A.f32([16, 16]); Z = A.f32([16, 2, 2, 16]); tq = A.f32([16, 16])
            for g2 in range(2):
                DMA("sync", bre.ap[g2 * 64:(g2 + 1) * 64], ssm_b_re[l].rearrange("(a b) p m -> b p a m", b=2)[g2], [], [bre.b])
                DMA("sync", bim.ap[g2 * 64:(g2 + 1) * 64], ssm_b_im[l].rearrange("(a b) p m -> b p a m", b=2)[g2], [], [bim.b])
            MSET("pool", Z, 0.0)
            cre_b = cf.ap[:, 4, :].unsqueeze(2).to_broadcast([128, 16, 16])
            cim_b = cf.ap[:, 5, :].unsqueeze(2).to_broadcast([128, 16, 16])
            for g2 in range(2):
                ps_ = slice(g2 * 64, (g2 + 1) * 64)
                TT_("dve", tq.ap[ps_], bim.ap[ps_], cim_b[ps_], ALU.mult, [bim, cf], [tq])
                TT_("dve", Z.ap[ps_, :, 0, g2, :], bre.ap[ps_], cre_b[ps_], ALU.mult, [bre, cf], [Z])
                TT_("dve", Z.ap[ps_, :, 0, g2, :], Z.ap[ps_, :, 0, g2, :], tq.ap[ps_], ALU.subtract, [Z, tq], [Z])
                TT_("dve", tq.ap[ps_], bre.ap[ps_], cim_b[ps_], ALU.mult, [bre, cf], [tq])
                TT_("dve", Z.ap[ps_, :, 1, g2, :], bim.ap[ps_], cre_b[ps_], ALU.mult, [bim, cf], [Z])
                TT_("dve", Z.ap[ps_, :, 1, g2, :], Z.ap[ps_, :, 1, g2, :], tq.ap[ps_], ALU.add, [Z, tq], [Z])
            for ch in range(4):
                for ri in range(2):
                    pt = PS()
                    zin = A.f32([128])
                    CP("pool", zin.ap.rearrange("p (a b c) -> p a b c", a=4, b=2), Z.ap[:, ch * 4:(ch + 1) * 4, ri, :, :], [Z], [zin])
                    TR(pt.ap[:, 0:128], zin.ap, identf, [zin], [pt])
                    CP("dve", T2.ap[:, ch, ri, :], pt.ap[:, 0:128], [pt], [T2])
            MSET("pool", T2f, 0.0)
            for i4 in range(4):
                CP("pool", T2f.ap[i4 * 32:(i4 + 1) * 32, :, i4, :, :], T2.ap[i4 * 32:(i4 + 1) * 32, :, :, :], [T2], [T2f])
            if stop_after == "Cc":
                return tb
            Zc = A.f32([4, 2, 128])
            MSET("pool", Zc, 0.0)
            for ri, csrc in enumerate([ssm_c_re, ssm_c_im]):
                cview = csrc[l].rearrange("(ch i b) m p -> i b m ch p", i=4, b=2)
                for i4 in range(4):
                    for g2 in range(2):
                        r0 = i4 * 32 + g2 * 16
                        DMA("sync", Zc.ap[r0:r0 + 16, :, ri, g2 * 64:(g2 + 1) * 64], cview[i4, g2], [], [Zc.b])
            for ch in range(4):
                for ri in range(2):
                    pt = PS()
                    TR(pt.ap[:, 0:128], Zc.ap[:, ch, ri, :], identf, [Zc], [pt])
                    if ri == 0:
                        CP("dve", CT.ap[:, ch, ri, :], pt.ap[:, 0:128], [pt], [CT])
                        TS_("dve", CTn.ap[:, ch, :], pt.ap[:, 0:128], -1.0, None, ALU.mult, None, [pt], [CTn])
                    else:
                        TS_("dve", CT.ap[:, ch, ri, :], pt.ap[:, 0:128], -1.0, None, ALU.mult, None, [pt], [CT])
            dcol = A.f32([4]); dnat = A.f32([128], parts=4)
            DMA("sync", dnat.ap, ssm_d[l].rearrange("(c p) -> c p", p=128), [], [dnat.b])
            pt = PS()
            TR(pt.ap[:, 0:4], dnat.ap, identf, [dnat], [pt])
            CP("dve", dcol.ap, pt.ap[:, 0:4], [pt], [dcol])
            for ch in range(4):
                TS_("dve", Dd.ap[:, ch, :], identf.ap, dcol.ap[:, ch:ch + 1], None, ALU.mult, None, [identf, dcol], [Dd])
            if stop_after == "Cd":
                return tb
            arb = A.f32([2048]); aib = A.f32([2048]); ldb = A.f32([32])
            DMA("sync", arb.ap, ssm_a_re[l].rearrange("g p -> (g p)").partition_broadcast(128), [], [arb.b])
            DMA("sync", aib.ap, ssm_a_im[l].rearrange("g p -> (g p)").partition_broadcast(128), [], [aib.b])
            DMA("sync", ldb.ap, ssm_log_dt[l].partition_broadcast(128), [], [ldb.b])
            ACT(ldb.ap, ldb.ap, AF.Exp, [ldb], [ldb])
            dtbb = ldb.ap.unsqueeze(2).to_broadcast([128, 32, 64])
            TT_("dve", arb.ap.rearrange("p (g q) -> p g q", q=64), arb.ap.rearrange("p (g q) -> p g q", q=64), dtbb, ALU.mult, [arb, ldb], [arb])
            TT_("dve", aib.ap.rearrange("p (g q) -> p g q", q=64), aib.ap.rearrange("p (g q) -> p g q", q=64), dtbb, ALU.mult, [aib, ldb], [aib])
            angT = angS; kT_ = kS; cT_ = cS; sT_ = sS; eT = eS
            fl = lambda t: t.ap.rearrange("p a b -> p (a b)")
            TS_("dve", fl(angT), aib.ap, tcol.ap, None, ALU.mult, None, [aib, tcol], [angT])
            ACT(fl(eT), arb.ap, AF.Exp, [arb, ntcol], [eT], scale=ntcol.ap)
            sincos(angT, 2048, cT_, sT_, kT_)
            TT_("dve", Ainv.ap[:, :, 0, :], eT.ap, cT_.ap, ALU.mult, [eT, cT_], [Ainv])
            STT(Ainv.ap[:, :, 1, :], eT.ap, -1.0, sT_.ap, ALU.mult, ALU.mult, [eT, sT_], [Ainv])
            DMA("pool", WGLU.ap, w_glu[l].rearrange("(kc p) n -> p kc n", p=128), [], [WGLU.b])
            tb["mark"] = mark
            return tb

        def phase_C(S, l, tb, hc_init_from=None):
            s = S.s; QB = S.QB; nblk = S.T // QB
            Ainv = tb["Ainv"]; ApT = tb["ApT"]; T2f = tb["T2f"]; CT = tb["CT"]; Dd = tb["Dd"]; a1 = tb["a1"]; WGLU = tb["WGLU"]; CTn = tb["CTn"]
            uTb = [A.bf([4, QB]), A.bf([4, QB])]
            bus = [A.f32([16, 2, 128]), A.f32([16, 2, 128])]
            cus = A.f32([16, 2, 128])
            t1 = A.bf([16, 128]); t2 = A.bf([16, 128]); t3 = A.bf([16, 128]); t4 = A.bf([16, 128])
            w1 = A.bf([16, 128]); w2 = A.bf([16, 128]); w3 = A.bf([16, 128]); w4 = A.bf([16, 128])
            hl = A.f32([16, 2]); hc = A.f32([16, 2]); hq = A.f32([16, 2])
            y2 = A.f32([DSSM]); yin = A.f32([DSSM]); zt = A.bf([DSSM]); zT = A.bf([4, 128]); sg = A.f32([4, 128]); soT = A.bf([4, 128])
            hn = A.f32([128], parts=16)

            def carry_from_hl():
                TT_("dve", hq.ap[:, :, 0], a1.ap[:, :, 0], hl.ap[:, :, 0], ALU.mult, [a1, hl], [hq])
                TT_("dve", hq.ap[:, :, 1], a1.ap[:, :, 1], hl.ap[:, :, 1], ALU.mult, [a1, hl], [hq])
                TT_("dve", hc.ap[:, :, 0], hq.ap[:, :, 0], hq.ap[:, :, 1], ALU.subtract, [hq], [hc])
                TT_("dve", hq.ap[:, :, 0], a1.ap[:, :, 0], hl.ap[:, :, 1], ALU.mult, [a1, hl], [hq])
                TT_("dve", hq.ap[:, :, 1], a1.ap[:, :, 1], hl.ap[:, :, 0], ALU.mult, [a1, hl], [hq])
                TT_("dve", hc.ap[:, :, 1], hq.ap[:, :, 0], hq.ap[:, :, 1], ALU.add, [hq], [hc])

            if S.koff > 0:
                for ri, src in enumerate([sre, sim]):
                    DMA("sync", hn.ap, src[l].rearrange("(a b) p -> a (b p)", b=2), [], [hn.b])
                    pt = PSB[5]
                    TR(pt.ap[:, 0:16], hn.ap, identf, [hn], [pt])
                    CP("dve", hl.ap[:, :, ri], pt.ap[:, 0:16], [pt], [hl])
                carry_from_hl()
            else:
                MSET("dve", hc, 0.0)

            def stage1(blk):
                t0 = blk * QB
                ut = uTb[blk % 2]; bu = bus[blk % 2]
                DMA("sync", ut.ap, S.uT.rearrange("(c p) t -> p c t", p=128)[:, :, t0:t0 + QB], [S.B_uT], [ut.b])
                for ch in range(4):
                    pb = [PSB[(ch % 2) * 2], PSB[(ch % 2) * 2 + 1]]
                    for hf_ in range(2):
                        MM(pb[hf_].ap[0:QB, :], ut.ap[:, ch, :], T2f.ap[:, ch, hf_ * 2:hf_ * 2 + 2, :, :].rearrange("p i r q -> p (i r q)"), True, True, [ut, T2f], [pb[hf_]])
                        CP("act", bu.ap[0:QB, ch * 4 + hf_ * 2:ch * 4 + hf_ * 2 + 2].rearrange("p i r q -> p (i r q)"), pb[hf_].ap[0:QB, :], [pb[hf_]], [bu])

            def stage2a(blk):
                t0 = blk * QB
                ut = uTb[blk % 2]; bu = bus[blk % 2]
                TT_("dve", w1.ap[0:QB], bu.ap[0:QB, :, 0, :], Ainv.ap[0:QB, :, 0, :], ALU.mult, [bu, Ainv], [w1])
                TT_("pool", w4.ap[0:QB], bu.ap[0:QB, :, 1, :], Ainv.ap[0:QB, :, 0, :], ALU.mult, [bu, Ainv], [w4])
                TT_("dve", w2.ap[0:QB], bu.ap[0:QB, :, 1, :], Ainv.ap[0:QB, :, 1, :], ALU.mult, [bu, Ainv], [w2])
                TT_("dve", w3.ap[0:QB], bu.ap[0:QB, :, 0, :], Ainv.ap[0:QB, :, 1, :], ALU.mult, [bu, Ainv], [w3])

            def stageC(blk):
                t0 = blk * QB
                ut = uTb[blk % 2]; bu = bus[blk % 2]
                for ch in range(4):
                    pc = [PSB[4], PSB[5]]
                    for i4 in range(4):
                        pr = ch * 4 + i4
                        tgt = pc[i4 // 2]
                        cre_ = ((i4 % 2) * 2 + 0) * 128; cim_ = ((i4 % 2) * 2 + 1) * 128
                        MM(tgt.ap[:, cre_:cre_ + QB], w1.ap[0:QB, pr, :], tri.ap[0:QB, 0:QB], True, False, [w1, tri], [tgt])
                        MM(tgt.ap[:, cre_:cre_ + QB], w2.ap[0:QB, pr, :], ntri.ap[0:QB, 0:QB], False, True, [w2, ntri], [tgt])
                        MM(tgt.ap[:, cim_:cim_ + QB], w3.ap[0:QB, pr, :], tri.ap[0:QB, 0:QB], True, False, [w3, tri], [tgt])
                        MM(tgt.ap[:, cim_:cim_ + QB], w4.ap[0:QB, pr, :], tri.ap[0:QB, 0:QB], False, True, [w4, tri], [tgt])
                    for hf_ in range(2):
                        CP("act", cus.ap[:, ch * 4 + hf_ * 2:ch * 4 + hf_ * 2 + 2, :, 0:QB], pc[hf_].ap.rearrange("p (i r q) -> p i r q", i=2, r=2)[:, :, :, 0:QB], [pc[hf_]], [cus])

            def stageH(blk):
                t0 = blk * QB
                ut = uTb[blk % 2]; bu = bus[blk % 2]
                TT_("dve", cus.ap[:, :, 0, 0:QB], cus.ap[:, :, 0, 0:QB], hc.ap[:, :, 0].unsqueeze(2).to_broadcast([128, 16, QB]), ALU.add, [cus, hc], [cus])
                TT_("pool", cus.ap[:, :, 1, 0:QB], cus.ap[:, :, 1, 0:QB], hc.ap[:, :, 1].unsqueeze(2).to_broadcast([128, 16, QB]), ALU.add, [cus, hc], [cus])
                apr = ApT.ap[:, :, 0, 0:QB]; api = ApT.ap[:, :, 1, 0:QB]
                cre = cus.ap[:, :, 0, 0:QB]; cim = cus.ap[:, :, 1, 0:QB]
                L = QB - 1
                TT_("dve", hq.ap[:, :, 0], ApT.ap[:, :, 0, L], cus.ap[:, :, 0, L], ALU.mult, [ApT, cus], [hq])
                TT_("dve", hq.ap[:, :, 1], ApT.ap[:, :, 1, L], cus.ap[:, :, 1, L], ALU.mult, [ApT, cus], [hq])
                TT_("dve", hl.ap[:, :, 0], hq.ap[:, :, 0], hq.ap[:, :, 1], ALU.subtract, [hq], [hl])
                TT_("dve", hq.ap[:, :, 0], ApT.ap[:, :, 0, L], cus.ap[:, :, 1, L], ALU.mult, [ApT, cus], [hq])
                TT_("dve", hq.ap[:, :, 1], ApT.ap[:, :, 1, L], cus.ap[:, :, 0, L], ALU.mult, [ApT, cus], [hq])
                TT_("dve", hl.ap[:, :, 1], hq.ap[:, :, 0], hq.ap[:, :, 1], ALU.add, [hq], [hl])
                carry_from_hl()
                TT_("dve", t1.ap[:, :, 0:QB], apr, cre, ALU.mult, [ApT, cus], [t1])
                TT_("pool", t4.ap[:, :, 0:QB], api, cre, ALU.mult, [ApT, cus], [t4])
                TT_("dve", t2.ap[:, :, 0:QB], api, cim, ALU.mult, [ApT, cus], [t2])
                TT_("dve", t3.ap[:, :, 0:QB], apr, cim, ALU.mult, [ApT, cus], [t3])
                py = PSB[6]
                for ch in range(4):
                    MM(py.ap[0:QB, ch * 128:(ch + 1) * 128], ut.ap[:, ch, :], Dd.ap[:, ch, :], ch == 0, False, [ut, Dd], [py])
                for ch in range(4):
                    for i4 in range(4):
                        pr = ch * 4 + i4
                        o_ = py.ap[0:QB, pr * 32:(pr + 1) * 32]
                        cs_ = slice(i4 * 32, (i4 + 1) * 32)
                        MM(o_, t1.ap[:, pr, 0:QB], CT.ap[:, ch, 0, cs_], False, False, [t1, CT], [py])
                        MM(o_, t2.ap[:, pr, 0:QB], CTn.ap[:, ch, cs_], False, False, [t2, CTn], [py])
                        MM(o_, t3.ap[:, pr, 0:QB], CT.ap[:, ch, 1, cs_], False, False, [t3, CT], [py])
                        MM(o_, t4.ap[:, pr, 0:QB], CT.ap[:, ch, 1, cs_], False, pr == 15, [t4, CT], [py])

            def stage2c(blk):
                t0 = blk * QB
                ut = uTb[blk % 2]; bu = bus[blk % 2]
                py = PSB[6]
                ACT(y2.ap[0:QB], py.ap[0:QB, :], AF.Square, [py], [y2])
                TS_("dve", y2.ap[0:QB], y2.ap[0:QB], 0.044715, 1.0, ALU.mult, ALU.add, [y2], [y2])
                TT_("dve", yin.ap[0:QB], y2.ap[0:QB], py.ap[0:QB, :], ALU.mult, [y2, py], [yin])
                ACT(yin.ap[0:QB], yin.ap[0:QB], AF.Sigmoid, [yin], [yin], scale=1.5957691216057308)
                TT_("dve", zt.ap[0:QB], yin.ap[0:QB], py.ap[0:QB, :], ALU.mult, [yin, py], [zt])
                pz = PSB[7]
                pzb = pz.ap.bitcast(BF16)
                for c in range(4):
                    TR(pzb[:, c * 128:c * 128 + QB], zt.ap[0:QB, c * 128:(c + 1) * 128], identb, [zt], [pz])
                CP("act", zT.ap[:, :, 0:QB], pzb[:, 0:512].rearrange("p (a b) -> p a b", b=128)[:, :, 0:QB], [pz], [zT])
                pg = PSB[7]
                for m in range(4):
                    for kc in range(4):
                        MM(pg.ap[:, m * 128:m * 128 + QB], WGLU.ap[:, kc, m * 128:(m + 1) * 128], zT.ap[:, kc, 0:QB], kc == 0, kc == 3, [WGLU, zT], [pg])
                ACT(sg.ap[:, :, 0:QB], pg.ap.rearrange("p (a b) -> p a b", b=128)[:, :, 0:QB], AF.Sigmoid, [pg], [sg])
                TT_("dve", soT.ap[:, :, 0:QB], zT.ap[:, :, 0:QB], sg.ap[:, :, 0:QB], ALU.mult, [zT, sg], [soT])
                DMA("sync", S.soT.rearrange("(c p) t -> p c t", p=128)[:, :, t0:t0 + QB], soT.ap[:, :, 0:QB], [soT.b], [S.B_soT])

            stage1(0)
            stage2a(0)
            stageC(0)
            for blk in range(nblk):
                if blk + 1 < nblk:
                    stage1(blk + 1)
                stageH(blk)
                if blk + 1 < nblk:
                    stage2a(blk + 1)
                    stageC(blk + 1)
                stage2c(blk)
            for ri, dst in enumerate([S.nr_o, S.ni_o]):
                pt = PSB[4 + ri]
                hlc = A.f32([16])
                CP("dve", hlc.ap, hl.ap[:, :, ri], [hl], [hlc])
                TR(pt.ap[0:16, 0:128], hlc.ap, identf, [hlc], [pt])
                ho = A.f32([128], parts=16)
                CP("dve", ho.ap, pt.ap[0:16, 0:128], [pt], [ho])
                DMA("sync", dst[l].rearrange("(a b) p -> a (b p)", b=2), ho.ap, [ho.b], [])

        def post_norm_residual(S, l, oT, osq, xt, xn, gslot, rstd, tmp2):
            TT = S.TT
            pss = PS()
            for c in range(8):
                MM(pss.ap[:, 0:TT], onesM.ap, osq.ap[:, c, :], c == 0, c == 7, [onesM, osq], [pss])
            ACT(rstd.ap, pss.ap[:, 0:TT], AF.Sqrt, [pss, epsc], [rstd], bias=epsc.ap, scale=1.0)
            P.add("dve", lambda e: e.reciprocal(out=rstd.ap, in_=rstd.ap), [rstd.b], [rstd.b])
            for c in range(8):
                tm = tmp2[c % 2]
                STT(tm.ap, oT.ap[:, c, :], PAR.ap[:, l, S.s, gslot, c:c + 1], rstd.ap, ALU.mult, ALU.mult, [oT, PAR, rstd], [tm])
                TT_("dve", xn.ap[:, c, :], xt.ap[:, c, :], tm.ap, ALU.add, [xt, tm], [xn])

        def phase_DE(S, l, WOUT):
            s = S.s; TT = S.TT; QB = S.QB; nsub = TT // QB
            xt = A.f32([8, TT])
            xn = A.f32([8, TT]); oT = A.f32([8, TT]); sq = A.bf([8, TT]); osq = sq
            rstd = A.f32([TT]); tmp2 = [A.f32([TT]), A.f32([TT])]
            mixT = A.bf([8, TT]); hf = A.bf([8, TT]); hT = A.bf([NJ, TT]); sgt = A.f32([TT])
            WGs = [A.bf([8, 256]), A.bf([8, 256])]; WUs = [A.bf([8, 256]), A.bf([8, 256])]
            WDq = [A.bf([NJ, 256]), A.bf([NJ, 256])]
            wgi = 0; wdi = 0
            def load_mix(i):
                t0 = i * TT
                DMA("sync", mixT.ap[:, 4:8, :], S.soT.rearrange("(c p) t -> p c t", p=128)[:, :, t0:t0 + TT], [S.B_soT], [mixT.b])
                for j in range(nsub):
                    g = (t0 // QB) + j
                    pa = PS(); pab = pa.ap.bitcast(BF16)
                    for c in range(4):
                        TR(pab[:, c * 128:c * 128 + QB], ATT[s].ap[0:QB, g, c * 128:(c + 1) * 128], identb, [ATT[s]], [pa])
                    CP("act", mixT.ap[:, 0:4, j * QB:(j + 1) * QB], pab[:, 0:512].rearrange("p (a b) -> p a b", b=128)[:, :, 0:QB], [pa], [mixT])

            load_mix(0)
            for i in range(S.nt):
                t0 = i * TT
                DMA("sync", xt.ap, S.xT.rearrange("(c p) t -> p c t", p=128)[:, :, t0:t0 + TT], [S.B_xT[i]], [xt.b])
                pos = []
                for m in range(8):
                    po = PS()
                    for kc in range(8):
                        MM(po.ap[:, 0:TT], WOUT.ap[:, kc, m * 128:(m + 1) * 128], mixT.ap[:, kc, :], kc == 0, kc == 7, [WOUT, mixT], [po])
                    CP("dve", oT.ap[:, m, :], po.ap[:, 0:TT], [po], [oT])
                    ACT(osq.ap[:, m, :], po.ap[:, 0:TT], AF.Square, [po], [osq])
                if i + 1 < S.nt:
                    load_mix(i + 1)
                post_norm_residual(S, l, oT, osq, xt, xn, 2, rstd, tmp2)
                rmsnorm_mod(S, l, xn, 3, hf, tmp2, sq, rstd)
                for n0 in range(0, DFF, 256):
                    wg = WGs[wgi % 2]; wu = WUs[wgi % 2]; wgi += 1
                    DMA("sync", wg.ap, WG2[l][:, :, n0:n0 + 256], [B_WFF[l]], [wg.b])
                    DMA("sync", wu.ap, WU2[l][:, :, n0:n0 + 256], [B_WFF[l]], [wu.b])
                    for jj in range(2):
                        j = n0 // 128 + jj
                        pg = PS(); pu = PS()
                        for kc in range(8):
                            MM(pg.ap[:, 0:TT], wg.ap[:, kc, jj * 128:(jj + 1) * 128], hf.ap[:, kc, :], kc == 0, kc == 7, [wg, hf], [pg])
                        for kc in range(8):
                            MM(pu.ap[:, 0:TT], wu.ap[:, kc, jj * 128:(jj + 1) * 128], hf.ap[:, kc, :], kc == 0, kc == 7, [wu, hf], [pu])
                        ACT(sgt.ap, pg.ap[:, 0:TT], AF.Silu, [pg], [sgt])
                        TT_("dve", hT.ap[:, j, :], sgt.ap, pu.ap[:, 0:TT], ALU.mult, [sgt, pu], [hT])
                for q4 in range(4):
                    WD = WDq[wdi % 2]; wdi += 1
                    DMA("sync", WD.ap, WD2[l][:, :, q4 * 256:(q4 + 1) * 256], [B_WFF[l]], [WD.b])
                    for mm in range(2):
                        m = q4 * 2 + mm
                        pf = PS()
                        for j in range(NJ):
                            MM(pf.ap[:, 0:TT], WD.ap[:, j, mm * 128:(mm + 1) * 128], hT.ap[:, j, :], j == 0, j == NJ - 1, [WD, hT], [pf])
                        CP("dve", oT.ap[:, m, :], pf.ap[:, 0:TT], [pf], [oT])
                        ACT(osq.ap[:, m, :], pf.ap[:, 0:TT], AF.Square, [pf], [osq])
                post_norm_residual(S, l, oT, osq, xn, xt, 5, rstd, tmp2)
                DMA("sync", S.xT.rearrange("(c p) t -> p c t", p=128)[:, :, t0:t0 + TT], xt.ap, [xt.b], [S.B_xT[i]])

        def phase_final():
            A.reset()
            xin_ = [A.f32([8, 128]), A.f32([8, 128])]
            xo_ = [A.f32([D]), A.f32([D])]
            cnt = 0
            for S in STREAMS:
                qb = S.QB
                for i in range(S.T // qb):
                    xi = xin_[cnt % 2]; xo = xo_[cnt % 2]; cnt += 1
                    ti = (i * qb) // S.TT
                    DMA("sync", xi.ap[:, :, 0:qb], S.xT.rearrange("(c p) t -> p c t", p=128)[:, :, i * qb:(i + 1) * qb], [S.B_xT[ti]], [xi.b])
                    for half in range(2):
                        pt = PS()
                        for c4 in range(4):
                            TR(pt.ap[0:qb, c4 * 128:(c4 + 1) * 128], xi.ap[:, half * 4 + c4, 0:qb], identf, [xi], [pt])
                        CP("act" if half == 0 else "dve", xo.ap[0:qb, half * 512:(half + 1) * 512], pt.ap[0:qb, :], [pt], [xo])
                    DMA("sync", S.y_out[i * qb:(i + 1) * qb, :], xo.ap[0:qb, :], [xo.b], [])

        for l in range(DEPTH if stop_after != "XF" else 0):
            P.barrier(); A.reset()
            WIN = A.bf([8, DIN])
            DMA("pool", WIN.ap, w_in[l].rearrange("(kc p) n -> p kc n", p=128), [], [WIN.b])
            mark = A.off
            for S in STREAMS:
                A.off = mark
                phase_A(S, l, WIN)
                P.barrier()
            if stop_after is not None and stop_after.startswith("A"):
                break
            P.barrier(); A.reset()
            for S in STREAMS:
                A.reset()
                phase_B(S, l)
                P.barrier()
            if stop_after == "B":
                break
            P.barrier(); A.reset()
            tb = s5_prep(l)
            P.barrier()
            if stop_after in ("C0", "Ca", "Cb", "Cc", "Cd"):
                break
            A.off = tb["mark"]
            mark = A.off
            for S in STREAMS:
                A.off = mark
                phase_C(S, l, tb)
                P.barrier()
            if stop_after is not None and stop_after.startswith("C"):
                break
            P.barrier(); A.reset()
            WOUT = A.bf([8, D])
            DMA("pool", WOUT.ap, w_out[l].rearrange("(kc p) n -> p kc n", p=128), [], [WOUT.b])
            mark = A.off
            for S in STREAMS:
                A.off = mark
                phase_DE(S, l, WOUT)
                P.barrier()
        P.barrier()
        phase_final()


    body_fn()

    with nc.allow_low_precision("bf16 matmul operands, fp32 accumulate"):
        P.emit()
    P.close()
    return nc


_CACHE = {}


def kernel(x_prompt, x_sample, c_prompt, c_sample, cache_k, cache_v, cache_logf,
           state_ssm_re, state_ssm_im, w_ada, b_ada, g_pre_mix, g_post_mix, g_pre_ffn,
           g_post_ffn, w_in, b_forget, ssm_a_re, ssm_a_im, ssm_log_dt, ssm_b_re, ssm_b_im,
           ssm_c_re, ssm_c_im, ssm_d, w_glu, w_out, w_gate, w_up, w_down, _stop_after=None):
    f = lambda a: np.ascontiguousarray(np.asarray(a, dtype=np.float32))
    x_prompt = f(x_prompt); x_sample = f(x_sample)
    B, T, _ = x_prompt.shape
    BS, TS, _ = x_sample.shape
    DEPTH = w_in.shape[0]
    PAST = cache_k.shape[3]
    NC = 8
    key = (T, TS, PAST, DEPTH, _stop_after)
    if key not in _CACHE:
        _CACHE[key] = build(T, TS, PAST, DEPTH, _stop_after)
    nc = _CACHE[key]
    shared = dict(w_ada=f(w_ada), b_ada=f(b_ada), g_pre_mix=f(g_pre_mix), g_post_mix=f(g_post_mix),
                  g_pre_ffn=f(g_pre_ffn), g_post_ffn=f(g_post_ffn), w_in=f(w_in), b_forget=f(b_forget),
                  ssm_a_re=f(ssm_a_re), ssm_a_im=f(ssm_a_im), ssm_log_dt=f(ssm_log_dt),
                  ssm_b_re=f(ssm_b_re), ssm_b_im=f(ssm_b_im), ssm_c_re=f(ssm_c_re), ssm_c_im=f(ssm_c_im),
                  ssm_d=f(ssm_d), w_glu=f(w_glu), w_out=f(w_out), w_gate=f(w_gate), w_up=f(w_up), w_down=f(w_down))
    cache_k = f(cache_k); cache_v = f(cache_v); cache_logf = f(cache_logf)
    state_ssm_re = f(state_ssm_re); state_ssm_im = f(state_ssm_im)
    c_prompt = f(c_prompt); c_sample = f(c_sample)
    in_maps = []
    for c in range(NC):
        bp = c % B
        bs = c % BS
        m = dict(shared)
        m["xp"] = x_prompt[bp]; m["xs"] = x_sample[bs]
        m["c2"] = np.ascontiguousarray(np.stack([c_prompt[bp], c_sample[bs]], axis=0))
        m["ck"] = np.ascontiguousarray(cache_k[:, bs]); m["cv"] = np.ascontiguousarray(cache_v[:, bs])
        m["clf"] = np.ascontiguousarray(cache_logf[:, bs])
        m["sre"] = np.ascontiguousarray(state_ssm_re[:, bs]); m["sim"] = np.ascontiguousarray(state_ssm_im[:, bs])
        in_maps.append(m)
    res = run_bass_kernel_spmd(nc, in_maps, core_ids=list(range(NC)))
    R = res.results
    yp = np.stack([R[b]["yp"] for b in range(B)], axis=0)
    ys = np.stack([R[b]["ys"] for b in range(BS)], axis=0)
    st = lambda name, n: np.stack([R[b][name] for b in range(n)], axis=1)
    return (yp, ys, st("nkp", B), st("nvp", B), st("nlp", B), st("nrp", B), st("nip", B),
            st("nks", BS), st("nvs", BS), st("nls", BS), st("nrs", BS), st("nis", BS))
```
